# Optimizing a Trainium2 kernel written in Bass

```python
import math
import jax, jax.numpy as jnp
from jax import lax
import numpy as np

D_MODEL = 1024
BATCH = 8
SEQ = 8192
DEPTH = 4

CTX_LEN = 256
GRID_W = 64
EPS = 1e-6
NEG_INF = -1e30

GLA_HEADS = 6
GLA_DK = 32
GLA_DV = 64
GLA_LOWRANK = 16
GLA_TAU = 16.0
GLA_CHUNK = 16
ROPE_BASE = 10000.0
SC_WIDTH = 256
SC_GROUPS = 4
NA_HEADS = 6
NA_DH = 64
NA_WIN_ROWS = 8
NA_WIN_COLS = 16
FFN_DIM = 2816

GLA_QK = GLA_HEADS * GLA_DK
GLA_V = GLA_HEADS * GLA_DV
NA_W = NA_HEADS * NA_DH
MIX_WIDTH = GLA_V + SC_WIDTH + NA_W
IN_SPLIT = (GLA_QK, GLA_QK, GLA_V, GLA_LOWRANK, GLA_LOWRANK, GLA_V,
            SC_WIDTH, SC_WIDTH, SC_WIDTH, NA_W, NA_W, NA_W)
IN_WIDTH = 2 * GLA_QK + 2 * GLA_V + 2 * GLA_LOWRANK + 3 * SC_WIDTH + 3 * NA_W

kernel_name = "hybrid_gla_shortconv_natten_dit"


def rmsnorm(t, g):
    tf = t.astype(jnp.float32)
    y = tf * lax.rsqrt(jnp.mean(tf * tf, axis=-1, keepdims=True) + EPS)
    return (y * g.astype(jnp.float32)).astype(t.dtype)


def modulate(t, shift, scale):
    return t * (1 + scale) + shift


def split_cols(u):
    parts, start = [], 0
    for size in IN_SPLIT:
        parts.append(u[..., start:start + size])
        start += size
    return parts


def dwconv3(t, w, b):
    tp = jnp.pad(t, ((0, 0), (1, 1), (0, 0)))
    return tp[:, :-2] * w[0] + tp[:, 1:-1] * w[1] + tp[:, 2:] * w[2] + b


def axial_rope(n_tokens, dim):
    t = jnp.arange(n_tokens)
    n_freq = dim // 4
    inv_freq = ROPE_BASE ** (-jnp.arange(n_freq, dtype=jnp.float32) / n_freq)
    row = (t // GRID_W).astype(jnp.float32)[:, None] * inv_freq
    col = (t % GRID_W).astype(jnp.float32)[:, None] * inv_freq
    ang = jnp.concatenate([row, col], axis=-1)
    return jnp.cos(ang)[None, :, None, :], jnp.sin(ang)[None, :, None, :]


def apply_rope(t, cos, sin):
    t1, t2 = t[..., 0::2], t[..., 1::2]
    return jnp.stack([t1 * cos - t2 * sin, t1 * sin + t2 * cos], axis=-1).reshape(t.shape)


def gla_chunked(q, k, v, g, s0):
    bsz, n_tok, h, dk = q.shape
    dv = v.shape[-1]
    n_chunks = n_tok // GLA_CHUNK
    rs = lambda t: t.reshape(bsz, n_chunks, GLA_CHUNK, h, t.shape[-1])
    q, k, v, g = rs(q), rs(k), rs(v), rs(g)
    b = jnp.cumsum(g, axis=2)
    b_last = b[:, :, -1:]
    q_dec = q * jnp.exp(b)
    k_inv = k * jnp.exp(-b)
    k_end = k * jnp.exp(b_last - b)
    causal = jnp.tril(jnp.ones((GLA_CHUNK, GLA_CHUNK), dtype=bool))
    a = jnp.einsum('bnihk,bnjhk->bnhij', q_dec, k_inv)
    a = jnp.where(causal, a, 0.0)
    o_intra = jnp.einsum('bnhij,bnjhv->bnihv', a, v)

    def step(s, xs):
        qd, ke, vv, dec = xs
        o = jnp.einsum('bihk,bhkv->bihv', qd, s)
        s = s * dec[..., None] + jnp.einsum('bjhk,bjhv->bhkv', ke, vv)
        return s, o

    xs = (jnp.moveaxis(q_dec, 1, 0), jnp.moveaxis(k_end, 1, 0), jnp.moveaxis(v, 1, 0),
          jnp.moveaxis(jnp.exp(b_last[:, :, 0]), 1, 0))
    s_final, o_inter = lax.scan(step, s0, xs)
    o = o_intra + jnp.moveaxis(o_inter, 0, 1)
    return o.reshape(bsz, n_tok, h, dv), s_final


def gla_prep(q, k, v, glr_f, glr_b, wg2_fw, bg_fw, wg2_bw, bg_bw, rope):
    bsz, n_tok, _ = q.shape
    heads = lambda t, d: t.astype(jnp.float32).reshape(bsz, n_tok, GLA_HEADS, d)
    q = heads(q, GLA_DK) * (GLA_DK ** -0.5)
    k = heads(k, GLA_DK)
    v = heads(v, GLA_DV)
    if rope is not None:
        q = apply_rope(q, rope[0], rope[1])
        k = apply_rope(k, rope[0], rope[1])
    g_f = heads(jax.nn.log_sigmoid((glr_f @ wg2_fw + bg_fw).astype(jnp.float32)), GLA_DK) / GLA_TAU
    g_b = heads(jax.nn.log_sigmoid((glr_b @ wg2_bw + bg_bw).astype(jnp.float32)), GLA_DK) / GLA_TAU
    return q, k, v, g_f, g_b


def gla_output(o, r, norm_g):
    of = o * lax.rsqrt(jnp.mean(o * o, axis=-1, keepdims=True) + EPS) * norm_g.astype(jnp.float32)
    bsz, n_tok = o.shape[:2]
    return (of.reshape(bsz, n_tok, GLA_V) * jax.nn.silu(r.astype(jnp.float32))).astype(r.dtype)


def gla_mixer(lat, cpar, rope, wg2_fw, bg_fw, wg2_bw, bg_bw, norm_g, need_ctx_out):
    flip = lambda t: jnp.flip(t, axis=1)
    qc, kc, vc, gfc, gbc = gla_prep(*cpar[:5], wg2_fw, bg_fw, wg2_bw, bg_bw, None)
    ql, kl, vl, gfl, gbl = gla_prep(*lat[:5], wg2_fw, bg_fw, wg2_bw, bg_bw, rope)
    s0 = jnp.zeros((qc.shape[0], GLA_HEADS, GLA_DK, GLA_DV), jnp.float32)
    oc_f, sc_f = gla_chunked(qc, kc, vc, gfc, s0)
    oc_b, sc_b = gla_chunked(flip(qc), flip(kc), flip(vc), flip(gbc), s0)
    ol_f, _ = gla_chunked(ql, kl, vl, gfl, sc_f)
    ol_b, _ = gla_chunked(flip(ql), flip(kl), flip(vl), flip(gbl), sc_b)
    out_lat = gla_output(ol_f + flip(ol_b), lat[5], norm_g)
    out_ctx = gla_output(oc_f + flip(oc_b), cpar[5], norm_g) if need_ctx_out else None
    return out_lat, out_ctx


def short_conv(b_gate, c_gate, xh, w, bias):
    return b_gate * dwconv3(c_gate * xh, w, bias)


def neighbourhood_attention(q, k, v, k_ctx, v_ctx, rpb):
    bsz, n_tok, _ = q.shape
    rows = n_tok // GRID_W
    kh = min(NA_WIN_ROWS, rows)
    kw = NA_WIN_COLS
    grid = lambda t: t.reshape(bsz, rows, GRID_W, NA_HEADS, NA_DH)
    qg = grid(q.astype(jnp.float32) * (NA_DH ** -0.5))
    kg, vg = grid(k), grid(v)
    kc = k_ctx.reshape(bsz, -1, NA_HEADS, NA_DH).astype(jnp.float32)
    vc = v_ctx.reshape(bsz, -1, NA_HEADS, NA_DH).astype(jnp.float32)
    col = jnp.arange(GRID_W)
    col_start = jnp.clip(col - kw // 2, 0, GRID_W - kw)
    col_mask = (col[None, :] >= col_start[:, None]) & (col[None, :] < col_start[:, None] + kw)
    dc_idx = jnp.clip(col[None, :] - col[:, None] + NA_WIN_COLS - 1, 0, 2 * NA_WIN_COLS - 2)
    rpb_cols = jnp.take(rpb.astype(jnp.float32), dc_idx, axis=2)

    def row_block(r):
        rs = jnp.clip(r - kh // 2, 0, rows - kh)
        q_r = lax.dynamic_index_in_dim(qg, r, axis=1, keepdims=False)
        k_band = lax.dynamic_slice_in_dim(kg, rs, kh, axis=1).astype(jnp.float32)
        v_band = lax.dynamic_slice_in_dim(vg, rs, kh, axis=1).astype(jnp.float32)
        dr_idx = rs + jnp.arange(kh) - r + NA_WIN_ROWS - 1
        bias = jnp.take(rpb_cols, dr_idx, axis=1)
        s_loc = jnp.einsum('bqhd,bikhd->bhqik', q_r, k_band) + jnp.transpose(bias, (0, 2, 1, 3))[None]
        s_loc = jnp.where(col_mask[None, None, :, None, :], s_loc, NEG_INF)
        s_ctx = jnp.einsum('bqhd,bchd->bhqc', q_r, kc)
        s = jnp.concatenate([s_loc.reshape(bsz, NA_HEADS, GRID_W, kh * GRID_W), s_ctx], axis=-1)
        p = jax.nn.softmax(s, axis=-1)
        p_loc = p[..., :kh * GRID_W].reshape(bsz, NA_HEADS, GRID_W, kh, GRID_W)
        p_ctx = p[..., kh * GRID_W:]
        o = jnp.einsum('bhqik,bikhd->bqhd', p_loc, v_band) + jnp.einsum('bhqc,bchd->bqhd', p_ctx, vc)
        return o.astype(q.dtype)

    out = lax.map(row_block, jnp.arange(rows))
    return jnp.moveaxis(out, 0, 1).reshape(bsz, n_tok, NA_W)


def context_attention(q, k, v):
    bsz, n_tok, _ = q.shape
    hs = lambda t: t.astype(jnp.float32).reshape(bsz, n_tok, NA_HEADS, NA_DH)
    s = jnp.einsum('bqhd,bkhd->bhqk', hs(q) * (NA_DH ** -0.5), hs(k))
    o = jnp.einsum('bhqk,bkhd->bqhd', jax.nn.softmax(s, axis=-1), hs(v))
    return o.reshape(bsz, n_tok, NA_W).astype(q.dtype)


def conv_ffn(h, w_up, cw, cb, w_down):
    u = dwconv3(h @ w_up, cw, cb)
    a, b = jnp.split(u, 2, axis=-1)
    return (jax.nn.silu(a) * b) @ w_down


def setup_inputs(seed: int = 0) -> dict:
    key = jax.random.key(seed)
    ks = jax.random.split(key, 24)
    nrm = lambda k, shape, s: jax.random.normal(k, shape, jnp.float32) * s
    return {
        "x": nrm(ks[0], (BATCH, SEQ, D_MODEL), 1.0),
        "c": nrm(ks[1], (BATCH, D_MODEL), 1.0),
        "ctx": nrm(ks[2], (BATCH, CTX_LEN, D_MODEL), 1.0),
        "c_ctx": nrm(ks[3], (D_MODEL,), 1.0),
        "w_ada": nrm(ks[4], (DEPTH, D_MODEL, 6 * D_MODEL), 0.5 * D_MODEL ** -0.5),
        "b_ada": nrm(ks[5], (DEPTH, 6 * D_MODEL), 0.02),
        "norm_mix_g": 1.0 + nrm(ks[6], (DEPTH, D_MODEL), 0.05),
        "norm_ffn_g": 1.0 + nrm(ks[7], (DEPTH, D_MODEL), 0.05),
        "w_in": nrm(ks[8], (DEPTH, D_MODEL, IN_WIDTH), D_MODEL ** -0.5),
        "gla_wg2_fw": nrm(ks[9], (DEPTH, GLA_LOWRANK, GLA_QK), GLA_LOWRANK ** -0.5),
        "gla_bg_fw": nrm(ks[10], (DEPTH, GLA_QK), 0.02),
        "gla_wg2_bw": nrm(ks[11], (DEPTH, GLA_LOWRANK, GLA_QK), GLA_LOWRANK ** -0.5),
        "gla_bg_bw": nrm(ks[12], (DEPTH, GLA_QK), 0.02),
        "gla_norm_g": 1.0 + nrm(ks[13], (DEPTH, GLA_DV), 0.05),
        "sc_conv_w": nrm(ks[14], (DEPTH, 3, SC_WIDTH), 3 ** -0.5),
        "sc_conv_b": nrm(ks[15], (DEPTH, SC_WIDTH), 0.02),
        "na_rpb": nrm(ks[16], (DEPTH, NA_HEADS, 2 * NA_WIN_ROWS - 1, 2 * NA_WIN_COLS - 1), 0.02),
        "w_out": nrm(ks[17], (DEPTH, MIX_WIDTH, D_MODEL), MIX_WIDTH ** -0.5),
        "ffn_w_up": nrm(ks[18], (DEPTH, D_MODEL, 2 * FFN_DIM), D_MODEL ** -0.5),
        "ffn_conv_w": nrm(ks[19], (DEPTH, 3, 2 * FFN_DIM), 3 ** -0.5),
        "ffn_conv_b": nrm(ks[20], (DEPTH, 2 * FFN_DIM), 0.02),
        "ffn_w_down": nrm(ks[21], (DEPTH, FFN_DIM, D_MODEL), FFN_DIM ** -0.5),
        "final_norm_g": 1.0 + nrm(ks[22], (D_MODEL,), 0.05),
    }


def reference(x, c, ctx, c_ctx, w_ada, b_ada, norm_mix_g, norm_ffn_g, w_in, gla_wg2_fw, gla_bg_fw,
              gla_wg2_bw, gla_bg_bw, gla_norm_g, sc_conv_w, sc_conv_b, na_rpb, w_out, ffn_w_up,
              ffn_conv_w, ffn_conv_b, ffn_w_down, final_norm_g):
    n_lat = x.shape[1]
    rope = axial_rope(n_lat, GLA_DK)
    xc = ctx
    for layer in range(DEPTH):
        update_ctx = layer < DEPTH - 1
        mod = jax.nn.silu(c) @ w_ada[layer] + b_ada[layer]
        sh1, sc1, g1, sh2, sc2, g2 = jnp.split(mod[:, None, :], 6, axis=-1)
        modc = jax.nn.silu(c_ctx) @ w_ada[layer] + b_ada[layer]
        sh1c, sc1c, g1c, sh2c, sc2c, g2c = jnp.split(modc, 6, axis=-1)

        u = modulate(rmsnorm(x, norm_mix_g[layer]), sh1, sc1) @ w_in[layer]
        uc = modulate(rmsnorm(xc, norm_mix_g[layer]), sh1c, sc1c) @ w_in[layer]
        lat = split_cols(u)
        cpar = split_cols(uc)
        gla_lat, gla_ctx = gla_mixer(lat[:6], cpar[:6], rope, gla_wg2_fw[layer], gla_bg_fw[layer],
                                     gla_wg2_bw[layer], gla_bg_bw[layer], gla_norm_g[layer], update_ctx)
        sc_lat = short_conv(lat[6], lat[7], lat[8], sc_conv_w[layer], sc_conv_b[layer])
        na_lat = neighbourhood_attention(lat[9], lat[10], lat[11], cpar[10], cpar[11], na_rpb[layer])
        x = x + g1 * (jnp.concatenate([gla_lat, sc_lat, na_lat], axis=-1) @ w_out[layer])
        h = modulate(rmsnorm(x, norm_ffn_g[layer]), sh2, sc2)
        x = x + g2 * conv_ffn(h, ffn_w_up[layer], ffn_conv_w[layer], ffn_conv_b[layer], ffn_w_down[layer])

        if update_ctx:
            sc_ctx = short_conv(cpar[6], cpar[7], cpar[8], sc_conv_w[layer], sc_conv_b[layer])
            na_ctx = context_attention(cpar[9], cpar[10], cpar[11])
            xc = xc + g1c * (jnp.concatenate([gla_ctx, sc_ctx, na_ctx], axis=-1) @ w_out[layer])
            hc = modulate(rmsnorm(xc, norm_ffn_g[layer]), sh2c, sc2c)
            xc = xc + g2c * conv_ffn(hc, ffn_w_up[layer], ffn_conv_w[layer], ffn_conv_b[layer],
                                     ffn_w_down[layer])
    return rmsnorm(x, final_norm_g)
```

```python
import numpy as np
from contextlib import ExitStack
import concourse.bass as bass
import concourse.mybir as mybir
from concourse.bass_utils import run_bass_kernel_spmd

F32 = mybir.dt.float32
BF16 = mybir.dt.bfloat16
AF = mybir.ActivationFunctionType
ALU = mybir.AluOpType
AX = mybir.AxisListType

D = 1024
KD = 8
LC = 256
GRID_W = 64
EPS = 1e-6
FFN = 2816
NF = 22
INW = 3104
ENGS = ("pe", "act", "dve", "pool", "sp")


class Buf:
    __slots__ = ("name", "lw", "rd", "dsem", "persist")

    def __init__(self, name, persist=False):
        self.name = name
        self.lw = None
        self.rd = []
        self.dsem = None
        self.persist = persist


class Ins:
    __slots__ = ("eng", "fn", "deps", "signal", "count", "dma", "dsem", "dval")

    def __init__(self, eng, fn, dma):
        self.eng = eng
        self.fn = fn
        self.deps = []
        self.signal = False
        self.count = 0
        self.dma = dma
        self.dsem = None
        self.dval = 0


class Prog:
    def __init__(self):
        self.q = {e: [] for e in ENGS}
        self.dma_cnt = []
        self.dma_last = []
        self.dma_persist = []
        self.last = {e: None for e in ENGS}
        self.free_slots = {"sp": [], "pool": [], "act": []}
        self.pe_fence = False
        self.phase_slots = []

    def _dsem_for(self, buf, eng):
        if buf.dsem is None:
            if (not buf.persist) and self.free_slots[eng]:
                buf.dsem = self.free_slots[eng].pop()
            else:
                buf.dsem = len(self.dma_cnt)
                self.dma_cnt.append(0)
                self.dma_last.append(None)
                self.dma_persist.append(buf.persist)
            if not buf.persist:
                self.phase_slots.append((eng, buf.dsem))
        return buf.dsem

    def emit(self, eng, fn, reads=(), writes=(), dma_key=None):
        ins = Ins(eng, fn, dma_key is not None)
        deps = []
        raw = set()
        for b in reads:
            if b.lw is not None:
                deps.append(b.lw)
                raw.add(id(b.lw))
        for b in writes:
            if b.lw is not None:
                deps.append(b.lw)
            deps.extend(b.rd)
        if dma_key is None:
            deps = [d for d in deps if d.dma or d.eng != eng or eng != "pe"]
            if eng == "pe" and self.pe_fence and self.last["pe"] is not None:
                deps.append(self.last["pe"])
                self.pe_fence = False

        if dma_key is not None:
            s = self._dsem_for(dma_key, eng)
            ins.dsem = s
            self.dma_cnt[s] += 16
            ins.dval = self.dma_cnt[s]
            if self.dma_last[s] is not None:
                deps.append(self.dma_last[s])
            self.dma_last[s] = ins
        seen = set()
        for d in deps:
            if d is ins or id(d) in seen:
                continue
            seen.add(id(d))
            if not d.dma:
                d.signal = True
            ins.deps.append(d)
        for b in reads:
            b.rd.append(ins)
        for b in writes:
            b.lw = ins
            b.rd = []
        self.q[eng].append(ins)
        if dma_key is None:
            self.last[eng] = ins
        return ins

    def barrier(self):
        deps = [self.last[e] for e in ENGS if self.last[e] is not None]
        deps += [d for d, p in zip(self.dma_last, self.dma_persist) if d is not None and not p]
        for e in ("act", "dve", "pool", "sp"):
            ins = Ins(e, None, False)
            for d in deps:
                if not d.dma:
                    d.signal = True
                ins.deps.append(d)
            self.q[e].append(ins)
        for e_, sl in self.phase_slots:
            self.free_slots[e_].append(sl)
        self.phase_slots = []

    def replay(self, nc, stack, block, final_waits=()):
        for e in ENGS:
            c = 0
            for ins in self.q[e]:
                if (not ins.dma) and ins.signal:
                    c += 1
                    ins.count = c
        esem = {e: stack.enter_context(nc.semaphore("es_" + e)) for e in ENGS}
        dsem = [stack.enter_context(nc.semaphore("ds_%d" % i)) for i in range(len(self.dma_cnt))]
        prog = self

        def run(engname, engobj):
            waited_e = {e: 0 for e in ENGS}
            waited_d = [0] * len(prog.dma_cnt)
            for ins in prog.q[engname]:
                for d in ins.deps:
                    if d.dma:
                        if waited_d[d.dsem] < d.dval:
                            engobj.wait_ge(dsem[d.dsem], d.dval)
                            waited_d[d.dsem] = d.dval
                    else:
                        if waited_e[d.eng] < d.count:
                            engobj.wait_ge(esem[d.eng], d.count)
                            waited_e[d.eng] = d.count
                if ins.fn is None:
                    continue
                r = ins.fn(engobj)
                if ins.dma:
                    r.then_inc(dsem[ins.dsem], 16)
                elif ins.signal:
                    r.then_inc(esem[ins.eng], 1)
            if engname == "sp":
                for d in final_waits:
                    engobj.wait_ge(dsem[d.dsem], d.dval)

        block.tensor(lambda t: run("pe", t))
        block.scalar(lambda t: run("act", t))
        block.vector(lambda t: run("dve", t))
        block.gpsimd(lambda t: run("pool", t))
        block.sync(lambda t: run("sp", t))


class Tl:
    __slots__ = ("t", "b")

    def __init__(self, t, name, persist=False):
        self.t = t
        self.b = Buf(name, persist)


class Rot:
    def __init__(self, items):
        self.items = items
        self.i = 0

    def next(self):
        r = self.items[self.i % len(self.items)]
        self.i += 1
        return r


def build_program(L, DEPTH, stop=None):
    import os
    stop = stop or os.environ.get('KSTOP')
    T = LC + L
    NCH = T // 128
    NB = L // 128
    nc = bass.Bass("TRN2", target_bir_lowering=False)
    P = Prog()

    def din(name, shape, dt=F32):
        return nc.dram_tensor(name, list(shape), dt, kind="ExternalInput").ap()

    def dscr(name, shape, dt):
        return nc.dram_tensor(name, list(shape), dt).ap()

    xin = din("xin", [D, T])
    cT = din("cT", [128, 16])
    wada = din("wada", [DEPTH * D, 6 * D])
    bada = din("bada", [128, DEPTH * 96])
    gmix = din("gmix", [128, DEPTH * 16])
    gffn = din("gffn", [128, DEPTH * 16])
    gfin = din("gfin", [128, 8])
    win = din("win", [DEPTH * D, INW])
    wg = din("wg", [33, DEPTH * 384])
    gng = din("gng", [128, DEPTH * 384])
    scw = din("scw", [128, DEPTH * 8])
    nab = din("nab", [DEPTH * 5 * 128, 6 * 5 * 128])
    wout = din("wout", [DEPTH * D, D])
    wup = din("wup", [DEPTH * D, 2 * FFN])
    fcw = din("fcw", [128, DEPTH * 176])
    wdown = din("wdown", [DEPTH * FFN, D])
    rope = din("rope", [T, 576])
    ctri = din("ctri", [128, 512])
    cmask = din("cmask", [128, 2 * 768])
    cmisc = din("cmisc", [128, 128 + 128 + 384])
    out = nc.dram_tensor("out", [D, L], F32, kind="ExternalOutput").ap()

    XA = dscr("XA", [D, T], F32)
    XB = dscr("XB", [D, T], F32)
    scu = dscr("scu", [768, T], F32)
    naq = dscr("naq", [384, T], BF16)
    nak = dscr("nak", [384, T], BF16)
    nav = dscr("nav", [T, 390], BF16)
    gbT = dscr("gbT", [NCH * 96, 512], BF16)
    gkv = dscr("gkv", [T, 576], BF16)
    gof = dscr("gof", [T, 384], F32)
    gr = dscr("gr", [T, 384], F32)
    mixT = dscr("mixT", [D, T], BF16)
    wbf_in = dscr("wbf_in", [DEPTH * D, INW], BF16)
    wbf_out = dscr("wbf_out", [DEPTH * D, D], BF16)
    wbf_up = dscr("wbf_up", [DEPTH * D, 2 * FFN], BF16)
    wbf_dn = dscr("wbf_dn", [DEPTH * FFN, D], BF16)

    with ExitStack() as top:
        E = top.enter_context

        uid = [0]

        def sb(stack, name, shape, dt, persist=False):
            uid[0] += 1
            nm = "%s_%d" % (name, uid[0])
            return Tl(stack.enter_context(nc.sbuf_tensor(nm, list(shape), dt)), nm, persist)

        psum = E(nc.psum_tensor("psum", [128, 4096], F32))
        psum_bf = psum.bitcast(BF16)
        PB = [Buf("bank%d" % i, True) for i in range(8)]

        def ps(bank, c0, c1, p0=0, p1=128):
            return psum[p0:p1, bank * 512 + c0: bank * 512 + c1]

        def psb(bank, c0, c1, p0=0, p1=128):
            return psum_bf[p0:p1, bank * 1024 + c0: bank * 1024 + c1]

        arena = sb(top, "arena", [128, 8 * 2 * FFN], BF16, True)
        WOUT0 = 8 * INW
        cst = sb(top, "cst", [128, 8], F32, True)
        ones_bf = sb(top, "ones_bf", [128, 128], BF16, True)
        ident = sb(top, "ident", [128, 128], BF16, True)
        bmask = sb(top, "bmask", [128, 384], BF16, True)
        ones_f = sb(top, "ones_f", [128, 2], F32, True)
        mod = sb(top, "mod", [128, DEPTH * 96], F32, True)
        A1 = sb(top, "A1", [128, DEPTH * 16], F32, True)
        A2 = sb(top, "A2", [128, DEPTH * 16], F32, True)
        gmx = sb(top, "gmx", [128, DEPTH * 16], F32, True)
        gff = sb(top, "gff", [128, DEPTH * 16], F32, True)
        gfn = sb(top, "gfn", [128, 8], F32, True)
        scwt = sb(top, "scwt", [128, DEPTH * 8], F32, True)
        wgt = sb(top, "wgt", [33, DEPTH * 384], BF16, True)
        decb = sb(top, "decb", [96, NCH * 2], F32, True)

        def modcol(l, j, s):
            c = l * 96 + j * 2 + s
            return mod.t[:, c:c + 1]

        def acol(A, l, k, s):
            c = l * 16 + k * 2 + s
            return A.t[:, c:c + 1]

        def dma(eng, out_ap, in_ap, key, reads=(), writes=()):
            return P.emit(eng, lambda e: e.dma_start(out=out_ap, in_=in_ap), reads=reads, writes=writes, dma_key=key)

        def mm(out_ap, lhsT, rhs, start, stop, reads, writes, fence=False):
            if fence:
                P.pe_fence = True
            r = P.emit("pe", lambda e: e.matmul(out_ap, lhsT=lhsT, rhs=rhs, start=start, stop=stop),
                       reads=reads, writes=writes)
            if fence:
                P.pe_fence = True
            return r

        def tr(out_ap, in_ap, reads, writes):
            return P.emit("pe", lambda e: e.matmul(out_ap, lhsT=in_ap, rhs=ident.t[:], start=True, stop=True),
                          reads=list(reads) + [ident.b], writes=writes)

        def act(out_ap, in_ap, func, reads, writes, bias=None, scale=None):
            kw = {}
            if bias is not None:
                kw["bias"] = bias
            if scale is not None:
                kw["scale"] = scale
            return P.emit("act", lambda e: e.activation(out=out_ap, in_=in_ap, func=func, **kw), reads=reads,
                          writes=writes)

        def tt(eng, out_ap, a, b, op, reads, writes):
            return P.emit(eng, lambda e: e.tensor_tensor(out=out_ap, in0=a, in1=b, op=op), reads=reads, writes=writes)

        def ts(eng, out_ap, a, s1, op0, reads, writes, s2=None, op1=None):
            if op1 is None:
                return P.emit(eng, lambda e: e.tensor_scalar(out=out_ap, in0=a, scalar1=s1, scalar2=None, op0=op0),
                              reads=reads, writes=writes)
            return P.emit(eng, lambda e: e.tensor_scalar(out=out_ap, in0=a, scalar1=s1, scalar2=s2, op0=op0, op1=op1),
                          reads=reads, writes=writes)

        def stt(out_ap, a, s, b, op0, op1, reads, writes):
            return P.emit("dve", lambda e: e.scalar_tensor_tensor(out=out_ap, in0=a, scalar=s, in1=b, op0=op0, op1=op1),
                          reads=reads, writes=writes)

        def cp(eng, out_ap, in_ap, reads, writes):
            if eng == "act":
                return P.emit("act", lambda e: e.copy(out=out_ap, in_=in_ap), reads=reads, writes=writes)
            return P.emit(eng, lambda e: e.tensor_copy(out=out_ap, in_=in_ap), reads=reads, writes=writes)

        def mset(eng, ap, val, writes):
            return P.emit(eng, lambda e: e.memset(ap, val), writes=writes)

        def rows3(ap2d, p=128):
            return ap2d.rearrange("(m p) t -> p m t", p=p)

        mset("dve", cst.t[:, 0:1], EPS, [cst.b])
        mset("dve", cst.t[:, 1:2], float(np.log(32.0 ** -0.5)), [cst.b])
        mset("dve", cst.t[:, 2:3], 1.0, [cst.b])
        mset("dve", ones_f.t[:], 1.0, [ones_f.b])
        dma("pool", ones_bf.t[:], cmisc[:, 0:128], ones_bf.b, writes=[ones_bf.b])
        dma("pool", ident.t[:], cmisc[:, 128:256], ident.b, writes=[ident.b])
        dma("pool", bmask.t[:], cmisc[:, 256:640], bmask.b, writes=[bmask.b])
        dma("pool", wgt.t[:], wg, wgt.b, writes=[wgt.b])
        dma("sp", gmx.t[:], gmix, gmx.b, writes=[gmx.b])
        dma("sp", gff.t[:], gffn, gff.b, writes=[gff.b])
        dma("sp", gfn.t[:], gfin, gfn.b, writes=[gfn.b])
        dma("sp", scwt.t[:], scw, scwt.b, writes=[scwt.b])

        wcb = {}

        def convert_weights(l):
            for name, src, dst, rows in (("in", win, wbf_in, D), ("out", wout, wbf_out, D), ("up", wup, wbf_up, D),
                                         ("dn", wdown, wbf_dn, FFN)):
                for hf in range(2):
                    b = Buf("wcv_%s_%d_%d" % (name, l, hf), True)
                    wcb[(name, l, hf)] = b
                    r0 = l * rows + hf * (rows // 2)
                    r1 = l * rows + (hf + 1) * (rows // 2)
                    dma("pool", dst[r0:r1, :], src[r0:r1, :], b, writes=[b])

        def load_win(l):
            dma("sp", arena.t[:, 0:8 * INW].rearrange("p (k c) -> p k c", k=8), rows3(wbf_in[l * D:(l + 1) * D, :]),
                arena.b, reads=[wcb[("in", l, 0)], wcb[("in", l, 1)]], writes=[arena.b])

        def load_wout(l):
            dma("sp", arena.t[:, WOUT0:WOUT0 + 8 * D].rearrange("p (k c) -> p k c", k=8),
                rows3(wbf_out[l * D:(l + 1) * D, :]), arena.b, reads=[wcb[("out", l, 0)], wcb[("out", l, 1)]],
                writes=[arena.b])

        def load_wup(l):
            dma("sp", arena.t[:, 0:16 * FFN].rearrange("p (k c) -> p k c", k=8), rows3(wbf_up[l * D:(l + 1) * D, :]),
                arena.b, reads=[wcb[("up", l, 0)], wcb[("up", l, 1)]], writes=[arena.b])

        def load_wdown(l, wdn):
            dma("sp", wdn.t[:, :].rearrange("p (c d) -> p c d", c=NF), rows3(wbf_dn[l * FFN:(l + 1) * FFN, :]), wdn.b,
                reads=[wcb[("dn", l, 0)], wcb[("dn", l, 1)]], writes=[wdn.b])

        convert_weights(0)

        with ExitStack() as ph:
            cin = sb(ph, "cin", [128, 16], F32)
            csl = sb(ph, "csl", [128, 16], BF16)
            bad = sb(ph, "bad", [128, DEPTH * 96], F32)
            wst = [sb(ph, "wst%d" % i, [128, 8 * 768], BF16) for i in range(2)]
            dma("sp", cin.t[:], cT, cin.b, writes=[cin.b])
            dma("sp", bad.t[:], bada, bad.b, writes=[bad.b])
            act(csl.t[:], cin.t[:], AF.Silu, [cin.b], [csl.b])
            pi = 0
            for l in range(DEPTH):
                bank = l % 2
                for piece in range(8):
                    w = wst[pi % 2]
                    pi += 1
                    for k in range(KD):
                        dma("pool", w.t[:, k * 768:(k + 1) * 768],
                            wada[l * D + k * 128: l * D + (k + 1) * 128, piece * 768:(piece + 1) * 768], w.b,
                            writes=[w.b])
                    for jj in range(6):
                        j = piece * 6 + jj
                        for k in range(KD):
                            mm(ps(bank, j * 2, j * 2 + 2), w.t[:, k * 768 + jj * 128: k * 768 + (jj + 1) * 128],
                               csl.t[:, k * 2:(k + 1) * 2], k == 0, k == KD - 1, [w.b, csl.b], [PB[bank]])
                tt("dve", mod.t[:, l * 96:(l + 1) * 96], ps(bank, 0, 96), bad.t[:, l * 96:(l + 1) * 96], ALU.add,
                   [PB[bank], bad.b], [mod.b])
                stt(A1.t[:, l * 16:(l + 1) * 16], mod.t[:, l * 96 + 16: l * 96 + 32], 1.0, gmx.t[:, l * 16:(l + 1) * 16],
                    ALU.add, ALU.mult, [mod.b, gmx.b], [A1.b])
                stt(A2.t[:, l * 16:(l + 1) * 16], mod.t[:, l * 96 + 64: l * 96 + 80], 1.0, gff.t[:, l * 16:(l + 1) * 16],
                    ALU.add, ALU.mult, [mod.b, gff.b], [A2.b])
            P.barrier()

        def norm_mod(getx, n, A, l, shj, s, hT, sqs, sd, rstd, tmps, ssbank):
            for k in range(KD):
                xb, xap = getx(k)
                sq = sqs.next()
                act(sq.t[:, :n], xap, AF.Square, [xb.b], [sq.b])
                mm(ps(ssbank, 0, n), ones_bf.t[:], sq.t[:, :n], k == 0, k == KD - 1, [ones_bf.b, sq.b], [PB[ssbank]])
            act(sd.t[:, :n], ps(ssbank, 0, n), AF.Sqrt, [PB[ssbank], cst.b], [sd.b], bias=cst.t[:, 0:1], scale=1.0 / D)
            P.emit("dve", lambda e: e.reciprocal(out=rstd.t[:, :n], in_=sd.t[:, :n]), reads=[sd.b], writes=[rstd.b])
            for k in range(KD):
                xb, xap = getx(k)
                tm = tmps.next()
                stt(tm.t[:, :n], xap, acol(A, l, k, s), rstd.t[:, :n], ALU.mult, ALU.mult, [xb.b, A.b, rstd.b], [tm.b])
                act(hT.t[:, k, :n], tm.t[:, :n], AF.Identity, [tm.b, mod.b], [hT.b], bias=modcol(l, shj + k, s), scale=1.0)

        def xloader(X, lo, n, xks):
            def getx(k):
                xk = xks.next()
                dma("sp", xk.t[:, :n], X[k * 128:(k + 1) * 128, lo:lo + n], xk.b, writes=[xk.b])
                return xk, xk.t[:, :n]
            return getx

        tiles512 = [(0, LC, 1)] + [(LC + i * 512, 512, 0) for i in range(L // 512)]

        load_win(0)
        load_wout(0)
        for l in range(DEPTH):
            X0 = xin if l == 0 else XA
            if l + 1 < DEPTH:
                convert_weights(l + 1)

            if stop == 'PRO':
                break
            with ExitStack() as ph:
                xks = Rot([sb(ph, "xk%d" % i, [128, 512], F32) for i in range(3)])
                tri = sb(ph, "tri", [128, 512], F32)
                msk = sb(ph, "msk", [128, 384], BF16)
                dma("sp", tri.t[:], ctri, tri.b, writes=[tri.b])
                dma("pool", msk.t[:], cmask[:, 0:384], msk.b, writes=[msk.b])
                sqs = Rot([sb(ph, "sq%d" % i, [128, 512], BF16) for i in range(3)])
                sd = sb(ph, "sd", [128, 512], F32)
                rstd = sb(ph, "rstd", [128, 512], F32)
                tmps = Rot([sb(ph, "tm%d" % i, [128, 512], F32) for i in range(2)])
                hTs = [sb(ph, "hT%d" % i, [128, 8, 512], BF16) for i in range(2)]
                scos = Rot([sb(ph, "sco%d" % i, [128, 512], F32) for i in range(3)])
                nqks = Rot([sb(ph, "nqk%d" % i, [128, 512], BF16) for i in range(3)])
                glrs = [sb(ph, "glr%d" % i, [33, 512], BF16) for i in range(2)]

                def pool_(name, shape, dt, depth):
                    return [sb(ph, "%s%d" % (name, i), shape, dt) for i in range(depth)]
                rps = pool_("rp", [128, 576], F32, 2)
                qks = pool_("qk", [128, 384], F32, 2)
                kvs = pool_("kv", [128, 576], BF16, 6)
                rrs = pool_("rr", [128, 384], F32, 2)
                nvs = pool_("nv", [128, 390], BF16, 2)
                ees = pool_("ee", [128, 384], F32, 2)
                lls = pool_("ll", [128, 384], F32, 2)
                fac = sb(ph, "fac", [128, 6, 192], F32)
                tcs = sb(ph, "tcs", [128, 384], F32)
                m12 = sb(ph, "m12", [128, 2, 192], F32)
                rqs = pool_("rq", [128, 384], F32, 2)
                gps = pool_("gp", [128, 4, 192], BF16, 2)
                gTfs = pool_("gTf", [96, 512], BF16, 3)
                gTbs = pool_("gTb", [96, 512], BF16, 2)
                atms = pool_("atm", [128, 6, 128], BF16, 2)
                kefs = pool_("kef", [128, 192], BF16, 4)
                decs = pool_("dec", [96, 4], F32, 4)
                ofs = pool_("of", [128, 384], F32, 2)
                um = sb(ph, "um", [96, 384], F32)
                Sf = sb(ph, "Sf", [96, 384], F32)
                Sfb = sb(ph, "Sfb", [96, 384], BF16)
                for g_ in glrs:
                    mset("pool", g_.t[32:33, :], 1.0, [g_.b])
                for n_ in nvs:
                    mset("pool", n_.t[:], 1.0, [n_.b])
                mset("dve", Sf.t[:], 0.0, [Sf.b])
                mset("dve", Sfb.t[:], 0.0, [Sfb.b])
                fmb = Rot([1, 2])
                tmb = Rot([3, 4])
                evr = Rot(["act", "dve"])

                def tile_norm(ti):
                    t0, n, s = tiles512[ti]
                    norm_mod(xloader(X0, t0, n, xks), n, A1, l, 0, s, hTs[ti % 2], sqs, sd, rstd, tmps, 0)

                def tile_fm(ti):
                    t0, n, s = tiles512[ti]
                    hT = hTs[ti % 2]
                    glr = glrs[ti % 2]
                    for m in range(12):
                        bk = fmb.next()
                        c0 = 1536 + m * 128
                        for k in range(KD):
                            mm(ps(bk, 0, n), arena.t[:, k * INW + c0: k * INW + c0 + 128], hT.t[:, k, :n], k == 0,
                               k == KD - 1, [arena.b, hT.b], [PB[bk]])
                        if m < 6:
                            sco = scos.next()
                            cp(evr.next(), sco.t[:, :n], ps(bk, 0, n), [PB[bk]], [sco.b])
                            dma("sp", scu[m * 128:(m + 1) * 128, t0:t0 + n], sco.t[:, :n], sco.b, reads=[sco.b])
                        else:
                            nqk = nqks.next()
                            cp(evr.next(), nqk.t[:, :n], ps(bk, 0, n), [PB[bk]], [nqk.b])
                            dst = naq if m < 9 else nak
                            mm_ = (m - 6) % 3
                            dma("sp", dst[mm_ * 128:(mm_ + 1) * 128, t0:t0 + n], nqk.t[:, :n], nqk.b, reads=[nqk.b])
                    bk = fmb.next()
                    for k in range(KD):
                        mm(ps(bk, 0, n, 0, 32), arena.t[:, k * INW + 3072: k * INW + 3104], hT.t[:, k, :n], k == 0,
                           k == KD - 1, [arena.b, hT.b], [PB[bk]])
                    cp(evr.next(), glr.t[0:32, :n], ps(bk, 0, n, 0, 32), [PB[bk]], [glr.b])

                chunks = []
                for ti, (t0, n, s) in enumerate(tiles512):
                    for c in range(n // 128):
                        chunks.append((ti, c))

                def st0(i):
                    ti, c = chunks[i]
                    t0 = tiles512[ti][0]
                    tk0 = t0 + c * 128
                    cs = slice(c * 128, (c + 1) * 128)
                    hT = hTs[ti % 2]
                    qk, kv, rr, nv, rp = qks[i % 2], kvs[i % 6], rrs[i % 2], nvs[i % 2], rps[i % 2]
                    dma("sp", rp.t[:], rope[tk0:tk0 + 128, :], rp.b, writes=[rp.b])
                    for g in range(4):
                        bk = tmb.next()
                        for k in range(KD):
                            mm(ps(bk, 0, 384), hT.t[:, k, cs], arena.t[:, k * INW + g * 384: k * INW + (g + 1) * 384],
                               k == 0, k == KD - 1, [arena.b, hT.b], [PB[bk]])
                        if g == 0:
                            cp("act", qk.t[:], ps(bk, 0, 384), [PB[bk]], [qk.b])
                        elif g == 1:
                            cp("dve", kv.t[:, 0:384], ps(bk, 0, 384), [PB[bk]], [kv.b])
                        elif g == 2:
                            cp("act", rr.t[:], ps(bk, 0, 384), [PB[bk]], [rr.b])
                        else:
                            cp("dve", nv.t[:].rearrange("p (h e) -> p h e", h=6)[:, :, 0:64],
                               ps(bk, 0, 384).rearrange("p (h e) -> p h e", h=6), [PB[bk]], [nv.b])
                    dma("sp", gr[tk0:tk0 + 128, :], rr.t[:], rr.b, reads=[rr.b])
                    dma("sp", nav[tk0:tk0 + 128, :], nv.t[:], nv.b, reads=[nv.b])

                def st1(i):
                    ti, c = chunks[i]
                    cs = slice(c * 128, (c + 1) * 128)
                    glr = glrs[ti % 2]
                    qk, rp, rq, ee, ll = qks[i % 2], rps[i % 2], rqs[i % 2], ees[i % 2], lls[i % 2]
                    mm(ps(5, 0, 384), glr.t[0:33, cs], wgt.t[:, l * 384:(l + 1) * 384], True, True, [glr.b, wgt.b],
                       [PB[5]])
                    act(ee.t[:], ps(5, 0, 384), AF.Exp, [PB[5]], [ee.b], scale=-1.0)
                    act(ll.t[:], ee.t[:], AF.Ln, [ee.b, cst.b], [ll.b], bias=cst.t[:, 2:3], scale=1.0)
                    q4 = qk.t[:].rearrange("p (h t e) -> p h t e", h=12, t=2)
                    t4 = tcs.t[:].rearrange("p (h t e) -> p h t e", h=12, t=2)
                    r4 = rq.t[:].rearrange("p (h t e) -> p h t e", h=12, t=2)
                    sn = rp.t[:, 384:576].rearrange("p (h e) -> p h e", h=12)
                    tt("pool", tcs.t[:], qk.t[:], rp.t[:, 0:384], ALU.mult, [qk.b, rp.b], [tcs.b])
                    tt("pool", m12.t[:, 0, :].rearrange("p (h e) -> p h e", h=12), q4[:, :, 1, :], sn, ALU.mult,
                       [qk.b, rp.b], [m12.b])
                    tt("pool", m12.t[:, 1, :].rearrange("p (h e) -> p h e", h=12), q4[:, :, 0, :], sn, ALU.mult,
                       [qk.b, rp.b], [m12.b])
                    tt("pool", r4[:, :, 0, :], t4[:, :, 0, :], m12.t[:, 0, :].rearrange("p (h e) -> p h e", h=12),
                       ALU.subtract, [tcs.b, m12.b], [rq.b])
                    tt("pool", r4[:, :, 1, :], t4[:, :, 1, :], m12.t[:, 1, :].rearrange("p (h e) -> p h e", h=12),
                       ALU.add, [tcs.b, m12.b], [rq.b])

                def st2(i):
                    ti, c = chunks[i]
                    ch = (tiles512[ti][0] // 128) + c
                    tk0 = ch * 128
                    ll, rq, gp, kv, kef, dec = lls[i % 2], rqs[i % 2], gps[i % 2], kvs[i % 6], kefs[i % 4], decs[i % 4]
                    P.pe_fence = True
                    for d_ in range(2):
                        bk = 6 + d_
                        for jj in range(2):
                            mm(ps(bk, jj * 192, (jj + 1) * 192), tri.t[:, (d_ * 2 + jj) * 128:(d_ * 2 + jj + 1) * 128],
                               ll.t[:, d_ * 192:(d_ + 1) * 192], True, True, [tri.b, ll.b], [PB[bk]], fence=True)
                    for d_ in range(2):
                        for g in range(2):
                            i_ = d_ * 2 + g
                            mm(ps(5, 400 + i_, 401 + i_, 0, 96), ll.t[:, d_ * 192 + g * 96: d_ * 192 + (g + 1) * 96],
                               ones_f.t[:, 0:1], True, True, [ll.b, ones_f.b], [PB[5]], fence=True)
                    P.pe_fence = True
                    act(dec.t[:], ps(5, 400, 404, 0, 96), AF.Exp, [PB[5]], [dec.b], scale=-1.0 / 16)
                    cp("dve", decb.t[:, ch * 2:(ch + 1) * 2], dec.t[:, 2:4], [dec.b], [decb.b])
                    for d_ in range(2):
                        bk = 6 + d_
                        act(fac.t[:, d_ * 3 + 0, :], ps(bk, 0, 192), AF.Exp, [PB[bk], cst.b], [fac.b],
                            bias=cst.t[:, 1:2], scale=-1.0 / 16)
                        act(fac.t[:, d_ * 3 + 1, :], ps(bk, 0, 192), AF.Exp, [PB[bk]], [fac.b], scale=1.0 / 16)
                        act(fac.t[:, d_ * 3 + 2, :], ps(bk, 192, 384), AF.Exp, [PB[bk]], [fac.b], scale=-1.0 / 16)
                    tt("dve", gp.t[:, 0, :], rq.t[:, 0:192], fac.t[:, 0, :], ALU.mult, [rq.b, fac.b], [gp.b])
                    tt("dve", gp.t[:, 1, :], rq.t[:, 192:384], fac.t[:, 1, :], ALU.mult, [rq.b, fac.b], [gp.b])
                    tt("dve", gp.t[:, 2, :], rq.t[:, 0:192], fac.t[:, 3, :], ALU.mult, [rq.b, fac.b], [gp.b])
                    tt("dve", gp.t[:, 3, :], rq.t[:, 192:384], fac.t[:, 4, :], ALU.mult, [rq.b, fac.b], [gp.b])
                    tt("dve", kv.t[:, 384:576], rq.t[:, 192:384], fac.t[:, 5, :], ALU.mult, [rq.b, fac.b], [kv.b])
                    tt("pool", kef.t[:], rq.t[:, 192:384], fac.t[:, 2, :], ALU.mult, [rq.b, fac.b], [kef.b])
                    dma("sp", gkv[tk0:tk0 + 128, :], kv.t[:], kv.b, reads=[kv.b])

                def st3(i):
                    ti, c = chunks[i]
                    ch = (tiles512[ti][0] // 128) + c
                    gp, gTf, gTb = gps[i % 2], gTfs[i % 3], gTbs[i % 2]
                    for j in range(4):
                        for g in range(2):
                            tb_ = 1 if j < 2 else 2
                            tc_ = ((j % 2) * 2 + g) * 128
                            tr(ps(tb_, tc_, tc_ + 128, 0, 96), gp.t[:, j, g * 96:(g + 1) * 96], [gp.b], [PB[tb_]])
                    cp("act", gTf.t[:], ps(1, 0, 512, 0, 96), [PB[1]], [gTf.b])
                    cp("dve", gTb.t[:], ps(2, 0, 512, 0, 96), [PB[2]], [gTb.b])
                    dma("sp", gbT[ch * 96:(ch + 1) * 96, :], gTb.t[:], gTb.b, reads=[gTb.b])

                def st4(i):
                    gTf, atm = gTfs[i % 3], atms[i % 2]
                    for h in range(6):
                        g, hp = h // 3, h % 3
                        bk = 6 + g
                        mm(ps(bk, hp * 128, (hp + 1) * 128), gTf.t[32 * hp:32 * hp + 32, (2 + g) * 128:(3 + g) * 128],
                           gTf.t[32 * hp:32 * hp + 32, g * 128:(g + 1) * 128], True, True, [gTf.b], [PB[bk]], fence=True)
                    for g in range(2):
                        tt("dve", atm.t[:, 3 * g:3 * g + 3, :], ps(6 + g, 0, 384).rearrange("p (h i) -> p h i", h=3),
                           msk.t[:, 0:384].rearrange("p (h i) -> p h i", h=3), ALU.mult, [PB[6 + g], msk.b], [atm.b])

                def st5(i):
                    ti, c = chunks[i]
                    ch = (tiles512[ti][0] // 128) + c
                    tk0 = ch * 128
                    gTf, atm, kv, kef, dec, of = gTfs[i % 3], atms[i % 2], kvs[i % 6], kefs[i % 4], decs[i % 4], ofs[i % 2]
                    bo = tmb.next()
                    for h in range(6):
                        g, hp = h // 3, h % 3
                        mm(ps(bo, h * 64, (h + 1) * 64), atm.t[:, h, :], kv.t[:, h * 64:(h + 1) * 64], True, False,
                           [atm.b, kv.b], [PB[bo]])
                        mm(ps(bo, h * 64, (h + 1) * 64), gTf.t[32 * hp:32 * hp + 32, g * 128:(g + 1) * 128],
                           Sfb.t[32 * hp:32 * hp + 32, g * 192 + hp * 64: g * 192 + (hp + 1) * 64], False, True,
                           [gTf.b, Sfb.b], [PB[bo]], fence=True)
                    cp("act", of.t[:], ps(bo, 0, 384), [PB[bo]], [of.b])
                    dma("sp", gof[tk0:tk0 + 128, :], of.t[:], of.b, reads=[of.b])
                    for g in range(2):
                        mm(ps(0, g * 192, (g + 1) * 192, 0, 96), kef.t[:, g * 96:(g + 1) * 96],
                           kv.t[:, g * 192:(g + 1) * 192], True, True, [kef.b, kv.b], [PB[0]])
                    tt("dve", um.t[:], ps(0, 0, 384, 0, 96), bmask.t[0:96, :], ALU.mult, [PB[0], bmask.b], [um.b])
                    for g in range(2):
                        stt(Sf.t[:, g * 192:(g + 1) * 192], Sf.t[:, g * 192:(g + 1) * 192], dec.t[:, g:g + 1],
                            um.t[:, g * 192:(g + 1) * 192], ALU.mult, ALU.add, [Sf.b, dec.b, um.b], [Sf.b])
                    cp("dve", Sfb.t[:], Sf.t[:], [Sf.b], [Sfb.b])

                stages = [st0, st1, st2, st3, st4, st5]
                if stop and stop.startswith('S1s'):
                    stages = stages[:int(stop[3:])]
                nchunks = len(chunks)
                tile_norm(0)
                for step in range(nchunks + len(stages) - 1):
                    if step < nchunks:
                        ti, c = chunks[step]
                        if c == 0:
                            tile_fm(ti)
                        if c == min(1, tiles512[ti][1] // 128 - 1) and ti + 1 < len(tiles512):
                            tile_norm(ti + 1)
                    for si, st in enumerate(stages):
                        i = step - si
                        if 0 <= i < nchunks:
                            st(i)
                P.barrier()

            if stop and stop.startswith('S1'):
                break
            with ExitStack() as ph:
                def pool_(name, shape, dt, depth):
                    return [sb(ph, "%s%d" % (name, i), shape, dt) for i in range(depth)]
                gTs = pool_("gT", [96, 512], BF16, 3)
                kvs = pool_("kvb", [128, 576], BF16, 3)
                ofs = pool_("ofb", [128, 384], F32, 3)
                rrs = pool_("rrb", [128, 384], F32, 4)
                atms = pool_("atmb", [128, 6, 128], BF16, 2)
                oos = pool_("oo", [128, 384], F32, 2)
                o2 = sb(ph, "o2", [128, 384], F32)
                ssq = sb(ph, "ssq", [128, 6], F32)
                rs = sb(ph, "rs", [128, 6], F32)
                on = sb(ph, "on", [128, 384], F32)
                sr = sb(ph, "sr", [128, 384], F32)
                gl = sb(ph, "gl", [128, 384], BF16)
                glTs = pool_("glT", [128, 3, 128], BF16, 2)
                um = sb(ph, "umb", [96, 384], F32)
                Sb = sb(ph, "Sb", [96, 384], F32)
                Sbb = sb(ph, "Sbb", [96, 384], BF16)
                msk = sb(ph, "mskb", [128, 384], BF16)
                gngt = sb(ph, "gngt", [128, 384], F32)
                dma("pool", msk.t[:], cmask[:, 768:1152], msk.b, writes=[msk.b])
                dma("sp", gngt.t[:], gng[:, l * 384:(l + 1) * 384], gngt.b, writes=[gngt.b])
                mset("dve", Sb.t[:], 0.0, [Sb.b])
                mset("dve", Sbb.t[:], 0.0, [Sbb.b])
                order = [1, 0] + list(range(NCH - 1, 1, -1))

                def g0(i):
                    ch = order[i]
                    tk0 = ch * 128
                    gT, kv, of, rr = gTs[i % 3], kvs[i % 3], ofs[i % 3], rrs[i % 4]
                    dma("sp", gT.t[:], gbT[ch * 96:(ch + 1) * 96, :], gT.b, writes=[gT.b])
                    dma("sp", kv.t[:], gkv[tk0:tk0 + 128, :], kv.b, writes=[kv.b])
                    dma("sp", of.t[:], gof[tk0:tk0 + 128, :], of.b, writes=[of.b])
                    dma("sp", rr.t[:], gr[tk0:tk0 + 128, :], rr.b, writes=[rr.b])

                def g1(i):
                    gT, atm = gTs[i % 3], atms[i % 2]
                    bA0 = 2 * (i % 2)
                    for h in range(6):
                        g, hp = h // 3, h % 3
                        bk = bA0 + g
                        mm(ps(bk, hp * 128, (hp + 1) * 128), gT.t[32 * hp:32 * hp + 32, (2 + g) * 128:(3 + g) * 128],
                           gT.t[32 * hp:32 * hp + 32, g * 128:(g + 1) * 128], True, True, [gT.b], [PB[bk]], fence=True)
                    for g in range(2):
                        tt("dve", atm.t[:, 3 * g:3 * g + 3, :], ps(bA0 + g, 0, 384).rearrange("p (h i) -> p h i", h=3),
                           msk.t[:, 0:384].rearrange("p (h i) -> p h i", h=3), ALU.mult, [PB[bA0 + g], msk.b], [atm.b])

                def g2(i):
                    ch = order[i]
                    gT, kv, of, atm, oo = gTs[i % 3], kvs[i % 3], ofs[i % 3], atms[i % 2], oos[i % 2]
                    bO, bU = 4 + (i % 2), 6
                    for h in range(6):
                        g, hp = h // 3, h % 3
                        mm(ps(bO, h * 64, (h + 1) * 64), atm.t[:, h, :], kv.t[:, h * 64:(h + 1) * 64], True, False,
                           [atm.b, kv.b], [PB[bO]])
                        mm(ps(bO, h * 64, (h + 1) * 64), gT.t[32 * hp:32 * hp + 32, g * 128:(g + 1) * 128],
                           Sbb.t[32 * hp:32 * hp + 32, g * 192 + hp * 64: g * 192 + (hp + 1) * 64], False, True,
                           [gT.b, Sbb.b], [PB[bO]], fence=True)
                    for g in range(2):
                        mm(ps(bU, g * 192, (g + 1) * 192, 0, 96), kv.t[:, 384 + g * 96: 384 + (g + 1) * 96],
                           kv.t[:, g * 192:(g + 1) * 192], True, True, [kv.b], [PB[bU]])
                    tt("dve", um.t[:], ps(bU, 0, 384, 0, 96), bmask.t[0:96, :], ALU.mult, [PB[bU], bmask.b], [um.b])
                    for g in range(2):
                        stt(Sb.t[:, g * 192:(g + 1) * 192], Sb.t[:, g * 192:(g + 1) * 192],
                            decb.t[:, ch * 2 + g: ch * 2 + g + 1], um.t[:, g * 192:(g + 1) * 192], ALU.mult, ALU.add,
                            [Sb.b, decb.b, um.b], [Sb.b])
                    cp("dve", Sbb.t[:], Sb.t[:], [Sb.b], [Sbb.b])
                    tt("dve", oo.t[:], ps(bO, 0, 384), of.t[:], ALU.add, [PB[bO], of.b], [oo.b])

                def g3(i):
                    ch = order[i]
                    tk0 = ch * 128
                    oo, rr, glT = oos[i % 2], rrs[i % 4], glTs[i % 2]
                    tt("pool", o2.t[:], oo.t[:], oo.t[:], ALU.mult, [oo.b], [o2.b])
                    P.emit("dve", lambda e: e.tensor_reduce(
                        out=ssq.t[:], in_=o2.t[:].rearrange("p (h e) -> p h e", h=6), axis=AX.X, op=ALU.add),
                        reads=[o2.b], writes=[ssq.b])
                    act(rs.t[:], ssq.t[:], AF.Sqrt, [ssq.b, cst.b], [rs.b], bias=cst.t[:, 0:1], scale=1.0 / 64)
                    P.emit("dve", lambda e: e.reciprocal(out=rs.t[:], in_=rs.t[:]), reads=[rs.b], writes=[rs.b])
                    act(sr.t[:], rr.t[:], AF.Silu, [rr.b], [sr.b])
                    for h in range(6):
                        stt(on.t[:, h * 64:(h + 1) * 64], oo.t[:, h * 64:(h + 1) * 64], rs.t[:, h:h + 1],
                            gngt.t[:, h * 64:(h + 1) * 64], ALU.mult, ALU.mult, [oo.b, rs.b, gngt.b], [on.b])
                    tt("pool", gl.t[:], on.t[:], sr.t[:], ALU.mult, [on.b, sr.b], [gl.b])
                    for m in range(3):
                        tr(ps(7, m * 128, (m + 1) * 128), gl.t[:, m * 128:(m + 1) * 128], [gl.b], [PB[7]])
                    cp("act", glT.t[:], ps(7, 0, 384).rearrange("p (m t) -> p m t", m=3), [PB[7]], [glT.b])
                    dma("sp", rows3(mixT[0:384, tk0:tk0 + 128]), glT.t[:], glT.b, reads=[glT.b])

                stages = [g0, g1, g2, g3]
                for step in range(NCH + len(stages) - 1):
                    for si, st in enumerate(stages):
                        i = step - si
                        if 0 <= i < NCH:
                            st(i)
                P.barrier()

            if stop == 'GB':
                break
            with ExitStack() as ph:
                nb = sb(ph, "nb", [128, 6 * 5 * 128], F32)
                krs = [sb(ph, "kr%d" % i, [128, 3, 128], BF16) for i in range(8)]
                vrs = [sb(ph, "vr%d" % i, [128, 390], BF16) for i in range(8)]
                kc = sb(ph, "kc", [128, 3, 256], BF16)
                vc = sb(ph, "vc", [128, 2, 390], BF16)
                qts = Rot([sb(ph, "qt%d" % i, [128, 3, 128], BF16) for i in range(2)])
                stmps = Rot([sb(ph, "stmp%d" % i, [128, 640], F32) for i in range(2)])
                pts = Rot([sb(ph, "pt%d" % i, [128, 7, 128], BF16) for i in range(3)])
                rcs = [sb(ph, "rc%d" % i, [128, 6], F32) for i in range(2)]
                onts = [sb(ph, "ont%d" % i, [128, 384], BF16) for i in range(2)]
                naTs = Rot([sb(ph, "naT%d" % i, [128, 3, 128], BF16) for i in range(2)])
                dma("sp", kc.t[:], rows3(nak[:, 0:LC]), kc.b, writes=[kc.b])
                dma("sp", vc.t[:], nav[0:LC, :].rearrange("(c p) f -> p c f", p=128), vc.b, writes=[vc.b])
                loaded = {}
                cur_pat = [None]

                def ensure_chunk(cid):
                    slot = cid % 8
                    if loaded.get(slot) != cid:
                        tk = LC + cid * 128
                        dma("sp", krs[slot].t[:], rows3(nak[:, tk:tk + 128]), krs[slot].b, writes=[krs[slot].b])
                        dma("sp", vrs[slot].t[:], nav[tk:tk + 128, :], vrs[slot].b, writes=[vrs[slot].b])
                        loaded[slot] = cid
                    return slot

                def pat_of(a):
                    if a == 0:
                        return 1
                    if a == 1:
                        return 2
                    if a == NB - 2:
                        return 3
                    if a == NB - 1:
                        return 4
                    return 0

                blocks = [("c", 0), ("c", 1)] + [("l", a) for a in range(NB)]
                binfo = {}

                def blk_setup(bi):
                    kind, a = blocks[bi]
                    qt = qts.items[bi % 2]
                    if kind == "c":
                        tq = a * 128
                        slots = []
                    else:
                        tq = LC + a * 128
                        w0 = min(max(a - 2, 0), NB - 5)
                        slots = [ensure_chunk(w0 + c) for c in range(5)]
                        pat = pat_of(a)
                        if cur_pat[0] != pat:
                            r0 = (l * 5 + pat) * 128
                            dma("sp", nb.t[:], nab[r0:r0 + 128, :], nb.b, writes=[nb.b])
                            cur_pat[0] = pat
                    dma("sp", qt.t[:], rows3(naq[:, tq:tq + 128]), qt.b, writes=[qt.b])
                    binfo[bi] = (tq, slots, qt)

                def sa(g):
                    bi, h = divmod(g, 6)
                    if h == 0:
                        blk_setup(bi)
                    tq, slots, qt = binfo[bi]
                    m, pb = h // 2, (h % 2) * 64
                    b0 = 2 * (g % 2)
                    pt = pts.items[g % 3]
                    P.pe_fence = True
                    for c, sl in enumerate(slots):
                        bk, off = (b0, c * 128) if c < 4 else (b0 + 1, 0)
                        mm(ps(bk, off, off + 128), krs[sl].t[pb:pb + 64, m, :], qt.t[pb:pb + 64, m, :], True, True,
                           [krs[sl].b, qt.b], [PB[bk]])
                    for c in range(2):
                        mm(ps(b0 + 1, 128 + c * 128, 256 + c * 128), kc.t[pb:pb + 64, m, c * 128:(c + 1) * 128],
                           qt.t[pb:pb + 64, m, :], True, True, [kc.b, qt.b], [PB[b0 + 1]])
                    if slots:
                        stmp = stmps.items[g % 2]
                        nbh = nb.t[:, h * 640:(h + 1) * 640]
                        stt(stmp.t[:, 0:512], ps(b0, 0, 512), 0.125, nbh[:, 0:512], ALU.mult, ALU.add,
                            [PB[b0], nb.b], [stmp.b])
                        stt(stmp.t[:, 512:640], ps(b0 + 1, 0, 128), 0.125, nbh[:, 512:640], ALU.mult, ALU.add,
                            [PB[b0 + 1], nb.b], [stmp.b])
                        act(pt.t[:, 0:5, :], stmp.t[:].rearrange("p (c q) -> p c q", c=5), AF.Exp, [stmp.b], [pt.b])
                    act(pt.t[:, 5:7, :], ps(b0 + 1, 128, 384).rearrange("p (c q) -> p c q", c=2), AF.Exp, [PB[b0 + 1]],
                        [pt.b], scale=0.125)

                def sc(g):
                    bi, h = divmod(g, 6)
                    P.pe_fence = True
                    tq, slots, qt = binfo[bi]
                    bOV = 4 + (bi % 2)
                    pt = pts.items[g % 3]
                    nk = len(slots) + 2
                    ops_ = [(c, vrs[sl].t[:, h * 65:(h + 1) * 65], vrs[sl].b) for c, sl in enumerate(slots)]
                    ops_ += [(5 + c, vc.t[:, c, h * 65:(h + 1) * 65], vc.b) for c in range(2)]
                    for i_, (pc, vap, vb) in enumerate(ops_):
                        mm(ps(bOV, h * 65, (h + 1) * 65), pt.t[:, pc, :], vap, i_ == 0, i_ == nk - 1, [pt.b, vb], [PB[bOV]])

                def tail(bi):
                    tq, slots, qt = binfo[bi]
                    bOV = 4 + (bi % 2)
                    bTP = 6 + (bi % 2)
                    naT = naTs.items[bi % 2]
                    ont = onts[bi % 2]
                    rc = rcs[bi % 2]
                    ov3 = ps(bOV, 0, 390).rearrange("p (h e) -> p h e", h=6)
                    P.emit("dve", lambda e: e.reciprocal(out=rc.t[:], in_=ov3[:, :, 64]), reads=[PB[bOV]], writes=[rc.b])
                    for h in range(6):
                        if h % 2 == 0:
                            ts("dve", ont.t[:, h * 64:(h + 1) * 64], ps(bOV, h * 65, h * 65 + 64), rc.t[:, h:h + 1], ALU.mult,
                               [PB[bOV], rc.b], [ont.b])
                        else:
                            act(ont.t[:, h * 64:(h + 1) * 64], ps(bOV, h * 65, h * 65 + 64), AF.Identity, [PB[bOV], rc.b],
                                [ont.b], scale=rc.t[:, h:h + 1])
                    for m in range(3):
                        tr(ps(bTP, m * 128, (m + 1) * 128), ont.t[:, m * 128:(m + 1) * 128], [ont.b], [PB[bTP]])
                    cp("act", naT.t[:], ps(bTP, 0, 384).rearrange("p (m t) -> p m t", m=3), [PB[bTP]], [naT.b])
                    dma("sp", rows3(mixT[640:1024, tq:tq + 128]), naT.t[:], naT.b, reads=[naT.b])

                G = 6 * len(blocks)
                for g in range(G + 1):
                    if g < G:
                        sa(g)
                    if g >= 1:
                        sc(g - 1)
                        if (g - 1) % 6 == 5:
                            tail((g - 1) // 6)
                P.barrier()

            if stop == 'NA':
                break
            wstack = ExitStack()
            wdn = sb(wstack, "wdn", [128, NF * D], BF16, True)
            load_wdown(l, wdn)
            with ExitStack() as ph:
                mxs = [sb(ph, "mx%d" % i, [128, 8, 512], BF16) for i in range(2)]
                sus = [sb(ph, "su%d" % i, [128, 6, 514], F32) for i in range(2)]
                pr = sb(ph, "pr", [128, 2, 514], F32)
                cvs = Rot([sb(ph, "cv%d" % i, [128, 512], F32) for i in range(2)])
                xks = Rot([sb(ph, "xkc%d" % i, [128, 512], F32) for i in range(3)])
                xos = Rot([sb(ph, "xoc%d" % i, [128, 512], F32) for i in range(2)])
                bkr = Rot(list(range(8)))

                def c_prep(ti):
                    t0, n, s = tiles512[ti]
                    mx, su = mxs[ti % 2], sus[ti % 2]
                    s_lo, s_hi = (0, LC) if s == 1 else (LC, T)
                    lo, hi = max(t0 - 1, s_lo), min(t0 + n + 1, s_hi)
                    dst0 = lo - (t0 - 1)
                    if lo > t0 - 1:
                        mset("pool", su.t[:, :, 0:1], 0.0, [su.b])
                    if hi < t0 + n + 1:
                        mset("pool", su.t[:, :, n + 1:n + 2], 0.0, [su.b])
                    dma("sp", su.t[:, :, dst0:dst0 + hi - lo], rows3(scu[:, lo:hi]), su.b, writes=[su.b])
                    dma("sp", mx.t[:, 0:3, :n], rows3(mixT[0:384, t0:t0 + n]), mx.b, writes=[mx.b])
                    dma("sp", mx.t[:, 5:8, :n], rows3(mixT[640:1024, t0:t0 + n]), mx.b, writes=[mx.b])
                    tt("pool", pr.t[:, :, :n + 2], su.t[:, 2:4, :n + 2], su.t[:, 4:6, :n + 2], ALU.mult, [su.b], [pr.b])
                    for j in range(2):
                        cv = cvs.next()
                        wc = l * 8 + j * 4
                        act(cv.t[:, :n], pr.t[:, j, 1:n + 1], AF.Identity, [pr.b, scwt.b], [cv.b],
                            bias=scwt.t[:, wc + 3:wc + 4], scale=scwt.t[:, wc + 1:wc + 2])
                        stt(cv.t[:, :n], pr.t[:, j, 0:n], scwt.t[:, wc:wc + 1], cv.t[:, :n], ALU.mult, ALU.add,
                            [pr.b, scwt.b, cv.b], [cv.b])
                        stt(cv.t[:, :n], pr.t[:, j, 2:n + 2], scwt.t[:, wc + 2:wc + 3], cv.t[:, :n], ALU.mult, ALU.add,
                            [pr.b, scwt.b, cv.b], [cv.b])
                        tt("pool", mx.t[:, 3 + j, :n], su.t[:, j, 1:n + 1], cv.t[:, :n], ALU.mult, [su.b, cv.b], [mx.b])

                def c_main(ti):
                    t0, n, s = tiles512[ti]
                    mx = mxs[ti % 2]
                    for m in range(8):
                        bk = bkr.next()
                        xk = xks.next()
                        xo = xos.next()
                        dma("sp", xk.t[:, :n], X0[m * 128:(m + 1) * 128, t0:t0 + n], xk.b, writes=[xk.b])
                        for k in range(KD):
                            mm(ps(bk, 0, n), arena.t[:, WOUT0 + k * D + m * 128: WOUT0 + k * D + (m + 1) * 128],
                               mx.t[:, k, :n], k == 0, k == KD - 1, [arena.b, mx.b], [PB[bk]])
                        stt(xo.t[:, :n], ps(bk, 0, n), modcol(l, 16 + m, s), xk.t[:, :n], ALU.mult, ALU.add,
                            [PB[bk], mod.b, xk.b], [xo.b])
                        dma("sp", XB[m * 128:(m + 1) * 128, t0:t0 + n], xo.t[:, :n], xo.b, reads=[xo.b])

                c_prep(0)
                for ti in range(len(tiles512)):
                    if ti + 1 < len(tiles512):
                        c_prep(ti + 1)
                    c_main(ti)
                P.barrier()

            if stop == 'C':
                wstack.close()
                break
            load_wup(l)
            with ExitStack() as ph:
                WD = 484
                xks = Rot([sb(ph, "xkd%d" % i, [128, WD], F32) for i in range(3)])
                fcwt = sb(ph, "fcwt", [128, 176], F32)
                dma("sp", fcwt.t[:], fcw[:, l * 176:(l + 1) * 176], fcwt.b, writes=[fcwt.b])
                sqs = Rot([sb(ph, "sqd%d" % i, [128, WD], BF16) for i in range(2)])
                sd = sb(ph, "sdd", [128, WD], F32)
                rstd = sb(ph, "rstdd", [128, WD], F32)
                tmps = Rot([sb(ph, "tmd%d" % i, [128, WD], F32) for i in range(2)])
                h2s = [sb(ph, "h2_%d" % i, [128, 8, WD], BF16) for i in range(2)]
                acts = sb(ph, "acts", [128, NF, WD], BF16)
                cas = Rot([sb(ph, "ca%d" % i, [128, WD], F32) for i in range(2)])
                cbs = Rot([sb(ph, "cb%d" % i, [128, WD], F32) for i in range(2)])
                sas = Rot([sb(ph, "sa%d" % i, [128, WD], F32) for i in range(1)])
                xos = Rot([sb(ph, "xo%d" % i, [128, WD], F32) for i in range(1)])
                bkr = Rot([1, 2, 3, 4, 5, 6, 7])
                ntl = -(-L // 482)
                no_l = -(-L // ntl)
                dt = [(0, LC, 1)]
                o = 0
                while o < L:
                    dt.append((LC + o, min(no_l, L - o), 0))
                    o += no_l

                def geom(ti):
                    o0, no, s = dt[ti]
                    s_lo, s_hi = (0, LC) if s == 1 else (LC, T)
                    lo, hi = max(o0 - 1, s_lo), min(o0 + no + 1, s_hi)
                    return o0, no, s, lo, hi - lo, o0 - lo

                def d_norm(ti):
                    o0, no, s, lo, nin, off = geom(ti)
                    norm_mod(xloader(XB, lo, nin, xks), nin, A2, l, 24, s, h2s[ti % 2], sqs, sd, rstd, tmps, 0)

                def d_up(ti):
                    o0, no, s, lo, nin, off = geom(ti)
                    h2 = h2s[ti % 2]
                    i0 = 1 if off == 0 else 0
                    j0 = 1 if off + no == nin else 0
                    for c in range(NF):
                        bks, cvs_, wcs = [], [], []
                        for half, crot in ((0, cas), (1, cbs)):
                            bk = bkr.next()
                            fc = half * NF + c
                            col0 = fc * 128
                            for k in range(KD):
                                mm(ps(bk, 0, nin), arena.t[:, k * 2 * FFN + col0: k * 2 * FFN + col0 + 128], h2.t[:, k, :nin],
                                   k == 0, k == KD - 1, [arena.b, h2.b], [PB[bk]])
                            bks.append(bk)
                            cvs_.append(crot.next())
                            wcs.append(fc * 4)
                        for bk, cv, wc in zip(bks, cvs_, wcs):
                            act(cv.t[:, :no], ps(bk, off, off + no), AF.Identity, [PB[bk], fcwt.b], [cv.b],
                                bias=fcwt.t[:, wc + 3:wc + 4], scale=fcwt.t[:, wc + 1:wc + 2])
                        for bk, cv, wc in zip(bks, cvs_, wcs):
                            stt(cv.t[:, i0:no], ps(bk, off + i0 - 1, off + no - 1), fcwt.t[:, wc:wc + 1], cv.t[:, i0:no],
                                ALU.mult, ALU.add, [PB[bk], fcwt.b, cv.b], [cv.b])
                        for bk, cv, wc in zip(bks, cvs_, wcs):
                            stt(cv.t[:, 0:no - j0], ps(bk, off + 1, off + 1 + no - j0), fcwt.t[:, wc + 2:wc + 3],
                                cv.t[:, 0:no - j0], ALU.mult, ALU.add, [PB[bk], fcwt.b, cv.b], [cv.b])
                        sa = sas.next()
                        act(sa.t[:, :no], cvs_[0].t[:, :no], AF.Silu, [cvs_[0].b], [sa.b])
                        tt("pool", acts.t[:, c, :no], sa.t[:, :no], cvs_[1].t[:, :no], ALU.mult, [sa.b, cvs_[1].b], [acts.b])

                def d_down(ti):
                    o0, no, s, lo, nin, off = geom(ti)
                    for m in range(8):
                        bk = bkr.next()
                        for c in range(NF):
                            mm(ps(bk, 0, no), wdn.t[:, c * D + m * 128: c * D + (m + 1) * 128], acts.t[:, c, :no], c == 0,
                               c == NF - 1, [wdn.b, acts.b], [PB[bk]])
                        xk = xks.next()
                        xo = xos.next()
                        dma("sp", xk.t[:, :no], XB[m * 128:(m + 1) * 128, o0:o0 + no], xk.b, writes=[xk.b])
                        stt(xo.t[:, :no], ps(bk, 0, no), modcol(l, 40 + m, s), xk.t[:, :no], ALU.mult, ALU.add,
                            [PB[bk], mod.b, xk.b], [xo.b])
                        dma("sp", XA[m * 128:(m + 1) * 128, o0:o0 + no], xo.t[:, :no], xo.b, reads=[xo.b])

                d_norm(0)
                for ti in range(len(dt)):
                    d_up(ti)
                    if ti + 1 < len(dt):
                        d_norm(ti + 1)
                    d_down(ti)
                if l + 1 < DEPTH:
                    load_win(l + 1)
                    load_wout(l + 1)
                P.barrier()
            wstack.close()

        finals = []
        with ExitStack() as ph:
            if stop:
                tiles512 = tiles512[:1]
            xts = Rot([sb(ph, "xf%d" % i, [128, 8, 512], F32) for i in range(2)])
            sqs = Rot([sb(ph, "sqf%d" % i, [128, 512], BF16) for i in range(2)])
            sd = sb(ph, "sdf", [128, 512], F32)
            rstd = sb(ph, "rstdf", [128, 512], F32)
            xos = Rot([sb(ph, "xof%d" % i, [128, 8, 512], F32) for i in range(2)])
            for (t0, n, s) in tiles512[1:]:
                xt = xts.next()
                xo = xos.next()
                dma("sp", xt.t[:, :, :n], rows3(XA[:, t0:t0 + n]), xt.b, writes=[xt.b])
                for k in range(KD):
                    sq = sqs.next()
                    act(sq.t[:, :n], xt.t[:, k, :n], AF.Square, [xt.b], [sq.b])
                    mm(ps(0, 0, n), ones_bf.t[:], sq.t[:, :n], k == 0, k == KD - 1, [ones_bf.b, sq.b], [PB[0]])
                act(sd.t[:, :n], ps(0, 0, n), AF.Sqrt, [PB[0], cst.b], [sd.b], bias=cst.t[:, 0:1], scale=1.0 / D)
                P.emit("dve", lambda e, sd=sd, rstd=rstd, n=n: e.reciprocal(out=rstd.t[:, :n], in_=sd.t[:, :n]),
                       reads=[sd.b], writes=[rstd.b])
                for k in range(KD):
                    stt(xo.t[:, k, :n], xt.t[:, k, :n], gfn.t[:, k:k + 1], rstd.t[:, :n], ALU.mult, ALU.mult,
                        [xt.b, gfn.b, rstd.b], [xo.b])
                finals.append(dma("sp", rows3(out[:, t0 - LC:t0 - LC + n]), xo.t[:, :, :n], xo.b, reads=[xo.b]))

        with nc.Block() as block:
            P.replay(nc, top, block, final_waits=finals)
    return nc


def _na_bias_tables(rpb, L):
    rows = L // GRID_W
    NB = L // 128
    H = rpb.shape[0]
    reps = [2, 0, 1, NB - 2, NB - 1]
    kk = np.arange(640)
    wrow, kcol = kk // 64, kk % 64
    qq = np.arange(128)
    qr_l, qcol = qq // 64, qq % 64
    out = np.empty((5, 128, H, 5, 128), np.float32)
    for pi, a in enumerate(reps):
        w0 = min(max(a - 2, 0), NB - 5)
        krow = 2 * w0 + wrow
        qrow = 2 * a + qr_l
        rs = np.clip(qrow - 4, 0, rows - 8)
        vrow = (krow[:, None] >= rs[None, :]) & (krow[:, None] < rs[None, :] + 8)
        cstart = np.clip(qcol - 8, 0, GRID_W - 16)
        vcol = (kcol[:, None] >= cstart[None, :]) & (kcol[:, None] < cstart[None, :] + 16)
        dr = np.clip(krow[:, None] - qrow[None, :] + 7, 0, 14)
        dc = np.clip(kcol[:, None] - qcol[None, :] + 15, 0, 30)
        valid = vrow & vcol
        for h in range(H):
            tab = np.where(valid, rpb[h][dr, dc], np.float32(-1e30)).astype(np.float32)
            out[pi, :, h, :, :] = tab.reshape(5, 128, 128).transpose(1, 0, 2)
    return out.reshape(5, 128, H * 5 * 128)


def _rope_table(L):
    T = LC + L
    t = np.arange(L)
    inv = (10000.0 ** (-np.arange(8, dtype=np.float32) / 8)).astype(np.float32)
    row = (t // GRID_W).astype(np.float32)[:, None] * inv
    col = (t % GRID_W).astype(np.float32)[:, None] * inv
    ang = np.concatenate([row, col], axis=-1).astype(np.float32)
    cos = np.ones((T, 16), np.float32)
    sin = np.zeros((T, 16), np.float32)
    cos[LC:] = np.cos(ang)
    sin[LC:] = np.sin(ang)
    tab = np.empty((T, 576), np.float32)
    tab[:, 0:384] = np.tile(cos, (1, 24))
    tab[:, 384:576] = np.tile(sin, (1, 12))
    return tab


def _pvec(v, nchunk):
    return np.ascontiguousarray(v.reshape(nchunk, 128).T)


def prepare_inputs(inp, L, DEPTH, ncores):
    f = lambda a: np.ascontiguousarray(np.asarray(a, dtype=np.float32))
    x, c, ctx, c_ctx = f(inp["x"]), f(inp["c"]), f(inp["ctx"]), f(inp["c_ctx"])
    perm32 = np.concatenate([np.arange(0, 32, 2), np.arange(1, 32, 2)])
    permq = np.concatenate([h * 32 + perm32 for h in range(6)])
    w_in = f(inp["w_in"])[:DEPTH]
    cols = np.concatenate([permq, 192 + permq, np.arange(384, 768), np.arange(800, 1184), np.arange(2720, 3104),
                           np.arange(1184, 1952), np.arange(1952, 2336), np.arange(2336, 2720), np.arange(768, 800)])
    assert cols.shape[0] == INW
    win = np.ascontiguousarray(w_in[:, :, cols]).reshape(DEPTH * D, INW)
    wg = np.zeros((33, DEPTH, 384), np.float32)
    for l in range(DEPTH):
        wg[0:16, l, 0:192] = f(inp["gla_wg2_fw"])[l][:, permq]
        wg[16:32, l, 192:384] = f(inp["gla_wg2_bw"])[l][:, permq]
        wg[32, l, 0:192] = f(inp["gla_bg_fw"])[l][permq]
        wg[32, l, 192:384] = f(inp["gla_bg_bw"])[l][permq]
    rep2 = lambda a: np.repeat(a[..., None], 2, axis=-1)
    bada = np.stack([rep2(_pvec(f(inp["b_ada"])[l], 48)) for l in range(DEPTH)], 1).reshape(128, DEPTH * 96)
    gmix = np.stack([rep2(_pvec(f(inp["norm_mix_g"])[l], 8)) for l in range(DEPTH)], 1).reshape(128, DEPTH * 16)
    gffn = np.stack([rep2(_pvec(f(inp["norm_ffn_g"])[l], 8)) for l in range(DEPTH)], 1).reshape(128, DEPTH * 16)
    gfin = _pvec(f(inp["final_norm_g"]), 8)
    gng = np.stack([np.broadcast_to(np.tile(f(inp["gla_norm_g"])[l], 6)[None, :], (128, 384)) for l in range(DEPTH)],
                   1).reshape(128, DEPTH * 384)
    scw = np.empty((128, DEPTH, 2, 4), np.float32)
    fcw = np.empty((128, DEPTH, 44, 4), np.float32)
    for l in range(DEPTH):
        for t_ in range(3):
            scw[:, l, :, t_] = _pvec(f(inp["sc_conv_w"])[l, t_], 2)
            fcw[:, l, :, t_] = _pvec(f(inp["ffn_conv_w"])[l, t_], 44)
        scw[:, l, :, 3] = _pvec(f(inp["sc_conv_b"])[l], 2)
        fcw[:, l, :, 3] = _pvec(f(inp["ffn_conv_b"])[l], 44)
    nab = np.concatenate([_na_bias_tables(f(inp["na_rpb"])[l], L) for l in range(DEPTH)], 0).reshape(DEPTH * 5 * 128, 3840)
    j = np.arange(128)
    M1 = (j[:, None] <= j[None, :]).astype(np.float32)
    M2 = (j[:, None] > j[None, :]).astype(np.float32)
    M3 = (j[:, None] >= j[None, :]).astype(np.float32)
    M4 = (j[:, None] < j[None, :]).astype(np.float32)
    ctri = np.concatenate([M1, M2, M3, M4], 1)
    cmask = np.concatenate([np.tile(M1, (1, 6)), np.tile(M3, (1, 6))], 1)
    bm = np.zeros((128, 384), np.float32)
    for g in range(2):
        for hp in range(3):
            bm[32 * hp:32 * hp + 32, g * 192 + hp * 64: g * 192 + (hp + 1) * 64] = 1.0
    cmisc = np.concatenate([np.ones((128, 128), np.float32), np.eye(128, dtype=np.float32), bm], 1)
    shared = {
        "wada": f(inp["w_ada"])[:DEPTH].reshape(DEPTH * D, 6 * D), "bada": bada, "gmix": gmix, "gffn": gffn, "gfin": gfin,
        "win": win, "wg": wg.reshape(33, DEPTH * 384), "gng": np.ascontiguousarray(gng),
        "scw": scw.reshape(128, DEPTH * 8), "nab": np.ascontiguousarray(nab),
        "wout": f(inp["w_out"])[:DEPTH].reshape(DEPTH * D, D), "wup": f(inp["ffn_w_up"])[:DEPTH].reshape(DEPTH * D, 2 * FFN),
        "fcw": fcw.reshape(128, DEPTH * 176), "wdown": f(inp["ffn_w_down"])[:DEPTH].reshape(DEPTH * FFN, D),
        "rope": _rope_table(L), "ctri": ctri, "cmask": cmask, "cmisc": cmisc,
    }
    maps = []
    for b in range(ncores):
        m = dict(shared)
        m["xin"] = np.ascontiguousarray(np.concatenate([ctx[b].T, x[b].T], axis=1))
        cc = np.stack([_pvec(c[b], 8), _pvec(c_ctx, 8)], -1).reshape(128, 16)
        m["cT"] = np.ascontiguousarray(cc)
        maps.append(m)
    return maps


_NC_CACHE = {}


def run(inputs, L, DEPTH, ncores):
    key = (L, DEPTH)
    if key not in _NC_CACHE:
        _NC_CACHE[key] = build_program(L, DEPTH)
    nc = _NC_CACHE[key]
    maps = prepare_inputs(inputs, L, DEPTH, ncores)
    res = run_bass_kernel_spmd(nc, maps, core_ids=list(range(ncores)))
    return np.stack([np.ascontiguousarray(r["out"].T) for r in res.results], 0).astype(np.float32)


def kernel(**inputs):
    return run(inputs, 8192, 4, 8)
```

```python
import numpy as np
from contextlib import ExitStack
import concourse.bass as bass
import concourse.mybir as mybir
from concourse.bass_utils import run_bass_kernel_spmd

F32 = mybir.dt.float32
BF16 = mybir.dt.bfloat16
AF = mybir.ActivationFunctionType
ALU = mybir.AluOpType
AX = mybir.AxisListType

D = 1024
KD = 8
LC = 256
GRID_W = 64
EPS = 1e-6
FFN = 2816
NF = 22
INW = 3104
ENGS = ("pe", "act", "dve", "pool", "sp")


class Buf:
    __slots__ = ("name", "lw", "rd", "dsem", "persist")

    def __init__(self, name, persist=False):
        self.name = name
        self.lw = None
        self.rd = []
        self.dsem = None
        self.persist = persist


class Ins:
    __slots__ = ("eng", "fn", "deps", "signal", "count", "dma", "dsem", "dval")

    def __init__(self, eng, fn, dma):
        self.eng = eng
        self.fn = fn
        self.deps = []
        self.signal = False
        self.count = 0
        self.dma = dma
        self.dsem = None
        self.dval = 0


class Prog:
    def __init__(self):
        self.q = {e: [] for e in ENGS}
        self.dma_cnt = []
        self.dma_last = []
        self.dma_persist = []
        self.last = {e: None for e in ENGS}
        self.free_slots = {"sp": [], "pool": [], "act": []}
        self.pe_fence = False
        self.phase_slots = []

    def _dsem_for(self, buf, eng):
        if buf.dsem is None:
            if (not buf.persist) and self.free_slots[eng]:
                buf.dsem = self.free_slots[eng].pop()
            else:
                buf.dsem = len(self.dma_cnt)
                self.dma_cnt.append(0)
                self.dma_last.append(None)
                self.dma_persist.append(buf.persist)
            if not buf.persist:
                self.phase_slots.append((eng, buf.dsem))
        return buf.dsem

    def emit(self, eng, fn, reads=(), writes=(), dma_key=None):
        ins = Ins(eng, fn, dma_key is not None)
        deps = []
        raw = set()
        for b in reads:
            if b.lw is not None:
                deps.append(b.lw)
                raw.add(id(b.lw))
        for b in writes:
            if b.lw is not None:
                deps.append(b.lw)
            deps.extend(b.rd)
        if dma_key is None:
            deps = [d for d in deps if d.dma or d.eng != eng or eng != "pe"]
            if eng == "pe" and self.pe_fence and self.last["pe"] is not None:
                deps.append(self.last["pe"])
                self.pe_fence = False

        if dma_key is not None:
            s = self._dsem_for(dma_key, eng)
            ins.dsem = s
            self.dma_cnt[s] += 16
            ins.dval = self.dma_cnt[s]
            if self.dma_last[s] is not None:
                deps.append(self.dma_last[s])
            self.dma_last[s] = ins
        seen = set()
        for d in deps:
            if d is ins or id(d) in seen:
                continue
            seen.add(id(d))
            if not d.dma:
                d.signal = True
            ins.deps.append(d)
        for b in reads:
            b.rd.append(ins)
        for b in writes:
            b.lw = ins
            b.rd = []
        self.q[eng].append(ins)
        if dma_key is None:
            self.last[eng] = ins
        return ins

    def barrier(self):
        deps = [self.last[e] for e in ENGS if self.last[e] is not None]
        deps += [d for d, p in zip(self.dma_last, self.dma_persist) if d is not None and not p]
        for e in ("act", "dve", "pool", "sp"):
            ins = Ins(e, None, False)
            for d in deps:
                if not d.dma:
                    d.signal = True
                ins.deps.append(d)
            self.q[e].append(ins)
        for e_, sl in self.phase_slots:
            self.free_slots[e_].append(sl)
        self.phase_slots = []

    def replay(self, nc, stack, block, final_waits=()):
        for e in ENGS:
            c = 0
            for ins in self.q[e]:
                if (not ins.dma) and ins.signal:
                    c += 1
                    ins.count = c
        esem = {e: stack.enter_context(nc.semaphore("es_" + e)) for e in ENGS}
        dsem = [stack.enter_context(nc.semaphore("ds_%d" % i)) for i in range(len(self.dma_cnt))]
        prog = self

        def run(engname, engobj):
            waited_e = {e: 0 for e in ENGS}
            waited_d = [0] * len(prog.dma_cnt)
            for ins in prog.q[engname]:
                for d in ins.deps:
                    if d.dma:
                        if waited_d[d.dsem] < d.dval:
                            engobj.wait_ge(dsem[d.dsem], d.dval)
                            waited_d[d.dsem] = d.dval
                    else:
                        if waited_e[d.eng] < d.count:
                            engobj.wait_ge(esem[d.eng], d.count)
                            waited_e[d.eng] = d.count
                if ins.fn is None:
                    continue
                r = ins.fn(engobj)
                if ins.dma:
                    r.then_inc(dsem[ins.dsem], 16)
                elif ins.signal:
                    r.then_inc(esem[ins.eng], 1)
            if engname == "sp":
                for d in final_waits:
                    engobj.wait_ge(dsem[d.dsem], d.dval)

        block.tensor(lambda t: run("pe", t))
        block.scalar(lambda t: run("act", t))
        block.vector(lambda t: run("dve", t))
        block.gpsimd(lambda t: run("pool", t))
        block.sync(lambda t: run("sp", t))


class Tl:
    __slots__ = ("t", "b")

    def __init__(self, t, name, persist=False):
        self.t = t
        self.b = Buf(name, persist)


class Rot:
    def __init__(self, items):
        self.items = items
        self.i = 0

    def next(self):
        r = self.items[self.i % len(self.items)]
        self.i += 1
        return r


def build_program(L, DEPTH, stop=None):
    import os
    stop = stop or os.environ.get('KSTOP')
    T = LC + L
    NCH = T // 128
    NB = L // 128
    nc = bass.Bass("TRN2", target_bir_lowering=False)
    P = Prog()

    def din(name, shape, dt=F32):
        return nc.dram_tensor(name, list(shape), dt, kind="ExternalInput").ap()

    def dscr(name, shape, dt):
        return nc.dram_tensor(name, list(shape), dt).ap()

    xin = din("xin", [D, T])
    cT = din("cT", [128, 16])
    wada = din("wada", [DEPTH * D, 6 * D])
    bada = din("bada", [128, DEPTH * 96])
    gmix = din("gmix", [128, DEPTH * 16])
    gffn = din("gffn", [128, DEPTH * 16])
    gfin = din("gfin", [128, 8])
    win = din("win", [DEPTH * D, INW])
    wg = din("wg", [33, DEPTH * 384])
    gng = din("gng", [128, DEPTH * 384])
    scw = din("scw", [128, DEPTH * 8])
    nab = din("nab", [DEPTH * 5 * 128, 6 * 5 * 128])
    wout = din("wout", [DEPTH * D, D])
    wup = din("wup", [DEPTH * D, 2 * FFN])
    fcw = din("fcw", [128, DEPTH * 176])
    wdown = din("wdown", [DEPTH * FFN, D])
    rope = din("rope", [T, 576])
    ctri = din("ctri", [128, 512])
    cmask = din("cmask", [128, 2 * 768])
    cmisc = din("cmisc", [128, 128 + 128 + 384])
    out = nc.dram_tensor("out", [D, L], F32, kind="ExternalOutput").ap()

    XA = dscr("XA", [D, T], F32)
    XB = dscr("XB", [D, T], F32)
    scu = dscr("scu", [768, T], F32)
    naq = dscr("naq", [384, T], BF16)
    nak = dscr("nak", [384, T], BF16)
    nav = dscr("nav", [T, 390], BF16)
    gbT = dscr("gbT", [NCH * 96, 512], BF16)
    gkv = dscr("gkv", [T, 576], BF16)
    gof = dscr("gof", [T, 384], F32)
    gr = dscr("gr", [T, 384], F32)
    mixT = dscr("mixT", [D, T], BF16)

    with ExitStack() as top:
        E = top.enter_context

        uid = [0]

        def sb(stack, name, shape, dt, persist=False):
            uid[0] += 1
            nm = "%s_%d" % (name, uid[0])
            return Tl(stack.enter_context(nc.sbuf_tensor(nm, list(shape), dt)), nm, persist)

        psum = E(nc.psum_tensor("psum", [128, 4096], F32))
        psum_bf = psum.bitcast(BF16)
        PB = [Buf("bank%d" % i, True) for i in range(8)]

        def ps(bank, c0, c1, p0=0, p1=128):
            return psum[p0:p1, bank * 512 + c0: bank * 512 + c1]

        def psb(bank, c0, c1, p0=0, p1=128):
            return psum_bf[p0:p1, bank * 1024 + c0: bank * 1024 + c1]

        arena = sb(top, "arena", [128, 8 * 2 * FFN], BF16, True)
        WOUT0 = 8 * INW
        cst = sb(top, "cst", [128, 8], F32, True)
        ones_bf = sb(top, "ones_bf", [128, 128], BF16, True)
        ident = sb(top, "ident", [128, 128], BF16, True)
        bmask = sb(top, "bmask", [128, 384], BF16, True)
        ones_f = sb(top, "ones_f", [128, 2], F32, True)
        mod = sb(top, "mod", [128, DEPTH * 96], F32, True)
        A1 = sb(top, "A1", [128, DEPTH * 16], F32, True)
        A2 = sb(top, "A2", [128, DEPTH * 16], F32, True)
        gmx = sb(top, "gmx", [128, DEPTH * 16], F32, True)
        gff = sb(top, "gff", [128, DEPTH * 16], F32, True)
        gfn = sb(top, "gfn", [128, 8], F32, True)
        scwt = sb(top, "scwt", [128, DEPTH * 8], F32, True)
        wgt = sb(top, "wgt", [33, DEPTH * 384], BF16, True)
        decb = sb(top, "decb", [96, NCH * 2], F32, True)

        def modcol(l, j, s):
            c = l * 96 + j * 2 + s
            return mod.t[:, c:c + 1]

        def acol(A, l, k, s):
            c = l * 16 + k * 2 + s
            return A.t[:, c:c + 1]

        def dma(eng, out_ap, in_ap, key, reads=(), writes=()):
            return P.emit(eng, lambda e: e.dma_start(out=out_ap, in_=in_ap), reads=reads, writes=writes, dma_key=key)

        def mm(out_ap, lhsT, rhs, start, stop, reads, writes, fence=False):
            if fence:
                P.pe_fence = True
            r = P.emit("pe", lambda e: e.matmul(out_ap, lhsT=lhsT, rhs=rhs, start=start, stop=stop),
                       reads=reads, writes=writes)
            if fence:
                P.pe_fence = True
            return r

        def tr(out_ap, in_ap, reads, writes):
            return P.emit("pe", lambda e: e.matmul(out_ap, lhsT=in_ap, rhs=ident.t[:], start=True, stop=True),
                          reads=list(reads) + [ident.b], writes=writes)

        def act(out_ap, in_ap, func, reads, writes, bias=None, scale=None):
            kw = {}
            if bias is not None:
                kw["bias"] = bias
            if scale is not None:
                kw["scale"] = scale
            return P.emit("act", lambda e: e.activation(out=out_ap, in_=in_ap, func=func, **kw), reads=reads,
                          writes=writes)

        def tt(eng, out_ap, a, b, op, reads, writes):
            return P.emit(eng, lambda e: e.tensor_tensor(out=out_ap, in0=a, in1=b, op=op), reads=reads, writes=writes)

        def ts(eng, out_ap, a, s1, op0, reads, writes, s2=None, op1=None):
            if op1 is None:
                return P.emit(eng, lambda e: e.tensor_scalar(out=out_ap, in0=a, scalar1=s1, scalar2=None, op0=op0),
                              reads=reads, writes=writes)
            return P.emit(eng, lambda e: e.tensor_scalar(out=out_ap, in0=a, scalar1=s1, scalar2=s2, op0=op0, op1=op1),
                          reads=reads, writes=writes)

        def stt(out_ap, a, s, b, op0, op1, reads, writes):
            return P.emit("dve", lambda e: e.scalar_tensor_tensor(out=out_ap, in0=a, scalar=s, in1=b, op0=op0, op1=op1),
                          reads=reads, writes=writes)

        def cp(eng, out_ap, in_ap, reads, writes):
            if eng == "act":
                return P.emit("act", lambda e: e.copy(out=out_ap, in_=in_ap), reads=reads, writes=writes)
            return P.emit(eng, lambda e: e.tensor_copy(out=out_ap, in_=in_ap), reads=reads, writes=writes)

        def mset(eng, ap, val, writes):
            return P.emit(eng, lambda e: e.memset(ap, val), writes=writes)

        def rows3(ap2d, p=128):
            return ap2d.rearrange("(m p) t -> p m t", p=p)

        mset("dve", cst.t[:, 0:1], EPS, [cst.b])
        mset("dve", cst.t[:, 1:2], float(np.log(32.0 ** -0.5)), [cst.b])
        mset("dve", cst.t[:, 2:3], 1.0, [cst.b])
        mset("dve", ones_f.t[:], 1.0, [ones_f.b])
        dma("pool", ones_bf.t[:], cmisc[:, 0:128], ones_bf.b, writes=[ones_bf.b])
        dma("pool", ident.t[:], cmisc[:, 128:256], ident.b, writes=[ident.b])
        dma("pool", bmask.t[:], cmisc[:, 256:640], bmask.b, writes=[bmask.b])
        dma("pool", wgt.t[:], wg, wgt.b, writes=[wgt.b])
        dma("sp", gmx.t[:], gmix, gmx.b, writes=[gmx.b])
        dma("sp", gff.t[:], gffn, gff.b, writes=[gff.b])
        dma("sp", gfn.t[:], gfin, gfn.b, writes=[gfn.b])
        dma("sp", scwt.t[:], scw, scwt.b, writes=[scwt.b])

        def load_win(l):
            for k in range(KD):
                dma("pool", arena.t[:, k * INW:(k + 1) * INW], win[l * D + k * 128: l * D + (k + 1) * 128, :],
                    arena.b, writes=[arena.b])

        def load_wout(l):
            for k in range(KD):
                dma("pool", arena.t[:, WOUT0 + k * D: WOUT0 + (k + 1) * D],
                    wout[l * D + k * 128: l * D + (k + 1) * 128, :], arena.b, writes=[arena.b])

        def load_wup(l):
            for k in range(KD):
                dma("pool", arena.t[:, k * 2 * FFN:(k + 1) * 2 * FFN],
                    wup[l * D + k * 128: l * D + (k + 1) * 128, :], arena.b, writes=[arena.b])

        def load_wdown(l, wdn):
            for c in range(NF):
                dma("pool", wdn.t[:, c * D:(c + 1) * D], wdown[l * FFN + c * 128: l * FFN + (c + 1) * 128, :], wdn.b,
                    writes=[wdn.b])

        with ExitStack() as ph:
            cin = sb(ph, "cin", [128, 16], F32)
            csl = sb(ph, "csl", [128, 16], BF16)
            bad = sb(ph, "bad", [128, DEPTH * 96], F32)
            wst = [sb(ph, "wst%d" % i, [128, 8 * 768], BF16) for i in range(2)]
            dma("sp", cin.t[:], cT, cin.b, writes=[cin.b])
            dma("sp", bad.t[:], bada, bad.b, writes=[bad.b])
            act(csl.t[:], cin.t[:], AF.Silu, [cin.b], [csl.b])
            pi = 0
            for l in range(DEPTH):
                bank = l % 2
                for piece in range(8):
                    w = wst[pi % 2]
                    pi += 1
                    for k in range(KD):
                        dma("pool", w.t[:, k * 768:(k + 1) * 768],
                            wada[l * D + k * 128: l * D + (k + 1) * 128, piece * 768:(piece + 1) * 768], w.b,
                            writes=[w.b])
                    for jj in range(6):
                        j = piece * 6 + jj
                        for k in range(KD):
                            mm(ps(bank, j * 2, j * 2 + 2), w.t[:, k * 768 + jj * 128: k * 768 + (jj + 1) * 128],
                               csl.t[:, k * 2:(k + 1) * 2], k == 0, k == KD - 1, [w.b, csl.b], [PB[bank]])
                tt("dve", mod.t[:, l * 96:(l + 1) * 96], ps(bank, 0, 96), bad.t[:, l * 96:(l + 1) * 96], ALU.add,
                   [PB[bank], bad.b], [mod.b])
                stt(A1.t[:, l * 16:(l + 1) * 16], mod.t[:, l * 96 + 16: l * 96 + 32], 1.0, gmx.t[:, l * 16:(l + 1) * 16],
                    ALU.add, ALU.mult, [mod.b, gmx.b], [A1.b])
                stt(A2.t[:, l * 16:(l + 1) * 16], mod.t[:, l * 96 + 64: l * 96 + 80], 1.0, gff.t[:, l * 16:(l + 1) * 16],
                    ALU.add, ALU.mult, [mod.b, gff.b], [A2.b])
            P.barrier()

        def norm_mod(getx, n, A, l, shj, s, hT, sqs, sd, rstd, tmps, ssbank):
            for k in range(KD):
                xb, xap = getx(k)
                sq = sqs.next()
                act(sq.t[:, :n], xap, AF.Square, [xb.b], [sq.b])
                mm(ps(ssbank, 0, n), ones_bf.t[:], sq.t[:, :n], k == 0, k == KD - 1, [ones_bf.b, sq.b], [PB[ssbank]])
            act(sd.t[:, :n], ps(ssbank, 0, n), AF.Sqrt, [PB[ssbank], cst.b], [sd.b], bias=cst.t[:, 0:1], scale=1.0 / D)
            P.emit("dve", lambda e: e.reciprocal(out=rstd.t[:, :n], in_=sd.t[:, :n]), reads=[sd.b], writes=[rstd.b])
            for k in range(KD):
                xb, xap = getx(k)
                tm = tmps.next()
                stt(tm.t[:, :n], xap, acol(A, l, k, s), rstd.t[:, :n], ALU.mult, ALU.mult, [xb.b, A.b, rstd.b], [tm.b])
                act(hT.t[:, k, :n], tm.t[:, :n], AF.Identity, [tm.b, mod.b], [hT.b], bias=modcol(l, shj + k, s), scale=1.0)

        def xloader(X, lo, n, xks):
            def getx(k):
                xk = xks.next()
                dma("sp", xk.t[:, :n], X[k * 128:(k + 1) * 128, lo:lo + n], xk.b, writes=[xk.b])
                return xk, xk.t[:, :n]
            return getx

        tiles512 = [(0, LC, 1)] + [(LC + i * 512, 512, 0) for i in range(L // 512)]

        load_win(0)
        load_wout(0)
        for l in range(DEPTH):
            X0 = xin if l == 0 else XA

            if stop == 'PRO':
                break
            with ExitStack() as ph:
                xks = Rot([sb(ph, "xk%d" % i, [128, 512], F32) for i in range(3)])
                tri = sb(ph, "tri", [128, 512], F32)
                msk = sb(ph, "msk", [128, 384], BF16)
                dma("sp", tri.t[:], ctri, tri.b, writes=[tri.b])
                dma("pool", msk.t[:], cmask[:, 0:384], msk.b, writes=[msk.b])
                sqs = Rot([sb(ph, "sq%d" % i, [128, 512], BF16) for i in range(3)])
                sd = sb(ph, "sd", [128, 512], F32)
                rstd = sb(ph, "rstd", [128, 512], F32)
                tmps = Rot([sb(ph, "tm%d" % i, [128, 512], F32) for i in range(2)])
                hTs = [sb(ph, "hT%d" % i, [128, 8, 512], BF16) for i in range(2)]
                scos = Rot([sb(ph, "sco%d" % i, [128, 512], F32) for i in range(3)])
                nqks = Rot([sb(ph, "nqk%d" % i, [128, 512], BF16) for i in range(3)])
                glrs = [sb(ph, "glr%d" % i, [33, 512], BF16) for i in range(2)]

                def pool_(name, shape, dt, depth):
                    return [sb(ph, "%s%d" % (name, i), shape, dt) for i in range(depth)]
                rps = pool_("rp", [128, 576], F32, 2)
                qks = pool_("qk", [128, 384], F32, 2)
                kvs = pool_("kv", [128, 576], BF16, 6)
                rrs = pool_("rr", [128, 384], F32, 2)
                nvs = pool_("nv", [128, 390], BF16, 2)
                ees = pool_("ee", [128, 384], F32, 2)
                lls = pool_("ll", [128, 384], F32, 2)
                fac = sb(ph, "fac", [128, 6, 192], F32)
                tcs = sb(ph, "tcs", [128, 384], F32)
                m12 = sb(ph, "m12", [128, 2, 192], F32)
                rqs = pool_("rq", [128, 384], F32, 2)
                gps = pool_("gp", [128, 4, 192], BF16, 2)
                gTfs = pool_("gTf", [96, 512], BF16, 3)
                gTbs = pool_("gTb", [96, 512], BF16, 2)
                atms = pool_("atm", [128, 6, 128], BF16, 2)
                kefs = pool_("kef", [128, 192], BF16, 4)
                decs = pool_("dec", [96, 4], F32, 4)
                ofs = pool_("of", [128, 384], F32, 2)
                um = sb(ph, "um", [96, 384], F32)
                Sf = sb(ph, "Sf", [96, 384], F32)
                Sfb = sb(ph, "Sfb", [96, 384], BF16)
                for g_ in glrs:
                    mset("pool", g_.t[32:33, :], 1.0, [g_.b])
                for n_ in nvs:
                    mset("pool", n_.t[:], 1.0, [n_.b])
                mset("dve", Sf.t[:], 0.0, [Sf.b])
                mset("dve", Sfb.t[:], 0.0, [Sfb.b])
                fmb = Rot([1, 2])
                tmb = Rot([3, 4])
                evr = Rot(["act", "dve"])

                def tile_norm(ti):
                    t0, n, s = tiles512[ti]
                    norm_mod(xloader(X0, t0, n, xks), n, A1, l, 0, s, hTs[ti % 2], sqs, sd, rstd, tmps, 0)

                def tile_fm(ti):
                    t0, n, s = tiles512[ti]
                    hT = hTs[ti % 2]
                    glr = glrs[ti % 2]
                    for m in range(12):
                        bk = fmb.next()
                        c0 = 1536 + m * 128
                        for k in range(KD):
                            mm(ps(bk, 0, n), arena.t[:, k * INW + c0: k * INW + c0 + 128], hT.t[:, k, :n], k == 0,
                               k == KD - 1, [arena.b, hT.b], [PB[bk]])
                        if m < 6:
                            sco = scos.next()
                            cp(evr.next(), sco.t[:, :n], ps(bk, 0, n), [PB[bk]], [sco.b])
                            dma("sp", scu[m * 128:(m + 1) * 128, t0:t0 + n], sco.t[:, :n], sco.b, reads=[sco.b])
                        else:
                            nqk = nqks.next()
                            cp(evr.next(), nqk.t[:, :n], ps(bk, 0, n), [PB[bk]], [nqk.b])
                            dst = naq if m < 9 else nak
                            mm_ = (m - 6) % 3
                            dma("sp", dst[mm_ * 128:(mm_ + 1) * 128, t0:t0 + n], nqk.t[:, :n], nqk.b, reads=[nqk.b])
                    bk = fmb.next()
                    for k in range(KD):
                        mm(ps(bk, 0, n, 0, 32), arena.t[:, k * INW + 3072: k * INW + 3104], hT.t[:, k, :n], k == 0,
                           k == KD - 1, [arena.b, hT.b], [PB[bk]])
                    cp(evr.next(), glr.t[0:32, :n], ps(bk, 0, n, 0, 32), [PB[bk]], [glr.b])

                chunks = []
                for ti, (t0, n, s) in enumerate(tiles512):
                    for c in range(n // 128):
                        chunks.append((ti, c))

                def st0(i):
                    ti, c = chunks[i]
                    t0 = tiles512[ti][0]
                    tk0 = t0 + c * 128
                    cs = slice(c * 128, (c + 1) * 128)
                    hT = hTs[ti % 2]
                    qk, kv, rr, nv, rp = qks[i % 2], kvs[i % 6], rrs[i % 2], nvs[i % 2], rps[i % 2]
                    dma("sp", rp.t[:], rope[tk0:tk0 + 128, :], rp.b, writes=[rp.b])
                    for g in range(4):
                        bk = tmb.next()
                        for k in range(KD):
                            mm(ps(bk, 0, 384), hT.t[:, k, cs], arena.t[:, k * INW + g * 384: k * INW + (g + 1) * 384],
                               k == 0, k == KD - 1, [arena.b, hT.b], [PB[bk]])
                        if g == 0:
                            cp("act", qk.t[:], ps(bk, 0, 384), [PB[bk]], [qk.b])
                        elif g == 1:
                            cp("dve", kv.t[:, 0:384], ps(bk, 0, 384), [PB[bk]], [kv.b])
                        elif g == 2:
                            cp("act", rr.t[:], ps(bk, 0, 384), [PB[bk]], [rr.b])
                        else:
                            cp("dve", nv.t[:].rearrange("p (h e) -> p h e", h=6)[:, :, 0:64],
                               ps(bk, 0, 384).rearrange("p (h e) -> p h e", h=6), [PB[bk]], [nv.b])
                    dma("sp", gr[tk0:tk0 + 128, :], rr.t[:], rr.b, reads=[rr.b])
                    dma("sp", nav[tk0:tk0 + 128, :], nv.t[:], nv.b, reads=[nv.b])

                def st1(i):
                    ti, c = chunks[i]
                    cs = slice(c * 128, (c + 1) * 128)
                    glr = glrs[ti % 2]
                    qk, rp, rq, ee, ll = qks[i % 2], rps[i % 2], rqs[i % 2], ees[i % 2], lls[i % 2]
                    mm(ps(5, 0, 384), glr.t[0:33, cs], wgt.t[:, l * 384:(l + 1) * 384], True, True, [glr.b, wgt.b],
                       [PB[5]])
                    act(ee.t[:], ps(5, 0, 384), AF.Exp, [PB[5]], [ee.b], scale=-1.0)
                    act(ll.t[:], ee.t[:], AF.Ln, [ee.b, cst.b], [ll.b], bias=cst.t[:, 2:3], scale=1.0)
                    q4 = qk.t[:].rearrange("p (h t e) -> p h t e", h=12, t=2)
                    t4 = tcs.t[:].rearrange("p (h t e) -> p h t e", h=12, t=2)
                    r4 = rq.t[:].rearrange("p (h t e) -> p h t e", h=12, t=2)
                    sn = rp.t[:, 384:576].rearrange("p (h e) -> p h e", h=12)
                    tt("pool", tcs.t[:], qk.t[:], rp.t[:, 0:384], ALU.mult, [qk.b, rp.b], [tcs.b])
                    tt("pool", m12.t[:, 0, :].rearrange("p (h e) -> p h e", h=12), q4[:, :, 1, :], sn, ALU.mult,
                       [qk.b, rp.b], [m12.b])
                    tt("pool", m12.t[:, 1, :].rearrange("p (h e) -> p h e", h=12), q4[:, :, 0, :], sn, ALU.mult,
                       [qk.b, rp.b], [m12.b])
                    tt("pool", r4[:, :, 0, :], t4[:, :, 0, :], m12.t[:, 0, :].rearrange("p (h e) -> p h e", h=12),
                       ALU.subtract, [tcs.b, m12.b], [rq.b])
                    tt("pool", r4[:, :, 1, :], t4[:, :, 1, :], m12.t[:, 1, :].rearrange("p (h e) -> p h e", h=12),
                       ALU.add, [tcs.b, m12.b], [rq.b])

                def st2(i):
                    ti, c = chunks[i]
                    ch = (tiles512[ti][0] // 128) + c
                    tk0 = ch * 128
                    ll, rq, gp, kv, kef, dec = lls[i % 2], rqs[i % 2], gps[i % 2], kvs[i % 6], kefs[i % 4], decs[i % 4]
                    P.pe_fence = True
                    for d_ in range(2):
                        bk = 6 + d_
                        for jj in range(2):
                            mm(ps(bk, jj * 192, (jj + 1) * 192), tri.t[:, (d_ * 2 + jj) * 128:(d_ * 2 + jj + 1) * 128],
                               ll.t[:, d_ * 192:(d_ + 1) * 192], True, True, [tri.b, ll.b], [PB[bk]])
                    for d_ in range(2):
                        for g in range(2):
                            i_ = d_ * 2 + g
                            mm(ps(5, 400 + i_, 401 + i_, 0, 96), ll.t[:, d_ * 192 + g * 96: d_ * 192 + (g + 1) * 96],
                               ones_f.t[:, 0:1], True, True, [ll.b, ones_f.b], [PB[5]])
                    P.pe_fence = True
                    act(dec.t[:], ps(5, 400, 404, 0, 96), AF.Exp, [PB[5]], [dec.b], scale=-1.0 / 16)
                    cp("dve", decb.t[:, ch * 2:(ch + 1) * 2], dec.t[:, 2:4], [dec.b], [decb.b])
                    for d_ in range(2):
                        bk = 6 + d_
                        act(fac.t[:, d_ * 3 + 0, :], ps(bk, 0, 192), AF.Exp, [PB[bk], cst.b], [fac.b],
                            bias=cst.t[:, 1:2], scale=-1.0 / 16)
                        act(fac.t[:, d_ * 3 + 1, :], ps(bk, 0, 192), AF.Exp, [PB[bk]], [fac.b], scale=1.0 / 16)
                        act(fac.t[:, d_ * 3 + 2, :], ps(bk, 192, 384), AF.Exp, [PB[bk]], [fac.b], scale=-1.0 / 16)
                    tt("dve", gp.t[:, 0, :], rq.t[:, 0:192], fac.t[:, 0, :], ALU.mult, [rq.b, fac.b], [gp.b])
                    tt("dve", gp.t[:, 1, :], rq.t[:, 192:384], fac.t[:, 1, :], ALU.mult, [rq.b, fac.b], [gp.b])
                    tt("dve", gp.t[:, 2, :], rq.t[:, 0:192], fac.t[:, 3, :], ALU.mult, [rq.b, fac.b], [gp.b])
                    tt("dve", gp.t[:, 3, :], rq.t[:, 192:384], fac.t[:, 4, :], ALU.mult, [rq.b, fac.b], [gp.b])
                    tt("dve", kv.t[:, 384:576], rq.t[:, 192:384], fac.t[:, 5, :], ALU.mult, [rq.b, fac.b], [kv.b])
                    tt("pool", kef.t[:], rq.t[:, 192:384], fac.t[:, 2, :], ALU.mult, [rq.b, fac.b], [kef.b])
                    dma("sp", gkv[tk0:tk0 + 128, :], kv.t[:], kv.b, reads=[kv.b])

                def st3(i):
                    ti, c = chunks[i]
                    ch = (tiles512[ti][0] // 128) + c
                    gp, gTf, gTb = gps[i % 2], gTfs[i % 3], gTbs[i % 2]
                    for j in range(4):
                        for g in range(2):
                            tb_ = 1 if j < 2 else 2
                            tc_ = ((j % 2) * 2 + g) * 128
                            tr(ps(tb_, tc_, tc_ + 128, 0, 96), gp.t[:, j, g * 96:(g + 1) * 96], [gp.b], [PB[tb_]])
                    cp("act", gTf.t[:], ps(1, 0, 512, 0, 96), [PB[1]], [gTf.b])
                    cp("dve", gTb.t[:], ps(2, 0, 512, 0, 96), [PB[2]], [gTb.b])
                    dma("sp", gbT[ch * 96:(ch + 1) * 96, :], gTb.t[:], gTb.b, reads=[gTb.b])

                def st4(i):
                    gTf, atm = gTfs[i % 3], atms[i % 2]
                    for h in range(6):
                        g, hp = h // 3, h % 3
                        bk = 6 + g
                        mm(ps(bk, hp * 128, (hp + 1) * 128), gTf.t[32 * hp:32 * hp + 32, (2 + g) * 128:(3 + g) * 128],
                           gTf.t[32 * hp:32 * hp + 32, g * 128:(g + 1) * 128], True, True, [gTf.b], [PB[bk]], fence=True)
                    for g in range(2):
                        tt("dve", atm.t[:, 3 * g:3 * g + 3, :], ps(6 + g, 0, 384).rearrange("p (h i) -> p h i", h=3),
                           msk.t[:, 0:384].rearrange("p (h i) -> p h i", h=3), ALU.mult, [PB[6 + g], msk.b], [atm.b])

                def st5(i):
                    ti, c = chunks[i]
                    ch = (tiles512[ti][0] // 128) + c
                    tk0 = ch * 128
                    gTf, atm, kv, kef, dec, of = gTfs[i % 3], atms[i % 2], kvs[i % 6], kefs[i % 4], decs[i % 4], ofs[i % 2]
                    bo = tmb.next()
                    for h in range(6):
                        g, hp = h // 3, h % 3
                        mm(ps(bo, h * 64, (h + 1) * 64), atm.t[:, h, :], kv.t[:, h * 64:(h + 1) * 64], True, False,
                           [atm.b, kv.b], [PB[bo]])
                        mm(ps(bo, h * 64, (h + 1) * 64), gTf.t[32 * hp:32 * hp + 32, g * 128:(g + 1) * 128],
                           Sfb.t[32 * hp:32 * hp + 32, g * 192 + hp * 64: g * 192 + (hp + 1) * 64], False, True,
                           [gTf.b, Sfb.b], [PB[bo]], fence=True)
                    cp("act", of.t[:], ps(bo, 0, 384), [PB[bo]], [of.b])
                    dma("sp", gof[tk0:tk0 + 128, :], of.t[:], of.b, reads=[of.b])
                    for g in range(2):
                        mm(ps(0, g * 192, (g + 1) * 192, 0, 96), kef.t[:, g * 96:(g + 1) * 96],
                           kv.t[:, g * 192:(g + 1) * 192], True, True, [kef.b, kv.b], [PB[0]])
                    tt("dve", um.t[:], ps(0, 0, 384, 0, 96), bmask.t[0:96, :], ALU.mult, [PB[0], bmask.b], [um.b])
                    for g in range(2):
                        stt(Sf.t[:, g * 192:(g + 1) * 192], Sf.t[:, g * 192:(g + 1) * 192], dec.t[:, g:g + 1],
                            um.t[:, g * 192:(g + 1) * 192], ALU.mult, ALU.add, [Sf.b, dec.b, um.b], [Sf.b])
                    cp("dve", Sfb.t[:], Sf.t[:], [Sf.b], [Sfb.b])

                stages = [st0, st1, st2, st3, st4, st5]
                if stop and stop.startswith('S1s'):
                    stages = stages[:int(stop[3:])]
                nchunks = len(chunks)
                tile_norm(0)
                for step in range(nchunks + len(stages) - 1):
                    if step < nchunks:
                        ti, c = chunks[step]
                        if c == 0:
                            tile_fm(ti)
                        if c == min(1, tiles512[ti][1] // 128 - 1) and ti + 1 < len(tiles512):
                            tile_norm(ti + 1)
                    for si, st in enumerate(stages):
                        i = step - si
                        if 0 <= i < nchunks:
                            st(i)
                P.barrier()

            if stop and stop.startswith('S1'):
                break
            with ExitStack() as ph:
                def pool_(name, shape, dt, depth):
                    return [sb(ph, "%s%d" % (name, i), shape, dt) for i in range(depth)]
                gTs = pool_("gT", [96, 512], BF16, 3)
                kvs = pool_("kvb", [128, 576], BF16, 3)
                ofs = pool_("ofb", [128, 384], F32, 3)
                rrs = pool_("rrb", [128, 384], F32, 4)
                atms = pool_("atmb", [128, 6, 128], BF16, 2)
                oos = pool_("oo", [128, 384], F32, 2)
                o2 = sb(ph, "o2", [128, 384], F32)
                ssq = sb(ph, "ssq", [128, 6], F32)
                rs = sb(ph, "rs", [128, 6], F32)
                on = sb(ph, "on", [128, 384], F32)
                sr = sb(ph, "sr", [128, 384], F32)
                gl = sb(ph, "gl", [128, 384], BF16)
                glTs = pool_("glT", [128, 3, 128], BF16, 2)
                um = sb(ph, "umb", [96, 384], F32)
                Sb = sb(ph, "Sb", [96, 384], F32)
                Sbb = sb(ph, "Sbb", [96, 384], BF16)
                msk = sb(ph, "mskb", [128, 384], BF16)
                gngt = sb(ph, "gngt", [128, 384], F32)
                dma("pool", msk.t[:], cmask[:, 768:1152], msk.b, writes=[msk.b])
                dma("sp", gngt.t[:], gng[:, l * 384:(l + 1) * 384], gngt.b, writes=[gngt.b])
                mset("dve", Sb.t[:], 0.0, [Sb.b])
                mset("dve", Sbb.t[:], 0.0, [Sbb.b])
                order = [1, 0] + list(range(NCH - 1, 1, -1))

                def g0(i):
                    ch = order[i]
                    tk0 = ch * 128
                    gT, kv, of, rr = gTs[i % 3], kvs[i % 3], ofs[i % 3], rrs[i % 4]
                    dma("sp", gT.t[:], gbT[ch * 96:(ch + 1) * 96, :], gT.b, writes=[gT.b])
                    dma("sp", kv.t[:], gkv[tk0:tk0 + 128, :], kv.b, writes=[kv.b])
                    dma("sp", of.t[:], gof[tk0:tk0 + 128, :], of.b, writes=[of.b])
                    dma("sp", rr.t[:], gr[tk0:tk0 + 128, :], rr.b, writes=[rr.b])

                def g1(i):
                    gT, atm = gTs[i % 3], atms[i % 2]
                    bA0 = 2 * (i % 2)
                    for h in range(6):
                        g, hp = h // 3, h % 3
                        bk = bA0 + g
                        mm(ps(bk, hp * 128, (hp + 1) * 128), gT.t[32 * hp:32 * hp + 32, (2 + g) * 128:(3 + g) * 128],
                           gT.t[32 * hp:32 * hp + 32, g * 128:(g + 1) * 128], True, True, [gT.b], [PB[bk]], fence=True)
                    for g in range(2):
                        tt("dve", atm.t[:, 3 * g:3 * g + 3, :], ps(bA0 + g, 0, 384).rearrange("p (h i) -> p h i", h=3),
                           msk.t[:, 0:384].rearrange("p (h i) -> p h i", h=3), ALU.mult, [PB[bA0 + g], msk.b], [atm.b])

                def g2(i):
                    ch = order[i]
                    gT, kv, of, atm, oo = gTs[i % 3], kvs[i % 3], ofs[i % 3], atms[i % 2], oos[i % 2]
                    bO, bU = 4 + (i % 2), 6
                    for h in range(6):
                        g, hp = h // 3, h % 3
                        mm(ps(bO, h * 64, (h + 1) * 64), atm.t[:, h, :], kv.t[:, h * 64:(h + 1) * 64], True, False,
                           [atm.b, kv.b], [PB[bO]])
                        mm(ps(bO, h * 64, (h + 1) * 64), gT.t[32 * hp:32 * hp + 32, g * 128:(g + 1) * 128],
                           Sbb.t[32 * hp:32 * hp + 32, g * 192 + hp * 64: g * 192 + (hp + 1) * 64], False, True,
                           [gT.b, Sbb.b], [PB[bO]], fence=True)
                    for g in range(2):
                        mm(ps(bU, g * 192, (g + 1) * 192, 0, 96), kv.t[:, 384 + g * 96: 384 + (g + 1) * 96],
                           kv.t[:, g * 192:(g + 1) * 192], True, True, [kv.b], [PB[bU]])
                    tt("dve", um.t[:], ps(bU, 0, 384, 0, 96), bmask.t[0:96, :], ALU.mult, [PB[bU], bmask.b], [um.b])
                    for g in range(2):
                        stt(Sb.t[:, g * 192:(g + 1) * 192], Sb.t[:, g * 192:(g + 1) * 192],
                            decb.t[:, ch * 2 + g: ch * 2 + g + 1], um.t[:, g * 192:(g + 1) * 192], ALU.mult, ALU.add,
                            [Sb.b, decb.b, um.b], [Sb.b])
                    cp("dve", Sbb.t[:], Sb.t[:], [Sb.b], [Sbb.b])
                    tt("dve", oo.t[:], ps(bO, 0, 384), of.t[:], ALU.add, [PB[bO], of.b], [oo.b])

                def g3(i):
                    ch = order[i]
                    tk0 = ch * 128
                    oo, rr, glT = oos[i % 2], rrs[i % 4], glTs[i % 2]
                    tt("pool", o2.t[:], oo.t[:], oo.t[:], ALU.mult, [oo.b], [o2.b])
                    P.emit("dve", lambda e: e.tensor_reduce(
                        out=ssq.t[:], in_=o2.t[:].rearrange("p (h e) -> p h e", h=6), axis=AX.X, op=ALU.add),
                        reads=[o2.b], writes=[ssq.b])
                    act(rs.t[:], ssq.t[:], AF.Sqrt, [ssq.b, cst.b], [rs.b], bias=cst.t[:, 0:1], scale=1.0 / 64)
                    P.emit("dve", lambda e: e.reciprocal(out=rs.t[:], in_=rs.t[:]), reads=[rs.b], writes=[rs.b])
                    act(sr.t[:], rr.t[:], AF.Silu, [rr.b], [sr.b])
                    for h in range(6):
                        stt(on.t[:, h * 64:(h + 1) * 64], oo.t[:, h * 64:(h + 1) * 64], rs.t[:, h:h + 1],
                            gngt.t[:, h * 64:(h + 1) * 64], ALU.mult, ALU.mult, [oo.b, rs.b, gngt.b], [on.b])
                    tt("pool", gl.t[:], on.t[:], sr.t[:], ALU.mult, [on.b, sr.b], [gl.b])
                    for m in range(3):
                        tr(ps(7, m * 128, (m + 1) * 128), gl.t[:, m * 128:(m + 1) * 128], [gl.b], [PB[7]])
                    cp("act", glT.t[:], ps(7, 0, 384).rearrange("p (m t) -> p m t", m=3), [PB[7]], [glT.b])
                    dma("sp", rows3(mixT[0:384, tk0:tk0 + 128]), glT.t[:], glT.b, reads=[glT.b])

                stages = [g0, g1, g2, g3]
                for step in range(NCH + len(stages) - 1):
                    for si, st in enumerate(stages):
                        i = step - si
                        if 0 <= i < NCH:
                            st(i)
                P.barrier()

            if stop == 'GB':
                break
            with ExitStack() as ph:
                nb = sb(ph, "nb", [128, 6 * 5 * 128], F32)
                krs = [sb(ph, "kr%d" % i, [128, 3, 128], BF16) for i in range(8)]
                vrs = [sb(ph, "vr%d" % i, [128, 390], BF16) for i in range(8)]
                kc = sb(ph, "kc", [128, 3, 256], BF16)
                vc = sb(ph, "vc", [128, 2, 390], BF16)
                qts = Rot([sb(ph, "qt%d" % i, [128, 3, 128], BF16) for i in range(2)])
                stmps = Rot([sb(ph, "stmp%d" % i, [128, 640], F32) for i in range(2)])
                pts = Rot([sb(ph, "pt%d" % i, [128, 7, 128], BF16) for i in range(3)])
                rcs = [sb(ph, "rc%d" % i, [128, 6], F32) for i in range(2)]
                onts = [sb(ph, "ont%d" % i, [128, 384], BF16) for i in range(2)]
                naTs = Rot([sb(ph, "naT%d" % i, [128, 3, 128], BF16) for i in range(2)])
                dma("sp", kc.t[:], rows3(nak[:, 0:LC]), kc.b, writes=[kc.b])
                dma("sp", vc.t[:], nav[0:LC, :].rearrange("(c p) f -> p c f", p=128), vc.b, writes=[vc.b])
                loaded = {}
                cur_pat = [None]

                def ensure_chunk(cid):
                    slot = cid % 8
                    if loaded.get(slot) != cid:
                        tk = LC + cid * 128
                        dma("sp", krs[slot].t[:], rows3(nak[:, tk:tk + 128]), krs[slot].b, writes=[krs[slot].b])
                        dma("sp", vrs[slot].t[:], nav[tk:tk + 128, :], vrs[slot].b, writes=[vrs[slot].b])
                        loaded[slot] = cid
                    return slot

                def pat_of(a):
                    if a == 0:
                        return 1
                    if a == 1:
                        return 2
                    if a == NB - 2:
                        return 3
                    if a == NB - 1:
                        return 4
                    return 0

                blocks = [("c", 0), ("c", 1)] + [("l", a) for a in range(NB)]
                binfo = {}

                def blk_setup(bi):
                    kind, a = blocks[bi]
                    qt = qts.items[bi % 2]
                    if kind == "c":
                        tq = a * 128
                        slots = []
                    else:
                        tq = LC + a * 128
                        w0 = min(max(a - 2, 0), NB - 5)
                        slots = [ensure_chunk(w0 + c) for c in range(5)]
                        pat = pat_of(a)
                        if cur_pat[0] != pat:
                            r0 = (l * 5 + pat) * 128
                            dma("sp", nb.t[:], nab[r0:r0 + 128, :], nb.b, writes=[nb.b])
                            cur_pat[0] = pat
                    dma("sp", qt.t[:], rows3(naq[:, tq:tq + 128]), qt.b, writes=[qt.b])
                    binfo[bi] = (tq, slots, qt)

                def sa(g):
                    bi, h = divmod(g, 6)
                    if h == 0:
                        blk_setup(bi)
                    tq, slots, qt = binfo[bi]
                    m, pb = h // 2, (h % 2) * 64
                    b0 = 2 * (g % 2)
                    pt = pts.items[g % 3]
                    P.pe_fence = True
                    for c, sl in enumerate(slots):
                        bk, off = (b0, c * 128) if c < 4 else (b0 + 1, 0)
                        mm(ps(bk, off, off + 128), krs[sl].t[pb:pb + 64, m, :], qt.t[pb:pb + 64, m, :], True, True,
                           [krs[sl].b, qt.b], [PB[bk]])
                    for c in range(2):
                        mm(ps(b0 + 1, 128 + c * 128, 256 + c * 128), kc.t[pb:pb + 64, m, c * 128:(c + 1) * 128],
                           qt.t[pb:pb + 64, m, :], True, True, [kc.b, qt.b], [PB[b0 + 1]])
                    if slots:
                        stmp = stmps.items[g % 2]
                        nbh = nb.t[:, h * 640:(h + 1) * 640]
                        stt(stmp.t[:, 0:512], ps(b0, 0, 512), 0.125, nbh[:, 0:512], ALU.mult, ALU.add,
                            [PB[b0], nb.b], [stmp.b])
                        stt(stmp.t[:, 512:640], ps(b0 + 1, 0, 128), 0.125, nbh[:, 512:640], ALU.mult, ALU.add,
                            [PB[b0 + 1], nb.b], [stmp.b])
                        act(pt.t[:, 0:5, :], stmp.t[:].rearrange("p (c q) -> p c q", c=5), AF.Exp, [stmp.b], [pt.b])
                    act(pt.t[:, 5:7, :], ps(b0 + 1, 128, 384).rearrange("p (c q) -> p c q", c=2), AF.Exp, [PB[b0 + 1]],
                        [pt.b], scale=0.125)

                def sc(g):
                    bi, h = divmod(g, 6)
                    P.pe_fence = True
                    tq, slots, qt = binfo[bi]
                    bOV = 4 + (bi % 2)
                    pt = pts.items[g % 3]
                    nk = len(slots) + 2
                    ops_ = [(c, vrs[sl].t[:, h * 65:(h + 1) * 65], vrs[sl].b) for c, sl in enumerate(slots)]
                    ops_ += [(5 + c, vc.t[:, c, h * 65:(h + 1) * 65], vc.b) for c in range(2)]
                    for i_, (pc, vap, vb) in enumerate(ops_):
                        mm(ps(bOV, h * 65, (h + 1) * 65), pt.t[:, pc, :], vap, i_ == 0, i_ == nk - 1, [pt.b, vb], [PB[bOV]])

                def tail(bi):
                    tq, slots, qt = binfo[bi]
                    bOV = 4 + (bi % 2)
                    bTP = 6 + (bi % 2)
                    naT = naTs.items[bi % 2]
                    ont = onts[bi % 2]
                    rc = rcs[bi % 2]
                    ov3 = ps(bOV, 0, 390).rearrange("p (h e) -> p h e", h=6)
                    P.emit("dve", lambda e: e.reciprocal(out=rc.t[:], in_=ov3[:, :, 64]), reads=[PB[bOV]], writes=[rc.b])
                    for h in range(6):
                        if h % 2 == 0:
                            ts("dve", ont.t[:, h * 64:(h + 1) * 64], ps(bOV, h * 65, h * 65 + 64), rc.t[:, h:h + 1], ALU.mult,
                               [PB[bOV], rc.b], [ont.b])
                        else:
                            act(ont.t[:, h * 64:(h + 1) * 64], ps(bOV, h * 65, h * 65 + 64), AF.Identity, [PB[bOV], rc.b],
                                [ont.b], scale=rc.t[:, h:h + 1])
                    for m in range(3):
                        tr(ps(bTP, m * 128, (m + 1) * 128), ont.t[:, m * 128:(m + 1) * 128], [ont.b], [PB[bTP]])
                    cp("act", naT.t[:], ps(bTP, 0, 384).rearrange("p (m t) -> p m t", m=3), [PB[bTP]], [naT.b])
                    dma("sp", rows3(mixT[640:1024, tq:tq + 128]), naT.t[:], naT.b, reads=[naT.b])

                G = 6 * len(blocks)
                for g in range(G + 1):
                    if g < G:
                        sa(g)
                    if g >= 1:
                        sc(g - 1)
                        if (g - 1) % 6 == 5:
                            tail((g - 1) // 6)
                P.barrier()

            if stop == 'NA':
                break
            wstack = ExitStack()
            wdn = sb(wstack, "wdn", [128, NF * D], BF16, True)
            load_wdown(l, wdn)
            with ExitStack() as ph:
                mxs = [sb(ph, "mx%d" % i, [128, 8, 512], BF16) for i in range(2)]
                sus = [sb(ph, "su%d" % i, [128, 6, 514], F32) for i in range(2)]
                pr = sb(ph, "pr", [128, 2, 514], F32)
                cvs = Rot([sb(ph, "cv%d" % i, [128, 512], F32) for i in range(2)])
                xks = Rot([sb(ph, "xkc%d" % i, [128, 512], F32) for i in range(3)])
                xos = Rot([sb(ph, "xoc%d" % i, [128, 512], F32) for i in range(2)])
                bkr = Rot(list(range(8)))

                def c_prep(ti):
                    t0, n, s = tiles512[ti]
                    mx, su = mxs[ti % 2], sus[ti % 2]
                    s_lo, s_hi = (0, LC) if s == 1 else (LC, T)
                    lo, hi = max(t0 - 1, s_lo), min(t0 + n + 1, s_hi)
                    dst0 = lo - (t0 - 1)
                    if lo > t0 - 1:
                        mset("pool", su.t[:, :, 0:1], 0.0, [su.b])
                    if hi < t0 + n + 1:
                        mset("pool", su.t[:, :, n + 1:n + 2], 0.0, [su.b])
                    dma("sp", su.t[:, :, dst0:dst0 + hi - lo], rows3(scu[:, lo:hi]), su.b, writes=[su.b])
                    dma("sp", mx.t[:, 0:3, :n], rows3(mixT[0:384, t0:t0 + n]), mx.b, writes=[mx.b])
                    dma("sp", mx.t[:, 5:8, :n], rows3(mixT[640:1024, t0:t0 + n]), mx.b, writes=[mx.b])
                    tt("pool", pr.t[:, :, :n + 2], su.t[:, 2:4, :n + 2], su.t[:, 4:6, :n + 2], ALU.mult, [su.b], [pr.b])
                    for j in range(2):
                        cv = cvs.next()
                        wc = l * 8 + j * 4
                        act(cv.t[:, :n], pr.t[:, j, 1:n + 1], AF.Identity, [pr.b, scwt.b], [cv.b],
                            bias=scwt.t[:, wc + 3:wc + 4], scale=scwt.t[:, wc + 1:wc + 2])
                        stt(cv.t[:, :n], pr.t[:, j, 0:n], scwt.t[:, wc:wc + 1], cv.t[:, :n], ALU.mult, ALU.add,
                            [pr.b, scwt.b, cv.b], [cv.b])
                        stt(cv.t[:, :n], pr.t[:, j, 2:n + 2], scwt.t[:, wc + 2:wc + 3], cv.t[:, :n], ALU.mult, ALU.add,
                            [pr.b, scwt.b, cv.b], [cv.b])
                        tt("pool", mx.t[:, 3 + j, :n], su.t[:, j, 1:n + 1], cv.t[:, :n], ALU.mult, [su.b, cv.b], [mx.b])

                def c_main(ti):
                    t0, n, s = tiles512[ti]
                    mx = mxs[ti % 2]
                    for m in range(8):
                        bk = bkr.next()
                        xk = xks.next()
                        xo = xos.next()
                        dma("sp", xk.t[:, :n], X0[m * 128:(m + 1) * 128, t0:t0 + n], xk.b, writes=[xk.b])
                        for k in range(KD):
                            mm(ps(bk, 0, n), arena.t[:, WOUT0 + k * D + m * 128: WOUT0 + k * D + (m + 1) * 128],
                               mx.t[:, k, :n], k == 0, k == KD - 1, [arena.b, mx.b], [PB[bk]])
                        stt(xo.t[:, :n], ps(bk, 0, n), modcol(l, 16 + m, s), xk.t[:, :n], ALU.mult, ALU.add,
                            [PB[bk], mod.b, xk.b], [xo.b])
                        dma("sp", XB[m * 128:(m + 1) * 128, t0:t0 + n], xo.t[:, :n], xo.b, reads=[xo.b])

                c_prep(0)
                for ti in range(len(tiles512)):
                    if ti + 1 < len(tiles512):
                        c_prep(ti + 1)
                    c_main(ti)
                P.barrier()

            if stop == 'C':
                wstack.close()
                break
            load_wup(l)
            with ExitStack() as ph:
                WD = 484
                xks = Rot([sb(ph, "xkd%d" % i, [128, WD], F32) for i in range(3)])
                fcwt = sb(ph, "fcwt", [128, 176], F32)
                dma("sp", fcwt.t[:], fcw[:, l * 176:(l + 1) * 176], fcwt.b, writes=[fcwt.b])
                sqs = Rot([sb(ph, "sqd%d" % i, [128, WD], BF16) for i in range(2)])
                sd = sb(ph, "sdd", [128, WD], F32)
                rstd = sb(ph, "rstdd", [128, WD], F32)
                tmps = Rot([sb(ph, "tmd%d" % i, [128, WD], F32) for i in range(2)])
                h2s = [sb(ph, "h2_%d" % i, [128, 8, WD], BF16) for i in range(2)]
                acts = sb(ph, "acts", [128, NF, WD], BF16)
                cas = Rot([sb(ph, "ca%d" % i, [128, WD], F32) for i in range(2)])
                cbs = Rot([sb(ph, "cb%d" % i, [128, WD], F32) for i in range(2)])
                sas = Rot([sb(ph, "sa%d" % i, [128, WD], F32) for i in range(1)])
                xos = Rot([sb(ph, "xo%d" % i, [128, WD], F32) for i in range(1)])
                bkr = Rot([1, 2, 3, 4, 5, 6, 7])
                ntl = -(-L // 482)
                no_l = -(-L // ntl)
                dt = [(0, LC, 1)]
                o = 0
                while o < L:
                    dt.append((LC + o, min(no_l, L - o), 0))
                    o += no_l

                def geom(ti):
                    o0, no, s = dt[ti]
                    s_lo, s_hi = (0, LC) if s == 1 else (LC, T)
                    lo, hi = max(o0 - 1, s_lo), min(o0 + no + 1, s_hi)
                    return o0, no, s, lo, hi - lo, o0 - lo

                def d_norm(ti):
                    o0, no, s, lo, nin, off = geom(ti)
                    norm_mod(xloader(XB, lo, nin, xks), nin, A2, l, 24, s, h2s[ti % 2], sqs, sd, rstd, tmps, 0)

                def d_up(ti):
                    o0, no, s, lo, nin, off = geom(ti)
                    h2 = h2s[ti % 2]
                    i0 = 1 if off == 0 else 0
                    j0 = 1 if off + no == nin else 0
                    for c in range(NF):
                        bks, cvs_, wcs = [], [], []
                        for half, crot in ((0, cas), (1, cbs)):
                            bk = bkr.next()
                            fc = half * NF + c
                            col0 = fc * 128
                            for k in range(KD):
                                mm(ps(bk, 0, nin), arena.t[:, k * 2 * FFN + col0: k * 2 * FFN + col0 + 128], h2.t[:, k, :nin],
                                   k == 0, k == KD - 1, [arena.b, h2.b], [PB[bk]])
                            bks.append(bk)
                            cvs_.append(crot.next())
                            wcs.append(fc * 4)
                        for bk, cv, wc in zip(bks, cvs_, wcs):
                            act(cv.t[:, :no], ps(bk, off, off + no), AF.Identity, [PB[bk], fcwt.b], [cv.b],
                                bias=fcwt.t[:, wc + 3:wc + 4], scale=fcwt.t[:, wc + 1:wc + 2])
                        for bk, cv, wc in zip(bks, cvs_, wcs):
                            stt(cv.t[:, i0:no], ps(bk, off + i0 - 1, off + no - 1), fcwt.t[:, wc:wc + 1], cv.t[:, i0:no],
                                ALU.mult, ALU.add, [PB[bk], fcwt.b, cv.b], [cv.b])
                        for bk, cv, wc in zip(bks, cvs_, wcs):
                            stt(cv.t[:, 0:no - j0], ps(bk, off + 1, off + 1 + no - j0), fcwt.t[:, wc + 2:wc + 3],
                                cv.t[:, 0:no - j0], ALU.mult, ALU.add, [PB[bk], fcwt.b, cv.b], [cv.b])
                        sa = sas.next()
                        act(sa.t[:, :no], cvs_[0].t[:, :no], AF.Silu, [cvs_[0].b], [sa.b])
                        tt("pool", acts.t[:, c, :no], sa.t[:, :no], cvs_[1].t[:, :no], ALU.mult, [sa.b, cvs_[1].b], [acts.b])

                def d_down(ti):
                    o0, no, s, lo, nin, off = geom(ti)
                    for m in range(8):
                        bk = bkr.next()
                        for c in range(NF):
                            mm(ps(bk, 0, no), wdn.t[:, c * D + m * 128: c * D + (m + 1) * 128], acts.t[:, c, :no], c == 0,
                               c == NF - 1, [wdn.b, acts.b], [PB[bk]])
                        xk = xks.next()
                        xo = xos.next()
                        dma("sp", xk.t[:, :no], XB[m * 128:(m + 1) * 128, o0:o0 + no], xk.b, writes=[xk.b])
                        stt(xo.t[:, :no], ps(bk, 0, no), modcol(l, 40 + m, s), xk.t[:, :no], ALU.mult, ALU.add,
                            [PB[bk], mod.b, xk.b], [xo.b])
                        dma("sp", XA[m * 128:(m + 1) * 128, o0:o0 + no], xo.t[:, :no], xo.b, reads=[xo.b])

                d_norm(0)
                for ti in range(len(dt)):
                    d_up(ti)
                    if ti + 1 < len(dt):
                        d_norm(ti + 1)
                    d_down(ti)
                if l + 1 < DEPTH:
                    load_win(l + 1)
                    load_wout(l + 1)
                P.barrier()
            wstack.close()

        finals = []
        with ExitStack() as ph:
            if stop:
                tiles512 = tiles512[:1]
            xts = Rot([sb(ph, "xf%d" % i, [128, 8, 512], F32) for i in range(2)])
            sqs = Rot([sb(ph, "sqf%d" % i, [128, 512], BF16) for i in range(2)])
            sd = sb(ph, "sdf", [128, 512], F32)
            rstd = sb(ph, "rstdf", [128, 512], F32)
            xos = Rot([sb(ph, "xof%d" % i, [128, 8, 512], F32) for i in range(2)])
            for (t0, n, s) in tiles512[1:]:
                xt = xts.next()
                xo = xos.next()
                dma("sp", xt.t[:, :, :n], rows3(XA[:, t0:t0 + n]), xt.b, writes=[xt.b])
                for k in range(KD):
                    sq = sqs.next()
                    act(sq.t[:, :n], xt.t[:, k, :n], AF.Square, [xt.b], [sq.b])
                    mm(ps(0, 0, n), ones_bf.t[:], sq.t[:, :n], k == 0, k == KD - 1, [ones_bf.b, sq.b], [PB[0]])
                act(sd.t[:, :n], ps(0, 0, n), AF.Sqrt, [PB[0], cst.b], [sd.b], bias=cst.t[:, 0:1], scale=1.0 / D)
                P.emit("dve", lambda e, sd=sd, rstd=rstd, n=n: e.reciprocal(out=rstd.t[:, :n], in_=sd.t[:, :n]),
                       reads=[sd.b], writes=[rstd.b])
                for k in range(KD):
                    stt(xo.t[:, k, :n], xt.t[:, k, :n], gfn.t[:, k:k + 1], rstd.t[:, :n], ALU.mult, ALU.mult,
                        [xt.b, gfn.b, rstd.b], [xo.b])
                finals.append(dma("sp", rows3(out[:, t0 - LC:t0 - LC + n]), xo.t[:, :, :n], xo.b, reads=[xo.b]))

        with nc.Block() as block:
            P.replay(nc, top, block, final_waits=finals)
    return nc


def _na_bias_tables(rpb, L):
    rows = L // GRID_W
    NB = L // 128
    H = rpb.shape[0]
    reps = [2, 0, 1, NB - 2, NB - 1]
    kk = np.arange(640)
    wrow, kcol = kk // 64, kk % 64
    qq = np.arange(128)
    qr_l, qcol = qq // 64, qq % 64
    out = np.empty((5, 128, H, 5, 128), np.float32)
    for pi, a in enumerate(reps):
        w0 = min(max(a - 2, 0), NB - 5)
        krow = 2 * w0 + wrow
        qrow = 2 * a + qr_l
        rs = np.clip(qrow - 4, 0, rows - 8)
        vrow = (krow[:, None] >= rs[None, :]) & (krow[:, None] < rs[None, :] + 8)
        cstart = np.clip(qcol - 8, 0, GRID_W - 16)
        vcol = (kcol[:, None] >= cstart[None, :]) & (kcol[:, None] < cstart[None, :] + 16)
        dr = np.clip(krow[:, None] - qrow[None, :] + 7, 0, 14)
        dc = np.clip(kcol[:, None] - qcol[None, :] + 15, 0, 30)
        valid = vrow & vcol
        for h in range(H):
            tab = np.where(valid, rpb[h][dr, dc], np.float32(-1e30)).astype(np.float32)
            out[pi, :, h, :, :] = tab.reshape(5, 128, 128).transpose(1, 0, 2)
    return out.reshape(5, 128, H * 5 * 128)


def _rope_table(L):
    T = LC + L
    t = np.arange(L)
    inv = (10000.0 ** (-np.arange(8, dtype=np.float32) / 8)).astype(np.float32)
    row = (t // GRID_W).astype(np.float32)[:, None] * inv
    col = (t % GRID_W).astype(np.float32)[:, None] * inv
    ang = np.concatenate([row, col], axis=-1).astype(np.float32)
    cos = np.ones((T, 16), np.float32)
    sin = np.zeros((T, 16), np.float32)
    cos[LC:] = np.cos(ang)
    sin[LC:] = np.sin(ang)
    tab = np.empty((T, 576), np.float32)
    tab[:, 0:384] = np.tile(cos, (1, 24))
    tab[:, 384:576] = np.tile(sin, (1, 12))
    return tab


def _pvec(v, nchunk):
    return np.ascontiguousarray(v.reshape(nchunk, 128).T)


def prepare_inputs(inp, L, DEPTH, ncores):
    f = lambda a: np.ascontiguousarray(np.asarray(a, dtype=np.float32))
    x, c, ctx, c_ctx = f(inp["x"]), f(inp["c"]), f(inp["ctx"]), f(inp["c_ctx"])
    perm32 = np.concatenate([np.arange(0, 32, 2), np.arange(1, 32, 2)])
    permq = np.concatenate([h * 32 + perm32 for h in range(6)])
    w_in = f(inp["w_in"])[:DEPTH]
    cols = np.concatenate([permq, 192 + permq, np.arange(384, 768), np.arange(800, 1184), np.arange(2720, 3104),
                           np.arange(1184, 1952), np.arange(1952, 2336), np.arange(2336, 2720), np.arange(768, 800)])
    assert cols.shape[0] == INW
    win = np.ascontiguousarray(w_in[:, :, cols]).reshape(DEPTH * D, INW)
    wg = np.zeros((33, DEPTH, 384), np.float32)
    for l in range(DEPTH):
        wg[0:16, l, 0:192] = f(inp["gla_wg2_fw"])[l][:, permq]
        wg[16:32, l, 192:384] = f(inp["gla_wg2_bw"])[l][:, permq]
        wg[32, l, 0:192] = f(inp["gla_bg_fw"])[l][permq]
        wg[32, l, 192:384] = f(inp["gla_bg_bw"])[l][permq]
    rep2 = lambda a: np.repeat(a[..., None], 2, axis=-1)
    bada = np.stack([rep2(_pvec(f(inp["b_ada"])[l], 48)) for l in range(DEPTH)], 1).reshape(128, DEPTH * 96)
    gmix = np.stack([rep2(_pvec(f(inp["norm_mix_g"])[l], 8)) for l in range(DEPTH)], 1).reshape(128, DEPTH * 16)
    gffn = np.stack([rep2(_pvec(f(inp["norm_ffn_g"])[l], 8)) for l in range(DEPTH)], 1).reshape(128, DEPTH * 16)
    gfin = _pvec(f(inp["final_norm_g"]), 8)
    gng = np.stack([np.broadcast_to(np.tile(f(inp["gla_norm_g"])[l], 6)[None, :], (128, 384)) for l in range(DEPTH)],
                   1).reshape(128, DEPTH * 384)
    scw = np.empty((128, DEPTH, 2, 4), np.float32)
    fcw = np.empty((128, DEPTH, 44, 4), np.float32)
    for l in range(DEPTH):
        for t_ in range(3):
            scw[:, l, :, t_] = _pvec(f(inp["sc_conv_w"])[l, t_], 2)
            fcw[:, l, :, t_] = _pvec(f(inp["ffn_conv_w"])[l, t_], 44)
        scw[:, l, :, 3] = _pvec(f(inp["sc_conv_b"])[l], 2)
        fcw[:, l, :, 3] = _pvec(f(inp["ffn_conv_b"])[l], 44)
    nab = np.concatenate([_na_bias_tables(f(inp["na_rpb"])[l], L) for l in range(DEPTH)], 0).reshape(DEPTH * 5 * 128, 3840)
    j = np.arange(128)
    M1 = (j[:, None] <= j[None, :]).astype(np.float32)
    M2 = (j[:, None] > j[None, :]).astype(np.float32)
    M3 = (j[:, None] >= j[None, :]).astype(np.float32)
    M4 = (j[:, None] < j[None, :]).astype(np.float32)
    ctri = np.concatenate([M1, M2, M3, M4], 1)
    cmask = np.concatenate([np.tile(M1, (1, 6)), np.tile(M3, (1, 6))], 1)
    bm = np.zeros((128, 384), np.float32)
    for g in range(2):
        for hp in range(3):
            bm[32 * hp:32 * hp + 32, g * 192 + hp * 64: g * 192 + (hp + 1) * 64] = 1.0
    cmisc = np.concatenate([np.ones((128, 128), np.float32), np.eye(128, dtype=np.float32), bm], 1)
    shared = {
        "wada": f(inp["w_ada"])[:DEPTH].reshape(DEPTH * D, 6 * D), "bada": bada, "gmix": gmix, "gffn": gffn, "gfin": gfin,
        "win": win, "wg": wg.reshape(33, DEPTH * 384), "gng": np.ascontiguousarray(gng),
        "scw": scw.reshape(128, DEPTH * 8), "nab": np.ascontiguousarray(nab),
        "wout": f(inp["w_out"])[:DEPTH].reshape(DEPTH * D, D), "wup": f(inp["ffn_w_up"])[:DEPTH].reshape(DEPTH * D, 2 * FFN),
        "fcw": fcw.reshape(128, DEPTH * 176), "wdown": f(inp["ffn_w_down"])[:DEPTH].reshape(DEPTH * FFN, D),
        "rope": _rope_table(L), "ctri": ctri, "cmask": cmask, "cmisc": cmisc,
    }
    maps = []
    for b in range(ncores):
        m = dict(shared)
        m["xin"] = np.ascontiguousarray(np.concatenate([ctx[b].T, x[b].T], axis=1))
        cc = np.stack([_pvec(c[b], 8), _pvec(c_ctx, 8)], -1).reshape(128, 16)
        m["cT"] = np.ascontiguousarray(cc)
        maps.append(m)
    return maps


_NC_CACHE = {}


def run(inputs, L, DEPTH, ncores):
    key = (L, DEPTH)
    if key not in _NC_CACHE:
        _NC_CACHE[key] = build_program(L, DEPTH)
    nc = _NC_CACHE[key]
    maps = prepare_inputs(inputs, L, DEPTH, ncores)
    res = run_bass_kernel_spmd(nc, maps, core_ids=list(range(ncores)))
    return np.stack([np.ascontiguousarray(r["out"].T) for r in res.results], 0).astype(np.float32)


def kernel(**inputs):
    return run(inputs, 8192, 4, 8)
```

```python
import numpy as np
from contextlib import ExitStack
import concourse.bass as bass
import concourse.mybir as mybir
from concourse.bass_utils import run_bass_kernel_spmd

F32 = mybir.dt.float32
BF16 = mybir.dt.bfloat16
AF = mybir.ActivationFunctionType
ALU = mybir.AluOpType
AX = mybir.AxisListType

D = 1024
KD = 8
LC = 256
GRID_W = 64
EPS = 1e-6
FFN = 2816
NF = 22
INW = 3104
ENGS = ("pe", "act", "dve", "pool", "sp")


class Buf:
    __slots__ = ("name", "lw", "rd", "dsem", "persist")

    def __init__(self, name, persist=False):
        self.name = name
        self.lw = None
        self.rd = []
        self.dsem = None
        self.persist = persist


class Ins:
    __slots__ = ("eng", "fn", "deps", "signal", "count", "dma", "dsem", "dval")

    def __init__(self, eng, fn, dma):
        self.eng = eng
        self.fn = fn
        self.deps = []
        self.signal = False
        self.count = 0
        self.dma = dma
        self.dsem = None
        self.dval = 0


class Prog:
    def __init__(self):
        self.q = {e: [] for e in ENGS}
        self.dma_cnt = []
        self.dma_last = []
        self.dma_persist = []
        self.last = {e: None for e in ENGS}
        self.free_slots = {"sp": [], "pool": [], "act": []}
        self.pe_fence = False
        self.phase_slots = []

    def _dsem_for(self, buf, eng):
        if buf.dsem is None:
            if (not buf.persist) and self.free_slots[eng]:
                buf.dsem = self.free_slots[eng].pop()
            else:
                buf.dsem = len(self.dma_cnt)
                self.dma_cnt.append(0)
                self.dma_last.append(None)
                self.dma_persist.append(buf.persist)
            if not buf.persist:
                self.phase_slots.append((eng, buf.dsem))
        return buf.dsem

    def emit(self, eng, fn, reads=(), writes=(), dma_key=None):
        ins = Ins(eng, fn, dma_key is not None)
        deps = []
        raw = set()
        for b in reads:
            if b.lw is not None:
                deps.append(b.lw)
                raw.add(id(b.lw))
        for b in writes:
            if b.lw is not None:
                deps.append(b.lw)
            deps.extend(b.rd)
        if dma_key is None:
            deps = [d for d in deps if d.dma or d.eng != eng or eng != "pe"]
            if eng == "pe" and self.pe_fence and self.last["pe"] is not None:
                deps.append(self.last["pe"])
                self.pe_fence = False

        if dma_key is not None:
            s = self._dsem_for(dma_key, eng)
            ins.dsem = s
            self.dma_cnt[s] += 16
            ins.dval = self.dma_cnt[s]
            if self.dma_last[s] is not None:
                deps.append(self.dma_last[s])
            self.dma_last[s] = ins
        seen = set()
        for d in deps:
            if d is ins or id(d) in seen:
                continue
            seen.add(id(d))
            if not d.dma:
                d.signal = True
            ins.deps.append(d)
        for b in reads:
            b.rd.append(ins)
        for b in writes:
            b.lw = ins
            b.rd = []
        self.q[eng].append(ins)
        if dma_key is None:
            self.last[eng] = ins
        return ins

    def barrier(self):
        deps = [self.last[e] for e in ENGS if self.last[e] is not None]
        deps += [d for d, p in zip(self.dma_last, self.dma_persist) if d is not None and not p]
        for e in ("act", "dve", "pool", "sp"):
            ins = Ins(e, None, False)
            for d in deps:
                if not d.dma:
                    d.signal = True
                ins.deps.append(d)
            self.q[e].append(ins)
        for e_, sl in self.phase_slots:
            self.free_slots[e_].append(sl)
        self.phase_slots = []

    def replay(self, nc, stack, block, final_waits=()):
        for e in ENGS:
            c = 0
            for ins in self.q[e]:
                if (not ins.dma) and ins.signal:
                    c += 1
                    ins.count = c
        esem = {e: stack.enter_context(nc.semaphore("es_" + e)) for e in ENGS}
        dsem = [stack.enter_context(nc.semaphore("ds_%d" % i)) for i in range(len(self.dma_cnt))]
        prog = self

        def run(engname, engobj):
            waited_e = {e: 0 for e in ENGS}
            waited_d = [0] * len(prog.dma_cnt)
            for ins in prog.q[engname]:
                for d in ins.deps:
                    if d.dma:
                        if waited_d[d.dsem] < d.dval:
                            engobj.wait_ge(dsem[d.dsem], d.dval)
                            waited_d[d.dsem] = d.dval
                    else:
                        if waited_e[d.eng] < d.count:
                            engobj.wait_ge(esem[d.eng], d.count)
                            waited_e[d.eng] = d.count
                if ins.fn is None:
                    continue
                r = ins.fn(engobj)
                if ins.dma:
                    r.then_inc(dsem[ins.dsem], 16)
                elif ins.signal:
                    r.then_inc(esem[ins.eng], 1)
            if engname == "sp":
                for d in final_waits:
                    engobj.wait_ge(dsem[d.dsem], d.dval)

        block.tensor(lambda t: run("pe", t))
        block.scalar(lambda t: run("act", t))
        block.vector(lambda t: run("dve", t))
        block.gpsimd(lambda t: run("pool", t))
        block.sync(lambda t: run("sp", t))


class Tl:
    __slots__ = ("t", "b")

    def __init__(self, t, name, persist=False):
        self.t = t
        self.b = Buf(name, persist)


class Rot:
    def __init__(self, items):
        self.items = items
        self.i = 0

    def next(self):
        r = self.items[self.i % len(self.items)]
        self.i += 1
        return r


def build_program(L, DEPTH, stop=None):
    import os
    stop = stop or os.environ.get('KSTOP')
    T = LC + L
    NCH = T // 128
    NB = L // 128
    nc = bass.Bass("TRN2", target_bir_lowering=False)
    P = Prog()

    def din(name, shape, dt=F32):
        return nc.dram_tensor(name, list(shape), dt, kind="ExternalInput").ap()

    def dscr(name, shape, dt):
        return nc.dram_tensor(name, list(shape), dt).ap()

    xin = din("xin", [D, T])
    cT = din("cT", [128, 16])
    wada = din("wada", [DEPTH * D, 6 * D])
    bada = din("bada", [128, DEPTH * 96])
    gmix = din("gmix", [128, DEPTH * 16])
    gffn = din("gffn", [128, DEPTH * 16])
    gfin = din("gfin", [128, 8])
    win = din("win", [DEPTH * D, INW])
    wg = din("wg", [33, DEPTH * 384])
    gng = din("gng", [128, DEPTH * 384])
    scw = din("scw", [128, DEPTH * 8])
    nab = din("nab", [DEPTH * 5 * 128, 6 * 5 * 128])
    wout = din("wout", [DEPTH * D, D])
    wup = din("wup", [DEPTH * D, 2 * FFN])
    fcw = din("fcw", [128, DEPTH * 176])
    wdown = din("wdown", [DEPTH * FFN, D])
    rope = din("rope", [T, 576])
    ctri = din("ctri", [128, 512])
    cmask = din("cmask", [128, 2 * 768])
    cmisc = din("cmisc", [128, 128 + 128 + 384])
    out = nc.dram_tensor("out", [D, L], F32, kind="ExternalOutput").ap()

    XA = dscr("XA", [D, T], F32)
    XB = dscr("XB", [D, T], F32)
    scu = dscr("scu", [768, T], F32)
    naq = dscr("naq", [384, T], BF16)
    nak = dscr("nak", [384, T], BF16)
    nav = dscr("nav", [T, 390], BF16)
    gbT = dscr("gbT", [NCH * 96, 512], BF16)
    gkv = dscr("gkv", [T, 576], BF16)
    gof = dscr("gof", [T, 384], F32)
    gr = dscr("gr", [T, 384], F32)
    mixT = dscr("mixT", [D, T], BF16)

    with ExitStack() as top:
        E = top.enter_context

        uid = [0]

        def sb(stack, name, shape, dt, persist=False):
            uid[0] += 1
            nm = "%s_%d" % (name, uid[0])
            return Tl(stack.enter_context(nc.sbuf_tensor(nm, list(shape), dt)), nm, persist)

        psum = E(nc.psum_tensor("psum", [128, 4096], F32))
        psum_bf = psum.bitcast(BF16)
        PB = [Buf("bank%d" % i, True) for i in range(8)]

        def ps(bank, c0, c1, p0=0, p1=128):
            return psum[p0:p1, bank * 512 + c0: bank * 512 + c1]

        def psb(bank, c0, c1, p0=0, p1=128):
            return psum_bf[p0:p1, bank * 1024 + c0: bank * 1024 + c1]

        arena = sb(top, "arena", [128, 8 * 2 * FFN], BF16, True)
        WOUT0 = 8 * INW
        cst = sb(top, "cst", [128, 8], F32, True)
        ones_bf = sb(top, "ones_bf", [128, 128], BF16, True)
        ident = sb(top, "ident", [128, 128], BF16, True)
        bmask = sb(top, "bmask", [128, 384], BF16, True)
        ones_f = sb(top, "ones_f", [128, 2], F32, True)
        mod = sb(top, "mod", [128, DEPTH * 96], F32, True)
        A1 = sb(top, "A1", [128, DEPTH * 16], F32, True)
        A2 = sb(top, "A2", [128, DEPTH * 16], F32, True)
        gmx = sb(top, "gmx", [128, DEPTH * 16], F32, True)
        gff = sb(top, "gff", [128, DEPTH * 16], F32, True)
        gfn = sb(top, "gfn", [128, 8], F32, True)
        scwt = sb(top, "scwt", [128, DEPTH * 8], F32, True)
        wgt = sb(top, "wgt", [33, DEPTH * 384], BF16, True)
        decb = sb(top, "decb", [96, NCH * 2], F32, True)

        def modcol(l, j, s):
            c = l * 96 + j * 2 + s
            return mod.t[:, c:c + 1]

        def acol(A, l, k, s):
            c = l * 16 + k * 2 + s
            return A.t[:, c:c + 1]

        def dma(eng, out_ap, in_ap, key, reads=(), writes=()):
            return P.emit(eng, lambda e: e.dma_start(out=out_ap, in_=in_ap), reads=reads, writes=writes, dma_key=key)

        def mm(out_ap, lhsT, rhs, start, stop, reads, writes, fence=False):
            if fence:
                P.pe_fence = True
            r = P.emit("pe", lambda e: e.matmul(out_ap, lhsT=lhsT, rhs=rhs, start=start, stop=stop),
                       reads=reads, writes=writes)
            if fence:
                P.pe_fence = True
            return r

        def tr(out_ap, in_ap, reads, writes):
            return P.emit("pe", lambda e: e.matmul(out_ap, lhsT=in_ap, rhs=ident.t[:], start=True, stop=True),
                          reads=list(reads) + [ident.b], writes=writes)

        def act(out_ap, in_ap, func, reads, writes, bias=None, scale=None):
            kw = {}
            if bias is not None:
                kw["bias"] = bias
            if scale is not None:
                kw["scale"] = scale
            return P.emit("act", lambda e: e.activation(out=out_ap, in_=in_ap, func=func, **kw), reads=reads,
                          writes=writes)

        def tt(eng, out_ap, a, b, op, reads, writes):
            return P.emit(eng, lambda e: e.tensor_tensor(out=out_ap, in0=a, in1=b, op=op), reads=reads, writes=writes)

        def ts(eng, out_ap, a, s1, op0, reads, writes, s2=None, op1=None):
            if op1 is None:
                return P.emit(eng, lambda e: e.tensor_scalar(out=out_ap, in0=a, scalar1=s1, scalar2=None, op0=op0),
                              reads=reads, writes=writes)
            return P.emit(eng, lambda e: e.tensor_scalar(out=out_ap, in0=a, scalar1=s1, scalar2=s2, op0=op0, op1=op1),
                          reads=reads, writes=writes)

        def stt(out_ap, a, s, b, op0, op1, reads, writes):
            return P.emit("dve", lambda e: e.scalar_tensor_tensor(out=out_ap, in0=a, scalar=s, in1=b, op0=op0, op1=op1),
                          reads=reads, writes=writes)

        def cp(eng, out_ap, in_ap, reads, writes):
            if eng == "act":
                return P.emit("act", lambda e: e.copy(out=out_ap, in_=in_ap), reads=reads, writes=writes)
            return P.emit(eng, lambda e: e.tensor_copy(out=out_ap, in_=in_ap), reads=reads, writes=writes)

        def mset(eng, ap, val, writes):
            return P.emit(eng, lambda e: e.memset(ap, val), writes=writes)

        def rows3(ap2d, p=128):
            return ap2d.rearrange("(m p) t -> p m t", p=p)

        mset("dve", cst.t[:, 0:1], EPS, [cst.b])
        mset("dve", cst.t[:, 1:2], float(np.log(32.0 ** -0.5)), [cst.b])
        mset("dve", cst.t[:, 2:3], 1.0, [cst.b])
        mset("dve", ones_f.t[:], 1.0, [ones_f.b])
        dma("pool", ones_bf.t[:], cmisc[:, 0:128], ones_bf.b, writes=[ones_bf.b])
        dma("pool", ident.t[:], cmisc[:, 128:256], ident.b, writes=[ident.b])
        dma("pool", bmask.t[:], cmisc[:, 256:640], bmask.b, writes=[bmask.b])
        dma("pool", wgt.t[:], wg, wgt.b, writes=[wgt.b])
        dma("sp", gmx.t[:], gmix, gmx.b, writes=[gmx.b])
        dma("sp", gff.t[:], gffn, gff.b, writes=[gff.b])
        dma("sp", gfn.t[:], gfin, gfn.b, writes=[gfn.b])
        dma("sp", scwt.t[:], scw, scwt.b, writes=[scwt.b])

        def load_win(l):
            for k in range(KD):
                dma("pool", arena.t[:, k * INW:(k + 1) * INW], win[l * D + k * 128: l * D + (k + 1) * 128, :],
                    arena.b, writes=[arena.b])

        def load_wout(l):
            for k in range(KD):
                dma("pool", arena.t[:, WOUT0 + k * D: WOUT0 + (k + 1) * D],
                    wout[l * D + k * 128: l * D + (k + 1) * 128, :], arena.b, writes=[arena.b])

        def load_wup(l):
            for k in range(KD):
                dma("pool", arena.t[:, k * 2 * FFN:(k + 1) * 2 * FFN],
                    wup[l * D + k * 128: l * D + (k + 1) * 128, :], arena.b, writes=[arena.b])

        def load_wdown(l, wdn):
            for c in range(NF):
                dma("pool", wdn.t[:, c * D:(c + 1) * D], wdown[l * FFN + c * 128: l * FFN + (c + 1) * 128, :], wdn.b,
                    writes=[wdn.b])

        with ExitStack() as ph:
            cin = sb(ph, "cin", [128, 16], F32)
            csl = sb(ph, "csl", [128, 16], BF16)
            bad = sb(ph, "bad", [128, DEPTH * 96], F32)
            wst = [sb(ph, "wst%d" % i, [128, 8 * 768], BF16) for i in range(2)]
            dma("sp", cin.t[:], cT, cin.b, writes=[cin.b])
            dma("sp", bad.t[:], bada, bad.b, writes=[bad.b])
            act(csl.t[:], cin.t[:], AF.Silu, [cin.b], [csl.b])
            pi = 0
            for l in range(DEPTH):
                bank = l % 2
                for piece in range(8):
                    w = wst[pi % 2]
                    pi += 1
                    for k in range(KD):
                        dma("pool", w.t[:, k * 768:(k + 1) * 768],
                            wada[l * D + k * 128: l * D + (k + 1) * 128, piece * 768:(piece + 1) * 768], w.b,
                            writes=[w.b])
                    for jj in range(6):
                        j = piece * 6 + jj
                        for k in range(KD):
                            mm(ps(bank, j * 2, j * 2 + 2), w.t[:, k * 768 + jj * 128: k * 768 + (jj + 1) * 128],
                               csl.t[:, k * 2:(k + 1) * 2], k == 0, k == KD - 1, [w.b, csl.b], [PB[bank]])
                tt("dve", mod.t[:, l * 96:(l + 1) * 96], ps(bank, 0, 96), bad.t[:, l * 96:(l + 1) * 96], ALU.add,
                   [PB[bank], bad.b], [mod.b])
                stt(A1.t[:, l * 16:(l + 1) * 16], mod.t[:, l * 96 + 16: l * 96 + 32], 1.0, gmx.t[:, l * 16:(l + 1) * 16],
                    ALU.add, ALU.mult, [mod.b, gmx.b], [A1.b])
                stt(A2.t[:, l * 16:(l + 1) * 16], mod.t[:, l * 96 + 64: l * 96 + 80], 1.0, gff.t[:, l * 16:(l + 1) * 16],
                    ALU.add, ALU.mult, [mod.b, gff.b], [A2.b])
            P.barrier()

        def norm_mod(getx, n, A, l, shj, s, hT, sqs, sd, rstd, tmps, ssbank):
            for k in range(KD):
                xb, xap = getx(k)
                sq = sqs.next()
                act(sq.t[:, :n], xap, AF.Square, [xb.b], [sq.b])
                mm(ps(ssbank, 0, n), ones_bf.t[:], sq.t[:, :n], k == 0, k == KD - 1, [ones_bf.b, sq.b], [PB[ssbank]])
            act(sd.t[:, :n], ps(ssbank, 0, n), AF.Sqrt, [PB[ssbank], cst.b], [sd.b], bias=cst.t[:, 0:1], scale=1.0 / D)
            P.emit("dve", lambda e: e.reciprocal(out=rstd.t[:, :n], in_=sd.t[:, :n]), reads=[sd.b], writes=[rstd.b])
            for k in range(KD):
                xb, xap = getx(k)
                tm = tmps.next()
                stt(tm.t[:, :n], xap, acol(A, l, k, s), rstd.t[:, :n], ALU.mult, ALU.mult, [xb.b, A.b, rstd.b], [tm.b])
                act(hT.t[:, k, :n], tm.t[:, :n], AF.Identity, [tm.b, mod.b], [hT.b], bias=modcol(l, shj + k, s), scale=1.0)

        def xloader(X, lo, n, xks):
            def getx(k):
                xk = xks.next()
                dma("sp", xk.t[:, :n], X[k * 128:(k + 1) * 128, lo:lo + n], xk.b, writes=[xk.b])
                return xk, xk.t[:, :n]
            return getx

        tiles512 = [(0, LC, 1)] + [(LC + i * 512, 512, 0) for i in range(L // 512)]

        load_win(0)
        load_wout(0)
        for l in range(DEPTH):
            X0 = xin if l == 0 else XA

            if stop == 'PRO':
                break
            with ExitStack() as ph:
                xks = Rot([sb(ph, "xk%d" % i, [128, 512], F32) for i in range(3)])
                tri = sb(ph, "tri", [128, 512], F32)
                msk = sb(ph, "msk", [128, 384], BF16)
                dma("sp", tri.t[:], ctri, tri.b, writes=[tri.b])
                dma("pool", msk.t[:], cmask[:, 0:384], msk.b, writes=[msk.b])
                sqs = Rot([sb(ph, "sq%d" % i, [128, 512], BF16) for i in range(3)])
                sd = sb(ph, "sd", [128, 512], F32)
                rstd = sb(ph, "rstd", [128, 512], F32)
                tmps = Rot([sb(ph, "tm%d" % i, [128, 512], F32) for i in range(2)])
                hTs = [sb(ph, "hT%d" % i, [128, 8, 512], BF16) for i in range(2)]
                scos = Rot([sb(ph, "sco%d" % i, [128, 512], F32) for i in range(3)])
                nqks = Rot([sb(ph, "nqk%d" % i, [128, 512], BF16) for i in range(3)])
                glrs = [sb(ph, "glr%d" % i, [33, 512], BF16) for i in range(2)]

                def pool_(name, shape, dt, depth):
                    return [sb(ph, "%s%d" % (name, i), shape, dt) for i in range(depth)]
                rps = pool_("rp", [128, 576], F32, 2)
                qks = pool_("qk", [128, 384], F32, 2)
                kvs = pool_("kv", [128, 576], BF16, 6)
                rrs = pool_("rr", [128, 384], F32, 2)
                nvs = pool_("nv", [128, 390], BF16, 2)
                ees = pool_("ee", [128, 384], F32, 2)
                lls = pool_("ll", [128, 384], F32, 2)
                fac = sb(ph, "fac", [128, 6, 192], F32)
                tcs = sb(ph, "tcs", [128, 384], F32)
                m12 = sb(ph, "m12", [128, 2, 192], F32)
                rqs = pool_("rq", [128, 384], F32, 2)
                gps = pool_("gp", [128, 4, 192], BF16, 2)
                gTfs = pool_("gTf", [96, 512], BF16, 3)
                gTbs = pool_("gTb", [96, 512], BF16, 2)
                atms = pool_("atm", [128, 6, 128], BF16, 2)
                kefs = pool_("kef", [128, 192], BF16, 4)
                decs = pool_("dec", [96, 4], F32, 4)
                ofs = pool_("of", [128, 384], F32, 2)
                um = sb(ph, "um", [96, 384], F32)
                Sf = sb(ph, "Sf", [96, 384], F32)
                Sfb = sb(ph, "Sfb", [96, 384], BF16)
                for g_ in glrs:
                    mset("pool", g_.t[32:33, :], 1.0, [g_.b])
                for n_ in nvs:
                    mset("pool", n_.t[:], 1.0, [n_.b])
                mset("dve", Sf.t[:], 0.0, [Sf.b])
                mset("dve", Sfb.t[:], 0.0, [Sfb.b])
                fmb = Rot([1, 2])
                tmb = Rot([3, 4])
                evr = Rot(["act", "dve"])

                def tile_norm(ti):
                    t0, n, s = tiles512[ti]
                    norm_mod(xloader(X0, t0, n, xks), n, A1, l, 0, s, hTs[ti % 2], sqs, sd, rstd, tmps, 0)

                def tile_fm(ti):
                    t0, n, s = tiles512[ti]
                    hT = hTs[ti % 2]
                    glr = glrs[ti % 2]
                    for m in range(12):
                        bk = fmb.next()
                        c0 = 1536 + m * 128
                        for k in range(KD):
                            mm(ps(bk, 0, n), arena.t[:, k * INW + c0: k * INW + c0 + 128], hT.t[:, k, :n], k == 0,
                               k == KD - 1, [arena.b, hT.b], [PB[bk]])
                        if m < 6:
                            sco = scos.next()
                            cp(evr.next(), sco.t[:, :n], ps(bk, 0, n), [PB[bk]], [sco.b])
                            dma("sp", scu[m * 128:(m + 1) * 128, t0:t0 + n], sco.t[:, :n], sco.b, reads=[sco.b])
                        else:
                            nqk = nqks.next()
                            cp(evr.next(), nqk.t[:, :n], ps(bk, 0, n), [PB[bk]], [nqk.b])
                            dst = naq if m < 9 else nak
                            mm_ = (m - 6) % 3
                            dma("sp", dst[mm_ * 128:(mm_ + 1) * 128, t0:t0 + n], nqk.t[:, :n], nqk.b, reads=[nqk.b])
                    bk = fmb.next()
                    for k in range(KD):
                        mm(ps(bk, 0, n, 0, 32), arena.t[:, k * INW + 3072: k * INW + 3104], hT.t[:, k, :n], k == 0,
                           k == KD - 1, [arena.b, hT.b], [PB[bk]])
                    cp(evr.next(), glr.t[0:32, :n], ps(bk, 0, n, 0, 32), [PB[bk]], [glr.b])

                chunks = []
                for ti, (t0, n, s) in enumerate(tiles512):
                    for c in range(n // 128):
                        chunks.append((ti, c))

                def st0(i):
                    ti, c = chunks[i]
                    t0 = tiles512[ti][0]
                    tk0 = t0 + c * 128
                    cs = slice(c * 128, (c + 1) * 128)
                    hT = hTs[ti % 2]
                    qk, kv, rr, nv, rp = qks[i % 2], kvs[i % 6], rrs[i % 2], nvs[i % 2], rps[i % 2]
                    dma("sp", rp.t[:], rope[tk0:tk0 + 128, :], rp.b, writes=[rp.b])
                    for g in range(4):
                        bk = tmb.next()
                        for k in range(KD):
                            mm(ps(bk, 0, 384), hT.t[:, k, cs], arena.t[:, k * INW + g * 384: k * INW + (g + 1) * 384],
                               k == 0, k == KD - 1, [arena.b, hT.b], [PB[bk]])
                        if g == 0:
                            cp("act", qk.t[:], ps(bk, 0, 384), [PB[bk]], [qk.b])
                        elif g == 1:
                            cp("dve", kv.t[:, 0:384], ps(bk, 0, 384), [PB[bk]], [kv.b])
                        elif g == 2:
                            cp("act", rr.t[:], ps(bk, 0, 384), [PB[bk]], [rr.b])
                        else:
                            cp("dve", nv.t[:].rearrange("p (h e) -> p h e", h=6)[:, :, 0:64],
                               ps(bk, 0, 384).rearrange("p (h e) -> p h e", h=6), [PB[bk]], [nv.b])
                    dma("sp", gr[tk0:tk0 + 128, :], rr.t[:], rr.b, reads=[rr.b])
                    dma("sp", nav[tk0:tk0 + 128, :], nv.t[:], nv.b, reads=[nv.b])

                def st1(i):
                    ti, c = chunks[i]
                    cs = slice(c * 128, (c + 1) * 128)
                    glr = glrs[ti % 2]
                    qk, rp, rq, ee, ll = qks[i % 2], rps[i % 2], rqs[i % 2], ees[i % 2], lls[i % 2]
                    mm(ps(5, 0, 384), glr.t[0:33, cs], wgt.t[:, l * 384:(l + 1) * 384], True, True, [glr.b, wgt.b],
                       [PB[5]])
                    act(ee.t[:], ps(5, 0, 384), AF.Exp, [PB[5]], [ee.b], scale=-1.0)
                    act(ll.t[:], ee.t[:], AF.Ln, [ee.b, cst.b], [ll.b], bias=cst.t[:, 2:3], scale=1.0)
                    q4 = qk.t[:].rearrange("p (h t e) -> p h t e", h=12, t=2)
                    t4 = tcs.t[:].rearrange("p (h t e) -> p h t e", h=12, t=2)
                    r4 = rq.t[:].rearrange("p (h t e) -> p h t e", h=12, t=2)
                    sn = rp.t[:, 384:576].rearrange("p (h e) -> p h e", h=12)
                    tt("pool", tcs.t[:], qk.t[:], rp.t[:, 0:384], ALU.mult, [qk.b, rp.b], [tcs.b])
                    tt("pool", m12.t[:, 0, :].rearrange("p (h e) -> p h e", h=12), q4[:, :, 1, :], sn, ALU.mult,
                       [qk.b, rp.b], [m12.b])
                    tt("pool", m12.t[:, 1, :].rearrange("p (h e) -> p h e", h=12), q4[:, :, 0, :], sn, ALU.mult,
                       [qk.b, rp.b], [m12.b])
                    tt("pool", r4[:, :, 0, :], t4[:, :, 0, :], m12.t[:, 0, :].rearrange("p (h e) -> p h e", h=12),
                       ALU.subtract, [tcs.b, m12.b], [rq.b])
                    tt("pool", r4[:, :, 1, :], t4[:, :, 1, :], m12.t[:, 1, :].rearrange("p (h e) -> p h e", h=12),
                       ALU.add, [tcs.b, m12.b], [rq.b])

                def st2(i):
                    ti, c = chunks[i]
                    ch = (tiles512[ti][0] // 128) + c
                    tk0 = ch * 128
                    ll, rq, gp, kv, kef, dec = lls[i % 2], rqs[i % 2], gps[i % 2], kvs[i % 6], kefs[i % 4], decs[i % 4]
                    P.pe_fence = True
                    for d_ in range(2):
                        bk = 6 + d_
                        for jj in range(2):
                            mm(ps(bk, jj * 192, (jj + 1) * 192), tri.t[:, (d_ * 2 + jj) * 128:(d_ * 2 + jj + 1) * 128],
                               ll.t[:, d_ * 192:(d_ + 1) * 192], True, True, [tri.b, ll.b], [PB[bk]])
                    for d_ in range(2):
                        for g in range(2):
                            i_ = d_ * 2 + g
                            mm(ps(5, 400 + i_, 401 + i_, 0, 96), ll.t[:, d_ * 192 + g * 96: d_ * 192 + (g + 1) * 96],
                               ones_f.t[:, 0:1], True, True, [ll.b, ones_f.b], [PB[5]])
                    P.pe_fence = True
                    act(dec.t[:], ps(5, 400, 404, 0, 96), AF.Exp, [PB[5]], [dec.b], scale=-1.0 / 16)
                    cp("dve", decb.t[:, ch * 2:(ch + 1) * 2], dec.t[:, 2:4], [dec.b], [decb.b])
                    for d_ in range(2):
                        bk = 6 + d_
                        act(fac.t[:, d_ * 3 + 0, :], ps(bk, 0, 192), AF.Exp, [PB[bk], cst.b], [fac.b],
                            bias=cst.t[:, 1:2], scale=-1.0 / 16)
                        act(fac.t[:, d_ * 3 + 1, :], ps(bk, 0, 192), AF.Exp, [PB[bk]], [fac.b], scale=1.0 / 16)
                        act(fac.t[:, d_ * 3 + 2, :], ps(bk, 192, 384), AF.Exp, [PB[bk]], [fac.b], scale=-1.0 / 16)
                    tt("dve", gp.t[:, 0, :], rq.t[:, 0:192], fac.t[:, 0, :], ALU.mult, [rq.b, fac.b], [gp.b])
                    tt("dve", gp.t[:, 1, :], rq.t[:, 192:384], fac.t[:, 1, :], ALU.mult, [rq.b, fac.b], [gp.b])
                    tt("dve", gp.t[:, 2, :], rq.t[:, 0:192], fac.t[:, 3, :], ALU.mult, [rq.b, fac.b], [gp.b])
                    tt("dve", gp.t[:, 3, :], rq.t[:, 192:384], fac.t[:, 4, :], ALU.mult, [rq.b, fac.b], [gp.b])
                    tt("dve", kv.t[:, 384:576], rq.t[:, 192:384], fac.t[:, 5, :], ALU.mult, [rq.b, fac.b], [kv.b])
                    tt("pool", kef.t[:], rq.t[:, 192:384], fac.t[:, 2, :], ALU.mult, [rq.b, fac.b], [kef.b])
                    dma("sp", gkv[tk0:tk0 + 128, :], kv.t[:], kv.b, reads=[kv.b])

                def st3(i):
                    ti, c = chunks[i]
                    ch = (tiles512[ti][0] // 128) + c
                    gp, gTf, gTb = gps[i % 2], gTfs[i % 3], gTbs[i % 2]
                    for j in range(4):
                        for g in range(2):
                            tb_ = 1 if j < 2 else 2
                            tc_ = ((j % 2) * 2 + g) * 128
                            tr(ps(tb_, tc_, tc_ + 128, 0, 96), gp.t[:, j, g * 96:(g + 1) * 96], [gp.b], [PB[tb_]])
                    cp("act", gTf.t[:], ps(1, 0, 512, 0, 96), [PB[1]], [gTf.b])
                    cp("dve", gTb.t[:], ps(2, 0, 512, 0, 96), [PB[2]], [gTb.b])
                    dma("sp", gbT[ch * 96:(ch + 1) * 96, :], gTb.t[:], gTb.b, reads=[gTb.b])

                def st4(i):
                    gTf, atm = gTfs[i % 3], atms[i % 2]
                    for h in range(6):
                        g, hp = h // 3, h % 3
                        bk = 6 + g
                        mm(ps(bk, hp * 128, (hp + 1) * 128), gTf.t[32 * hp:32 * hp + 32, (2 + g) * 128:(3 + g) * 128],
                           gTf.t[32 * hp:32 * hp + 32, g * 128:(g + 1) * 128], True, True, [gTf.b], [PB[bk]], fence=True)
                    for g in range(2):
                        tt("dve", atm.t[:, 3 * g:3 * g + 3, :], ps(6 + g, 0, 384).rearrange("p (h i) -> p h i", h=3),
                           msk.t[:, 0:384].rearrange("p (h i) -> p h i", h=3), ALU.mult, [PB[6 + g], msk.b], [atm.b])

                def st5(i):
                    ti, c = chunks[i]
                    ch = (tiles512[ti][0] // 128) + c
                    tk0 = ch * 128
                    gTf, atm, kv, kef, dec, of = gTfs[i % 3], atms[i % 2], kvs[i % 6], kefs[i % 4], decs[i % 4], ofs[i % 2]
                    bo = tmb.next()
                    for h in range(6):
                        g, hp = h // 3, h % 3
                        mm(ps(bo, h * 64, (h + 1) * 64), atm.t[:, h, :], kv.t[:, h * 64:(h + 1) * 64], True, False,
                           [atm.b, kv.b], [PB[bo]])
                        mm(ps(bo, h * 64, (h + 1) * 64), gTf.t[32 * hp:32 * hp + 32, g * 128:(g + 1) * 128],
                           Sfb.t[32 * hp:32 * hp + 32, g * 192 + hp * 64: g * 192 + (hp + 1) * 64], False, True,
                           [gTf.b, Sfb.b], [PB[bo]])
                    cp("act", of.t[:], ps(bo, 0, 384), [PB[bo]], [of.b])
                    dma("sp", gof[tk0:tk0 + 128, :], of.t[:], of.b, reads=[of.b])
                    for g in range(2):
                        mm(ps(0, g * 192, (g + 1) * 192, 0, 96), kef.t[:, g * 96:(g + 1) * 96],
                           kv.t[:, g * 192:(g + 1) * 192], True, True, [kef.b, kv.b], [PB[0]])
                    tt("dve", um.t[:], ps(0, 0, 384, 0, 96), bmask.t[0:96, :], ALU.mult, [PB[0], bmask.b], [um.b])
                    for g in range(2):
                        stt(Sf.t[:, g * 192:(g + 1) * 192], Sf.t[:, g * 192:(g + 1) * 192], dec.t[:, g:g + 1],
                            um.t[:, g * 192:(g + 1) * 192], ALU.mult, ALU.add, [Sf.b, dec.b, um.b], [Sf.b])
                    cp("dve", Sfb.t[:], Sf.t[:], [Sf.b], [Sfb.b])

                stages = [st0, st1, st2, st3, st4, st5]
                if stop and stop.startswith('S1s'):
                    stages = stages[:int(stop[3:])]
                nchunks = len(chunks)
                tile_norm(0)
                for step in range(nchunks + len(stages) - 1):
                    if step < nchunks:
                        ti, c = chunks[step]
                        if c == 0:
                            tile_fm(ti)
                        if c == min(1, tiles512[ti][1] // 128 - 1) and ti + 1 < len(tiles512):
                            tile_norm(ti + 1)
                    for si, st in enumerate(stages):
                        i = step - si
                        if 0 <= i < nchunks:
                            st(i)
                P.barrier()

            if stop and stop.startswith('S1'):
                break
            with ExitStack() as ph:
                def pool_(name, shape, dt, depth):
                    return [sb(ph, "%s%d" % (name, i), shape, dt) for i in range(depth)]
                gTs = pool_("gT", [96, 512], BF16, 3)
                kvs = pool_("kvb", [128, 576], BF16, 3)
                ofs = pool_("ofb", [128, 384], F32, 3)
                rrs = pool_("rrb", [128, 384], F32, 4)
                atms = pool_("atmb", [128, 6, 128], BF16, 2)
                oos = pool_("oo", [128, 384], F32, 2)
                o2 = sb(ph, "o2", [128, 384], F32)
                ssq = sb(ph, "ssq", [128, 6], F32)
                rs = sb(ph, "rs", [128, 6], F32)
                on = sb(ph, "on", [128, 384], F32)
                sr = sb(ph, "sr", [128, 384], F32)
                gl = sb(ph, "gl", [128, 384], BF16)
                glTs = pool_("glT", [128, 3, 128], BF16, 2)
                um = sb(ph, "umb", [96, 384], F32)
                Sb = sb(ph, "Sb", [96, 384], F32)
                Sbb = sb(ph, "Sbb", [96, 384], BF16)
                msk = sb(ph, "mskb", [128, 384], BF16)
                gngt = sb(ph, "gngt", [128, 384], F32)
                dma("pool", msk.t[:], cmask[:, 768:1152], msk.b, writes=[msk.b])
                dma("sp", gngt.t[:], gng[:, l * 384:(l + 1) * 384], gngt.b, writes=[gngt.b])
                mset("dve", Sb.t[:], 0.0, [Sb.b])
                mset("dve", Sbb.t[:], 0.0, [Sbb.b])
                order = [1, 0] + list(range(NCH - 1, 1, -1))

                def g0(i):
                    ch = order[i]
                    tk0 = ch * 128
                    gT, kv, of, rr = gTs[i % 3], kvs[i % 3], ofs[i % 3], rrs[i % 4]
                    dma("sp", gT.t[:], gbT[ch * 96:(ch + 1) * 96, :], gT.b, writes=[gT.b])
                    dma("sp", kv.t[:], gkv[tk0:tk0 + 128, :], kv.b, writes=[kv.b])
                    dma("sp", of.t[:], gof[tk0:tk0 + 128, :], of.b, writes=[of.b])
                    dma("sp", rr.t[:], gr[tk0:tk0 + 128, :], rr.b, writes=[rr.b])

                def g1(i):
                    gT, atm = gTs[i % 3], atms[i % 2]
                    bA0 = 2 * (i % 2)
                    for h in range(6):
                        g, hp = h // 3, h % 3
                        bk = bA0 + g
                        mm(ps(bk, hp * 128, (hp + 1) * 128), gT.t[32 * hp:32 * hp + 32, (2 + g) * 128:(3 + g) * 128],
                           gT.t[32 * hp:32 * hp + 32, g * 128:(g + 1) * 128], True, True, [gT.b], [PB[bk]], fence=True)
                    for g in range(2):
                        tt("dve", atm.t[:, 3 * g:3 * g + 3, :], ps(bA0 + g, 0, 384).rearrange("p (h i) -> p h i", h=3),
                           msk.t[:, 0:384].rearrange("p (h i) -> p h i", h=3), ALU.mult, [PB[bA0 + g], msk.b], [atm.b])

                def g2(i):
                    ch = order[i]
                    gT, kv, of, atm, oo = gTs[i % 3], kvs[i % 3], ofs[i % 3], atms[i % 2], oos[i % 2]
                    bO, bU = 4 + (i % 2), 6
                    for h in range(6):
                        g, hp = h // 3, h % 3
                        mm(ps(bO, h * 64, (h + 1) * 64), atm.t[:, h, :], kv.t[:, h * 64:(h + 1) * 64], True, False,
                           [atm.b, kv.b], [PB[bO]])
                        mm(ps(bO, h * 64, (h + 1) * 64), gT.t[32 * hp:32 * hp + 32, g * 128:(g + 1) * 128],
                           Sbb.t[32 * hp:32 * hp + 32, g * 192 + hp * 64: g * 192 + (hp + 1) * 64], False, True,
                           [gT.b, Sbb.b], [PB[bO]])
                    for g in range(2):
                        mm(ps(bU, g * 192, (g + 1) * 192, 0, 96), kv.t[:, 384 + g * 96: 384 + (g + 1) * 96],
                           kv.t[:, g * 192:(g + 1) * 192], True, True, [kv.b], [PB[bU]])
                    tt("dve", um.t[:], ps(bU, 0, 384, 0, 96), bmask.t[0:96, :], ALU.mult, [PB[bU], bmask.b], [um.b])
                    for g in range(2):
                        stt(Sb.t[:, g * 192:(g + 1) * 192], Sb.t[:, g * 192:(g + 1) * 192],
                            decb.t[:, ch * 2 + g: ch * 2 + g + 1], um.t[:, g * 192:(g + 1) * 192], ALU.mult, ALU.add,
                            [Sb.b, decb.b, um.b], [Sb.b])
                    cp("dve", Sbb.t[:], Sb.t[:], [Sb.b], [Sbb.b])
                    tt("dve", oo.t[:], ps(bO, 0, 384), of.t[:], ALU.add, [PB[bO], of.b], [oo.b])

                def g3(i):
                    ch = order[i]
                    tk0 = ch * 128
                    oo, rr, glT = oos[i % 2], rrs[i % 4], glTs[i % 2]
                    tt("pool", o2.t[:], oo.t[:], oo.t[:], ALU.mult, [oo.b], [o2.b])
                    P.emit("dve", lambda e: e.tensor_reduce(
                        out=ssq.t[:], in_=o2.t[:].rearrange("p (h e) -> p h e", h=6), axis=AX.X, op=ALU.add),
                        reads=[o2.b], writes=[ssq.b])
                    act(rs.t[:], ssq.t[:], AF.Sqrt, [ssq.b, cst.b], [rs.b], bias=cst.t[:, 0:1], scale=1.0 / 64)
                    P.emit("dve", lambda e: e.reciprocal(out=rs.t[:], in_=rs.t[:]), reads=[rs.b], writes=[rs.b])
                    act(sr.t[:], rr.t[:], AF.Silu, [rr.b], [sr.b])
                    for h in range(6):
                        stt(on.t[:, h * 64:(h + 1) * 64], oo.t[:, h * 64:(h + 1) * 64], rs.t[:, h:h + 1],
                            gngt.t[:, h * 64:(h + 1) * 64], ALU.mult, ALU.mult, [oo.b, rs.b, gngt.b], [on.b])
                    tt("pool", gl.t[:], on.t[:], sr.t[:], ALU.mult, [on.b, sr.b], [gl.b])
                    for m in range(3):
                        tr(ps(7, m * 128, (m + 1) * 128), gl.t[:, m * 128:(m + 1) * 128], [gl.b], [PB[7]])
                    cp("act", glT.t[:], ps(7, 0, 384).rearrange("p (m t) -> p m t", m=3), [PB[7]], [glT.b])
                    dma("sp", rows3(mixT[0:384, tk0:tk0 + 128]), glT.t[:], glT.b, reads=[glT.b])

                stages = [g0, g1, g2, g3]
                for step in range(NCH + len(stages) - 1):
                    for si, st in enumerate(stages):
                        i = step - si
                        if 0 <= i < NCH:
                            st(i)
                P.barrier()

            if stop == 'GB':
                break
            with ExitStack() as ph:
                nb = sb(ph, "nb", [128, 6 * 5 * 128], F32)
                krs = [sb(ph, "kr%d" % i, [128, 3, 128], BF16) for i in range(8)]
                vrs = [sb(ph, "vr%d" % i, [128, 390], BF16) for i in range(8)]
                kc = sb(ph, "kc", [128, 3, 256], BF16)
                vc = sb(ph, "vc", [128, 2, 390], BF16)
                qts = Rot([sb(ph, "qt%d" % i, [128, 3, 128], BF16) for i in range(2)])
                stmps = Rot([sb(ph, "stmp%d" % i, [128, 640], F32) for i in range(2)])
                pts = Rot([sb(ph, "pt%d" % i, [128, 7, 128], BF16) for i in range(3)])
                rcs = [sb(ph, "rc%d" % i, [128, 6], F32) for i in range(2)]
                onts = [sb(ph, "ont%d" % i, [128, 384], BF16) for i in range(2)]
                naTs = Rot([sb(ph, "naT%d" % i, [128, 3, 128], BF16) for i in range(2)])
                dma("sp", kc.t[:], rows3(nak[:, 0:LC]), kc.b, writes=[kc.b])
                dma("sp", vc.t[:], nav[0:LC, :].rearrange("(c p) f -> p c f", p=128), vc.b, writes=[vc.b])
                loaded = {}
                cur_pat = [None]

                def ensure_chunk(cid):
                    slot = cid % 8
                    if loaded.get(slot) != cid:
                        tk = LC + cid * 128
                        dma("sp", krs[slot].t[:], rows3(nak[:, tk:tk + 128]), krs[slot].b, writes=[krs[slot].b])
                        dma("sp", vrs[slot].t[:], nav[tk:tk + 128, :], vrs[slot].b, writes=[vrs[slot].b])
                        loaded[slot] = cid
                    return slot

                def pat_of(a):
                    if a == 0:
                        return 1
                    if a == 1:
                        return 2
                    if a == NB - 2:
                        return 3
                    if a == NB - 1:
                        return 4
                    return 0

                blocks = [("c", 0), ("c", 1)] + [("l", a) for a in range(NB)]
                binfo = {}

                def blk_setup(bi):
                    kind, a = blocks[bi]
                    qt = qts.items[bi % 2]
                    if kind == "c":
                        tq = a * 128
                        slots = []
                    else:
                        tq = LC + a * 128
                        w0 = min(max(a - 2, 0), NB - 5)
                        slots = [ensure_chunk(w0 + c) for c in range(5)]
                        pat = pat_of(a)
                        if cur_pat[0] != pat:
                            r0 = (l * 5 + pat) * 128
                            dma("sp", nb.t[:], nab[r0:r0 + 128, :], nb.b, writes=[nb.b])
                            cur_pat[0] = pat
                    dma("sp", qt.t[:], rows3(naq[:, tq:tq + 128]), qt.b, writes=[qt.b])
                    binfo[bi] = (tq, slots, qt)

                def sa(g):
                    bi, h = divmod(g, 6)
                    if h == 0:
                        blk_setup(bi)
                    tq, slots, qt = binfo[bi]
                    m, pb = h // 2, (h % 2) * 64
                    b0 = 2 * (g % 2)
                    pt = pts.items[g % 3]
                    for c, sl in enumerate(slots):
                        bk, off = (b0, c * 128) if c < 4 else (b0 + 1, 0)
                        mm(ps(bk, off, off + 128), krs[sl].t[pb:pb + 64, m, :], qt.t[pb:pb + 64, m, :], True, True,
                           [krs[sl].b, qt.b], [PB[bk]])
                    for c in range(2):
                        mm(ps(b0 + 1, 128 + c * 128, 256 + c * 128), kc.t[pb:pb + 64, m, c * 128:(c + 1) * 128],
                           qt.t[pb:pb + 64, m, :], True, True, [kc.b, qt.b], [PB[b0 + 1]])
                    if slots:
                        stmp = stmps.items[g % 2]
                        nbh = nb.t[:, h * 640:(h + 1) * 640]
                        stt(stmp.t[:, 0:512], ps(b0, 0, 512), 0.125, nbh[:, 0:512], ALU.mult, ALU.add,
                            [PB[b0], nb.b], [stmp.b])
                        stt(stmp.t[:, 512:640], ps(b0 + 1, 0, 128), 0.125, nbh[:, 512:640], ALU.mult, ALU.add,
                            [PB[b0 + 1], nb.b], [stmp.b])
                        act(pt.t[:, 0:5, :], stmp.t[:].rearrange("p (c q) -> p c q", c=5), AF.Exp, [stmp.b], [pt.b])
                    act(pt.t[:, 5:7, :], ps(b0 + 1, 128, 384).rearrange("p (c q) -> p c q", c=2), AF.Exp, [PB[b0 + 1]],
                        [pt.b], scale=0.125)

                def sc(g):
                    bi, h = divmod(g, 6)
                    tq, slots, qt = binfo[bi]
                    bOV = 4 + (bi % 2)
                    pt = pts.items[g % 3]
                    nk = len(slots) + 2
                    ops_ = [(c, vrs[sl].t[:, h * 65:(h + 1) * 65], vrs[sl].b) for c, sl in enumerate(slots)]
                    ops_ += [(5 + c, vc.t[:, c, h * 65:(h + 1) * 65], vc.b) for c in range(2)]
                    for i_, (pc, vap, vb) in enumerate(ops_):
                        mm(ps(bOV, h * 65, (h + 1) * 65), pt.t[:, pc, :], vap, i_ == 0, i_ == nk - 1, [pt.b, vb], [PB[bOV]])

                def tail(bi):
                    tq, slots, qt = binfo[bi]
                    bOV = 4 + (bi % 2)
                    bTP = 6 + (bi % 2)
                    naT = naTs.items[bi % 2]
                    ont = onts[bi % 2]
                    rc = rcs[bi % 2]
                    ov3 = ps(bOV, 0, 390).rearrange("p (h e) -> p h e", h=6)
                    P.emit("dve", lambda e: e.reciprocal(out=rc.t[:], in_=ov3[:, :, 64]), reads=[PB[bOV]], writes=[rc.b])
                    for h in range(6):
                        if h % 2 == 0:
                            ts("dve", ont.t[:, h * 64:(h + 1) * 64], ps(bOV, h * 65, h * 65 + 64), rc.t[:, h:h + 1], ALU.mult,
                               [PB[bOV], rc.b], [ont.b])
                        else:
                            act(ont.t[:, h * 64:(h + 1) * 64], ps(bOV, h * 65, h * 65 + 64), AF.Identity, [PB[bOV], rc.b],
                                [ont.b], scale=rc.t[:, h:h + 1])
                    for m in range(3):
                        tr(ps(bTP, m * 128, (m + 1) * 128), ont.t[:, m * 128:(m + 1) * 128], [ont.b], [PB[bTP]])
                    cp("act", naT.t[:], ps(bTP, 0, 384).rearrange("p (m t) -> p m t", m=3), [PB[bTP]], [naT.b])
                    dma("sp", rows3(mixT[640:1024, tq:tq + 128]), naT.t[:], naT.b, reads=[naT.b])

                G = 6 * len(blocks)
                for g in range(G + 1):
                    if g < G:
                        sa(g)
                    if g >= 1:
                        sc(g - 1)
                        if (g - 1) % 6 == 5:
                            tail((g - 1) // 6)
                P.barrier()

            if stop == 'NA':
                break
            wstack = ExitStack()
            wdn = sb(wstack, "wdn", [128, NF * D], BF16, True)
            load_wdown(l, wdn)
            with ExitStack() as ph:
                mxs = [sb(ph, "mx%d" % i, [128, 8, 512], BF16) for i in range(2)]
                sus = [sb(ph, "su%d" % i, [128, 6, 514], F32) for i in range(2)]
                pr = sb(ph, "pr", [128, 2, 514], F32)
                cvs = Rot([sb(ph, "cv%d" % i, [128, 512], F32) for i in range(2)])
                xks = Rot([sb(ph, "xkc%d" % i, [128, 512], F32) for i in range(3)])
                xos = Rot([sb(ph, "xoc%d" % i, [128, 512], F32) for i in range(2)])
                bkr = Rot(list(range(8)))

                def c_prep(ti):
                    t0, n, s = tiles512[ti]
                    mx, su = mxs[ti % 2], sus[ti % 2]
                    s_lo, s_hi = (0, LC) if s == 1 else (LC, T)
                    lo, hi = max(t0 - 1, s_lo), min(t0 + n + 1, s_hi)
                    dst0 = lo - (t0 - 1)
                    if lo > t0 - 1:
                        mset("pool", su.t[:, :, 0:1], 0.0, [su.b])
                    if hi < t0 + n + 1:
                        mset("pool", su.t[:, :, n + 1:n + 2], 0.0, [su.b])
                    dma("sp", su.t[:, :, dst0:dst0 + hi - lo], rows3(scu[:, lo:hi]), su.b, writes=[su.b])
                    dma("sp", mx.t[:, 0:3, :n], rows3(mixT[0:384, t0:t0 + n]), mx.b, writes=[mx.b])
                    dma("sp", mx.t[:, 5:8, :n], rows3(mixT[640:1024, t0:t0 + n]), mx.b, writes=[mx.b])
                    tt("pool", pr.t[:, :, :n + 2], su.t[:, 2:4, :n + 2], su.t[:, 4:6, :n + 2], ALU.mult, [su.b], [pr.b])
                    for j in range(2):
                        cv = cvs.next()
                        wc = l * 8 + j * 4
                        act(cv.t[:, :n], pr.t[:, j, 1:n + 1], AF.Identity, [pr.b, scwt.b], [cv.b],
                            bias=scwt.t[:, wc + 3:wc + 4], scale=scwt.t[:, wc + 1:wc + 2])
                        stt(cv.t[:, :n], pr.t[:, j, 0:n], scwt.t[:, wc:wc + 1], cv.t[:, :n], ALU.mult, ALU.add,
                            [pr.b, scwt.b, cv.b], [cv.b])
                        stt(cv.t[:, :n], pr.t[:, j, 2:n + 2], scwt.t[:, wc + 2:wc + 3], cv.t[:, :n], ALU.mult, ALU.add,
                            [pr.b, scwt.b, cv.b], [cv.b])
                        tt("pool", mx.t[:, 3 + j, :n], su.t[:, j, 1:n + 1], cv.t[:, :n], ALU.mult, [su.b, cv.b], [mx.b])

                def c_main(ti):
                    t0, n, s = tiles512[ti]
                    mx = mxs[ti % 2]
                    for m in range(8):
                        bk = bkr.next()
                        xk = xks.next()
                        xo = xos.next()
                        dma("sp", xk.t[:, :n], X0[m * 128:(m + 1) * 128, t0:t0 + n], xk.b, writes=[xk.b])
                        for k in range(KD):
                            mm(ps(bk, 0, n), arena.t[:, WOUT0 + k * D + m * 128: WOUT0 + k * D + (m + 1) * 128],
                               mx.t[:, k, :n], k == 0, k == KD - 1, [arena.b, mx.b], [PB[bk]])
                        stt(xo.t[:, :n], ps(bk, 0, n), modcol(l, 16 + m, s), xk.t[:, :n], ALU.mult, ALU.add,
                            [PB[bk], mod.b, xk.b], [xo.b])
                        dma("sp", XB[m * 128:(m + 1) * 128, t0:t0 + n], xo.t[:, :n], xo.b, reads=[xo.b])

                c_prep(0)
                for ti in range(len(tiles512)):
                    if ti + 1 < len(tiles512):
                        c_prep(ti + 1)
                    c_main(ti)
                P.barrier()

            if stop == 'C':
                wstack.close()
                break
            load_wup(l)
            with ExitStack() as ph:
                WD = 484
                xks = Rot([sb(ph, "xkd%d" % i, [128, WD], F32) for i in range(3)])
                fcwt = sb(ph, "fcwt", [128, 176], F32)
                dma("sp", fcwt.t[:], fcw[:, l * 176:(l + 1) * 176], fcwt.b, writes=[fcwt.b])
                sqs = Rot([sb(ph, "sqd%d" % i, [128, WD], BF16) for i in range(2)])
                sd = sb(ph, "sdd", [128, WD], F32)
                rstd = sb(ph, "rstdd", [128, WD], F32)
                tmps = Rot([sb(ph, "tmd%d" % i, [128, WD], F32) for i in range(2)])
                h2s = [sb(ph, "h2_%d" % i, [128, 8, WD], BF16) for i in range(2)]
                acts = sb(ph, "acts", [128, NF, WD], BF16)
                cas = Rot([sb(ph, "ca%d" % i, [128, WD], F32) for i in range(2)])
                cbs = Rot([sb(ph, "cb%d" % i, [128, WD], F32) for i in range(2)])
                sas = Rot([sb(ph, "sa%d" % i, [128, WD], F32) for i in range(1)])
                xos = Rot([sb(ph, "xo%d" % i, [128, WD], F32) for i in range(1)])
                bkr = Rot([1, 2, 3, 4, 5, 6, 7])
                ntl = -(-L // 482)
                no_l = -(-L // ntl)
                dt = [(0, LC, 1)]
                o = 0
                while o < L:
                    dt.append((LC + o, min(no_l, L - o), 0))
                    o += no_l

                def geom(ti):
                    o0, no, s = dt[ti]
                    s_lo, s_hi = (0, LC) if s == 1 else (LC, T)
                    lo, hi = max(o0 - 1, s_lo), min(o0 + no + 1, s_hi)
                    return o0, no, s, lo, hi - lo, o0 - lo

                def d_norm(ti):
                    o0, no, s, lo, nin, off = geom(ti)
                    norm_mod(xloader(XB, lo, nin, xks), nin, A2, l, 24, s, h2s[ti % 2], sqs, sd, rstd, tmps, 0)

                def d_up(ti):
                    o0, no, s, lo, nin, off = geom(ti)
                    h2 = h2s[ti % 2]
                    i0 = 1 if off == 0 else 0
                    j0 = 1 if off + no == nin else 0
                    for c in range(NF):
                        bks, cvs_, wcs = [], [], []
                        for half, crot in ((0, cas), (1, cbs)):
                            bk = bkr.next()
                            fc = half * NF + c
                            col0 = fc * 128
                            for k in range(KD):
                                mm(ps(bk, 0, nin), arena.t[:, k * 2 * FFN + col0: k * 2 * FFN + col0 + 128], h2.t[:, k, :nin],
                                   k == 0, k == KD - 1, [arena.b, h2.b], [PB[bk]])
                            bks.append(bk)
                            cvs_.append(crot.next())
                            wcs.append(fc * 4)
                        for bk, cv, wc in zip(bks, cvs_, wcs):
                            act(cv.t[:, :no], ps(bk, off, off + no), AF.Identity, [PB[bk], fcwt.b], [cv.b],
                                bias=fcwt.t[:, wc + 3:wc + 4], scale=fcwt.t[:, wc + 1:wc + 2])
                        for bk, cv, wc in zip(bks, cvs_, wcs):
                            stt(cv.t[:, i0:no], ps(bk, off + i0 - 1, off + no - 1), fcwt.t[:, wc:wc + 1], cv.t[:, i0:no],
                                ALU.mult, ALU.add, [PB[bk], fcwt.b, cv.b], [cv.b])
                        for bk, cv, wc in zip(bks, cvs_, wcs):
                            stt(cv.t[:, 0:no - j0], ps(bk, off + 1, off + 1 + no - j0), fcwt.t[:, wc + 2:wc + 3],
                                cv.t[:, 0:no - j0], ALU.mult, ALU.add, [PB[bk], fcwt.b, cv.b], [cv.b])
                        sa = sas.next()
                        act(sa.t[:, :no], cvs_[0].t[:, :no], AF.Silu, [cvs_[0].b], [sa.b])
                        tt("pool", acts.t[:, c, :no], sa.t[:, :no], cvs_[1].t[:, :no], ALU.mult, [sa.b, cvs_[1].b], [acts.b])

                def d_down(ti):
                    o0, no, s, lo, nin, off = geom(ti)
                    for m in range(8):
                        bk = bkr.next()
                        for c in range(NF):
                            mm(ps(bk, 0, no), wdn.t[:, c * D + m * 128: c * D + (m + 1) * 128], acts.t[:, c, :no], c == 0,
                               c == NF - 1, [wdn.b, acts.b], [PB[bk]])
                        xk = xks.next()
                        xo = xos.next()
                        dma("sp", xk.t[:, :no], XB[m * 128:(m + 1) * 128, o0:o0 + no], xk.b, writes=[xk.b])
                        stt(xo.t[:, :no], ps(bk, 0, no), modcol(l, 40 + m, s), xk.t[:, :no], ALU.mult, ALU.add,
                            [PB[bk], mod.b, xk.b], [xo.b])
                        dma("sp", XA[m * 128:(m + 1) * 128, o0:o0 + no], xo.t[:, :no], xo.b, reads=[xo.b])

                d_norm(0)
                for ti in range(len(dt)):
                    d_up(ti)
                    if ti + 1 < len(dt):
                        d_norm(ti + 1)
                    d_down(ti)
                if l + 1 < DEPTH:
                    load_win(l + 1)
                    load_wout(l + 1)
                P.barrier()
            wstack.close()

        finals = []
        with ExitStack() as ph:
            if stop:
                tiles512 = tiles512[:1]
            xts = Rot([sb(ph, "xf%d" % i, [128, 8, 512], F32) for i in range(2)])
            sqs = Rot([sb(ph, "sqf%d" % i, [128, 512], BF16) for i in range(2)])
            sd = sb(ph, "sdf", [128, 512], F32)
            rstd = sb(ph, "rstdf", [128, 512], F32)
            xos = Rot([sb(ph, "xof%d" % i, [128, 8, 512], F32) for i in range(2)])
            for (t0, n, s) in tiles512[1:]:
                xt = xts.next()
                xo = xos.next()
                dma("sp", xt.t[:, :, :n], rows3(XA[:, t0:t0 + n]), xt.b, writes=[xt.b])
                for k in range(KD):
                    sq = sqs.next()
                    act(sq.t[:, :n], xt.t[:, k, :n], AF.Square, [xt.b], [sq.b])
                    mm(ps(0, 0, n), ones_bf.t[:], sq.t[:, :n], k == 0, k == KD - 1, [ones_bf.b, sq.b], [PB[0]])
                act(sd.t[:, :n], ps(0, 0, n), AF.Sqrt, [PB[0], cst.b], [sd.b], bias=cst.t[:, 0:1], scale=1.0 / D)
                P.emit("dve", lambda e, sd=sd, rstd=rstd, n=n: e.reciprocal(out=rstd.t[:, :n], in_=sd.t[:, :n]),
                       reads=[sd.b], writes=[rstd.b])
                for k in range(KD):
                    stt(xo.t[:, k, :n], xt.t[:, k, :n], gfn.t[:, k:k + 1], rstd.t[:, :n], ALU.mult, ALU.mult,
                        [xt.b, gfn.b, rstd.b], [xo.b])
                finals.append(dma("sp", rows3(out[:, t0 - LC:t0 - LC + n]), xo.t[:, :, :n], xo.b, reads=[xo.b]))

        with nc.Block() as block:
            P.replay(nc, top, block, final_waits=finals)
    return nc


def _na_bias_tables(rpb, L):
    rows = L // GRID_W
    NB = L // 128
    H = rpb.shape[0]
    reps = [2, 0, 1, NB - 2, NB - 1]
    kk = np.arange(640)
    wrow, kcol = kk // 64, kk % 64
    qq = np.arange(128)
    qr_l, qcol = qq // 64, qq % 64
    out = np.empty((5, 128, H, 5, 128), np.float32)
    for pi, a in enumerate(reps):
        w0 = min(max(a - 2, 0), NB - 5)
        krow = 2 * w0 + wrow
        qrow = 2 * a + qr_l
        rs = np.clip(qrow - 4, 0, rows - 8)
        vrow = (krow[:, None] >= rs[None, :]) & (krow[:, None] < rs[None, :] + 8)
        cstart = np.clip(qcol - 8, 0, GRID_W - 16)
        vcol = (kcol[:, None] >= cstart[None, :]) & (kcol[:, None] < cstart[None, :] + 16)
        dr = np.clip(krow[:, None] - qrow[None, :] + 7, 0, 14)
        dc = np.clip(kcol[:, None] - qcol[None, :] + 15, 0, 30)
        valid = vrow & vcol
        for h in range(H):
            tab = np.where(valid, rpb[h][dr, dc], np.float32(-1e30)).astype(np.float32)
            out[pi, :, h, :, :] = tab.reshape(5, 128, 128).transpose(1, 0, 2)
    return out.reshape(5, 128, H * 5 * 128)


def _rope_table(L):
    T = LC + L
    t = np.arange(L)
    inv = (10000.0 ** (-np.arange(8, dtype=np.float32) / 8)).astype(np.float32)
    row = (t // GRID_W).astype(np.float32)[:, None] * inv
    col = (t % GRID_W).astype(np.float32)[:, None] * inv
    ang = np.concatenate([row, col], axis=-1).astype(np.float32)
    cos = np.ones((T, 16), np.float32)
    sin = np.zeros((T, 16), np.float32)
    cos[LC:] = np.cos(ang)
    sin[LC:] = np.sin(ang)
    tab = np.empty((T, 576), np.float32)
    tab[:, 0:384] = np.tile(cos, (1, 24))
    tab[:, 384:576] = np.tile(sin, (1, 12))
    return tab


def _pvec(v, nchunk):
    return np.ascontiguousarray(v.reshape(nchunk, 128).T)


def prepare_inputs(inp, L, DEPTH, ncores):
    f = lambda a: np.ascontiguousarray(np.asarray(a, dtype=np.float32))
    x, c, ctx, c_ctx = f(inp["x"]), f(inp["c"]), f(inp["ctx"]), f(inp["c_ctx"])
    perm32 = np.concatenate([np.arange(0, 32, 2), np.arange(1, 32, 2)])
    permq = np.concatenate([h * 32 + perm32 for h in range(6)])
    w_in = f(inp["w_in"])[:DEPTH]
    cols = np.concatenate([permq, 192 + permq, np.arange(384, 768), np.arange(800, 1184), np.arange(2720, 3104),
                           np.arange(1184, 1952), np.arange(1952, 2336), np.arange(2336, 2720), np.arange(768, 800)])
    assert cols.shape[0] == INW
    win = np.ascontiguousarray(w_in[:, :, cols]).reshape(DEPTH * D, INW)
    wg = np.zeros((33, DEPTH, 384), np.float32)
    for l in range(DEPTH):
        wg[0:16, l, 0:192] = f(inp["gla_wg2_fw"])[l][:, permq]
        wg[16:32, l, 192:384] = f(inp["gla_wg2_bw"])[l][:, permq]
        wg[32, l, 0:192] = f(inp["gla_bg_fw"])[l][permq]
        wg[32, l, 192:384] = f(inp["gla_bg_bw"])[l][permq]
    rep2 = lambda a: np.repeat(a[..., None], 2, axis=-1)
    bada = np.stack([rep2(_pvec(f(inp["b_ada"])[l], 48)) for l in range(DEPTH)], 1).reshape(128, DEPTH * 96)
    gmix = np.stack([rep2(_pvec(f(inp["norm_mix_g"])[l], 8)) for l in range(DEPTH)], 1).reshape(128, DEPTH * 16)
    gffn = np.stack([rep2(_pvec(f(inp["norm_ffn_g"])[l], 8)) for l in range(DEPTH)], 1).reshape(128, DEPTH * 16)
    gfin = _pvec(f(inp["final_norm_g"]), 8)
    gng = np.stack([np.broadcast_to(np.tile(f(inp["gla_norm_g"])[l], 6)[None, :], (128, 384)) for l in range(DEPTH)],
                   1).reshape(128, DEPTH * 384)
    scw = np.empty((128, DEPTH, 2, 4), np.float32)
    fcw = np.empty((128, DEPTH, 44, 4), np.float32)
    for l in range(DEPTH):
        for t_ in range(3):
            scw[:, l, :, t_] = _pvec(f(inp["sc_conv_w"])[l, t_], 2)
            fcw[:, l, :, t_] = _pvec(f(inp["ffn_conv_w"])[l, t_], 44)
        scw[:, l, :, 3] = _pvec(f(inp["sc_conv_b"])[l], 2)
        fcw[:, l, :, 3] = _pvec(f(inp["ffn_conv_b"])[l], 44)
    nab = np.concatenate([_na_bias_tables(f(inp["na_rpb"])[l], L) for l in range(DEPTH)], 0).reshape(DEPTH * 5 * 128, 3840)
    j = np.arange(128)
    M1 = (j[:, None] <= j[None, :]).astype(np.float32)
    M2 = (j[:, None] > j[None, :]).astype(np.float32)
    M3 = (j[:, None] >= j[None, :]).astype(np.float32)
    M4 = (j[:, None] < j[None, :]).astype(np.float32)
    ctri = np.concatenate([M1, M2, M3, M4], 1)
    cmask = np.concatenate([np.tile(M1, (1, 6)), np.tile(M3, (1, 6))], 1)
    bm = np.zeros((128, 384), np.float32)
    for g in range(2):
        for hp in range(3):
            bm[32 * hp:32 * hp + 32, g * 192 + hp * 64: g * 192 + (hp + 1) * 64] = 1.0
    cmisc = np.concatenate([np.ones((128, 128), np.float32), np.eye(128, dtype=np.float32), bm], 1)
    shared = {
        "wada": f(inp["w_ada"])[:DEPTH].reshape(DEPTH * D, 6 * D), "bada": bada, "gmix": gmix, "gffn": gffn, "gfin": gfin,
        "win": win, "wg": wg.reshape(33, DEPTH * 384), "gng": np.ascontiguousarray(gng),
        "scw": scw.reshape(128, DEPTH * 8), "nab": np.ascontiguousarray(nab),
        "wout": f(inp["w_out"])[:DEPTH].reshape(DEPTH * D, D), "wup": f(inp["ffn_w_up"])[:DEPTH].reshape(DEPTH * D, 2 * FFN),
        "fcw": fcw.reshape(128, DEPTH * 176), "wdown": f(inp["ffn_w_down"])[:DEPTH].reshape(DEPTH * FFN, D),
        "rope": _rope_table(L), "ctri": ctri, "cmask": cmask, "cmisc": cmisc,
    }
    maps = []
    for b in range(ncores):
        m = dict(shared)
        m["xin"] = np.ascontiguousarray(np.concatenate([ctx[b].T, x[b].T], axis=1))
        cc = np.stack([_pvec(c[b], 8), _pvec(c_ctx, 8)], -1).reshape(128, 16)
        m["cT"] = np.ascontiguousarray(cc)
        maps.append(m)
    return maps


_NC_CACHE = {}


def run(inputs, L, DEPTH, ncores):
    key = (L, DEPTH)
    if key not in _NC_CACHE:
        _NC_CACHE[key] = build_program(L, DEPTH)
    nc = _NC_CACHE[key]
    maps = prepare_inputs(inputs, L, DEPTH, ncores)
    res = run_bass_kernel_spmd(nc, maps, core_ids=list(range(ncores)))
    return np.stack([np.ascontiguousarray(r["out"].T) for r in res.results], 0).astype(np.float32)


def kernel(**inputs):
    return run(inputs, 8192, 4, 8)
```

```python
import numpy as np
from contextlib import ExitStack
import concourse.bass as bass
import concourse.mybir as mybir
from concourse.bass_utils import run_bass_kernel_spmd

F32 = mybir.dt.float32
BF16 = mybir.dt.bfloat16
AF = mybir.ActivationFunctionType
ALU = mybir.AluOpType
AX = mybir.AxisListType

D = 1024
KD = 8
LC = 256
GRID_W = 64
EPS = 1e-6
FFN = 2816
NF = 22
INW = 3104
ENGS = ("pe", "act", "dve", "pool", "sp")


class Buf:
    __slots__ = ("name", "lw", "rd", "dsem", "persist")

    def __init__(self, name, persist=False):
        self.name = name
        self.lw = None
        self.rd = []
        self.dsem = None
        self.persist = persist


class Ins:
    __slots__ = ("eng", "fn", "deps", "signal", "count", "dma", "dsem", "dval")

    def __init__(self, eng, fn, dma):
        self.eng = eng
        self.fn = fn
        self.deps = []
        self.signal = False
        self.count = 0
        self.dma = dma
        self.dsem = None
        self.dval = 0


class Prog:
    def __init__(self):
        self.q = {e: [] for e in ENGS}
        self.dma_cnt = []
        self.dma_last = []
        self.dma_persist = []
        self.last = {e: None for e in ENGS}
        self.free_slots = {"sp": [], "pool": [], "act": []}
        self.pe_fence = False
        self.phase_slots = []

    def _dsem_for(self, buf, eng):
        if buf.dsem is None:
            if (not buf.persist) and self.free_slots[eng]:
                buf.dsem = self.free_slots[eng].pop()
            else:
                buf.dsem = len(self.dma_cnt)
                self.dma_cnt.append(0)
                self.dma_last.append(None)
                self.dma_persist.append(buf.persist)
            if not buf.persist:
                self.phase_slots.append((eng, buf.dsem))
        return buf.dsem

    def emit(self, eng, fn, reads=(), writes=(), dma_key=None):
        ins = Ins(eng, fn, dma_key is not None)
        deps = []
        raw = set()
        for b in reads:
            if b.lw is not None:
                deps.append(b.lw)
                raw.add(id(b.lw))
        for b in writes:
            if b.lw is not None:
                deps.append(b.lw)
            deps.extend(b.rd)
        if dma_key is None:
            deps = [d for d in deps if d.dma or d.eng != eng or eng != "pe"]
            if eng == "pe" and self.pe_fence and self.last["pe"] is not None:
                deps.append(self.last["pe"])
                self.pe_fence = False

        if dma_key is not None:
            s = self._dsem_for(dma_key, eng)
            ins.dsem = s
            self.dma_cnt[s] += 16
            ins.dval = self.dma_cnt[s]
            if self.dma_last[s] is not None:
                deps.append(self.dma_last[s])
            self.dma_last[s] = ins
        seen = set()
        for d in deps:
            if d is ins or id(d) in seen:
                continue
            seen.add(id(d))
            if not d.dma:
                d.signal = True
            ins.deps.append(d)
        for b in reads:
            b.rd.append(ins)
        for b in writes:
            b.lw = ins
            b.rd = []
        self.q[eng].append(ins)
        if dma_key is None:
            self.last[eng] = ins
        return ins

    def barrier(self):
        deps = [self.last[e] for e in ENGS if self.last[e] is not None]
        deps += [d for d, p in zip(self.dma_last, self.dma_persist) if d is not None and not p]
        for e in ("act", "dve", "pool", "sp"):
            ins = Ins(e, None, False)
            for d in deps:
                if not d.dma:
                    d.signal = True
                ins.deps.append(d)
            self.q[e].append(ins)
        for e_, sl in self.phase_slots:
            self.free_slots[e_].append(sl)
        self.phase_slots = []

    def replay(self, nc, stack, block, final_waits=()):
        for e in ENGS:
            c = 0
            for ins in self.q[e]:
                if (not ins.dma) and ins.signal:
                    c += 1
                    ins.count = c
        esem = {e: stack.enter_context(nc.semaphore("es_" + e)) for e in ENGS}
        dsem = [stack.enter_context(nc.semaphore("ds_%d" % i)) for i in range(len(self.dma_cnt))]
        prog = self

        def run(engname, engobj):
            waited_e = {e: 0 for e in ENGS}
            waited_d = [0] * len(prog.dma_cnt)
            for ins in prog.q[engname]:
                for d in ins.deps:
                    if d.dma:
                        if waited_d[d.dsem] < d.dval:
                            engobj.wait_ge(dsem[d.dsem], d.dval)
                            waited_d[d.dsem] = d.dval
                    else:
                        if waited_e[d.eng] < d.count:
                            engobj.wait_ge(esem[d.eng], d.count)
                            waited_e[d.eng] = d.count
                if ins.fn is None:
                    continue
                r = ins.fn(engobj)
                if ins.dma:
                    r.then_inc(dsem[ins.dsem], 16)
                elif ins.signal:
                    r.then_inc(esem[ins.eng], 1)
            if engname == "sp":
                for d in final_waits:
                    engobj.wait_ge(dsem[d.dsem], d.dval)

        block.tensor(lambda t: run("pe", t))
        block.scalar(lambda t: run("act", t))
        block.vector(lambda t: run("dve", t))
        block.gpsimd(lambda t: run("pool", t))
        block.sync(lambda t: run("sp", t))


class Tl:
    __slots__ = ("t", "b")

    def __init__(self, t, name, persist=False):
        self.t = t
        self.b = Buf(name, persist)


class Rot:
    def __init__(self, items):
        self.items = items
        self.i = 0

    def next(self):
        r = self.items[self.i % len(self.items)]
        self.i += 1
        return r


def build_program(L, DEPTH, stop=None):
    import os
    stop = stop or os.environ.get('KSTOP')
    T = LC + L
    NCH = T // 128
    NB = L // 128
    nc = bass.Bass("TRN2", target_bir_lowering=False)
    P = Prog()

    def din(name, shape, dt=F32):
        return nc.dram_tensor(name, list(shape), dt, kind="ExternalInput").ap()

    def dscr(name, shape, dt):
        return nc.dram_tensor(name, list(shape), dt).ap()

    xin = din("xin", [D, T])
    cT = din("cT", [128, 16])
    wada = din("wada", [DEPTH * D, 6 * D])
    bada = din("bada", [128, DEPTH * 96])
    gmix = din("gmix", [128, DEPTH * 16])
    gffn = din("gffn", [128, DEPTH * 16])
    gfin = din("gfin", [128, 8])
    win = din("win", [DEPTH * D, INW])
    wg = din("wg", [33, DEPTH * 384])
    gng = din("gng", [128, DEPTH * 384])
    scw = din("scw", [128, DEPTH * 8])
    nab = din("nab", [DEPTH * 5 * 128, 6 * 5 * 128])
    wout = din("wout", [DEPTH * D, D])
    wup = din("wup", [DEPTH * D, 2 * FFN])
    fcw = din("fcw", [128, DEPTH * 176])
    wdown = din("wdown", [DEPTH * FFN, D])
    rope = din("rope", [T, 576])
    ctri = din("ctri", [128, 512])
    cmask = din("cmask", [128, 2 * 768])
    cmisc = din("cmisc", [128, 128 + 128 + 384])
    out = nc.dram_tensor("out", [D, L], F32, kind="ExternalOutput").ap()

    XA = dscr("XA", [D, T], F32)
    XB = dscr("XB", [D, T], F32)
    scu = dscr("scu", [768, T], F32)
    naq = dscr("naq", [384, T], BF16)
    nak = dscr("nak", [384, T], BF16)
    nav = dscr("nav", [T, 390], BF16)
    gbT = dscr("gbT", [NCH * 96, 512], BF16)
    gkv = dscr("gkv", [T, 576], BF16)
    gof = dscr("gof", [T, 384], F32)
    gr = dscr("gr", [T, 384], F32)
    mixT = dscr("mixT", [D, T], BF16)

    with ExitStack() as top:
        E = top.enter_context

        uid = [0]

        def sb(stack, name, shape, dt, persist=False):
            uid[0] += 1
            nm = "%s_%d" % (name, uid[0])
            return Tl(stack.enter_context(nc.sbuf_tensor(nm, list(shape), dt)), nm, persist)

        psum = E(nc.psum_tensor("psum", [128, 4096], F32))
        psum_bf = psum.bitcast(BF16)
        PB = [Buf("bank%d" % i, True) for i in range(8)]

        def ps(bank, c0, c1, p0=0, p1=128):
            return psum[p0:p1, bank * 512 + c0: bank * 512 + c1]

        def psb(bank, c0, c1, p0=0, p1=128):
            return psum_bf[p0:p1, bank * 1024 + c0: bank * 1024 + c1]

        arena = sb(top, "arena", [128, 8 * 2 * FFN], BF16, True)
        WOUT0 = 8 * INW
        cst = sb(top, "cst", [128, 8], F32, True)
        ones_bf = sb(top, "ones_bf", [128, 128], BF16, True)
        ident = sb(top, "ident", [128, 128], BF16, True)
        bmask = sb(top, "bmask", [128, 384], BF16, True)
        ones_f = sb(top, "ones_f", [128, 2], F32, True)
        mod = sb(top, "mod", [128, DEPTH * 96], F32, True)
        A1 = sb(top, "A1", [128, DEPTH * 16], F32, True)
        A2 = sb(top, "A2", [128, DEPTH * 16], F32, True)
        gmx = sb(top, "gmx", [128, DEPTH * 16], F32, True)
        gff = sb(top, "gff", [128, DEPTH * 16], F32, True)
        gfn = sb(top, "gfn", [128, 8], F32, True)
        scwt = sb(top, "scwt", [128, DEPTH * 8], F32, True)
        wgt = sb(top, "wgt", [33, DEPTH * 384], BF16, True)
        decb = sb(top, "decb", [96, NCH * 2], F32, True)

        def modcol(l, j, s):
            c = l * 96 + j * 2 + s
            return mod.t[:, c:c + 1]

        def acol(A, l, k, s):
            c = l * 16 + k * 2 + s
            return A.t[:, c:c + 1]

        def dma(eng, out_ap, in_ap, key, reads=(), writes=()):
            return P.emit(eng, lambda e: e.dma_start(out=out_ap, in_=in_ap), reads=reads, writes=writes, dma_key=key)

        def mm(out_ap, lhsT, rhs, start, stop, reads, writes, fence=False):
            if fence:
                P.pe_fence = True
            r = P.emit("pe", lambda e: e.matmul(out_ap, lhsT=lhsT, rhs=rhs, start=start, stop=stop),
                       reads=reads, writes=writes)
            if fence:
                P.pe_fence = True
            return r

        def tr(out_ap, in_ap, reads, writes):
            return P.emit("pe", lambda e: e.matmul(out_ap, lhsT=in_ap, rhs=ident.t[:], start=True, stop=True),
                          reads=list(reads) + [ident.b], writes=writes)

        def act(out_ap, in_ap, func, reads, writes, bias=None, scale=None):
            kw = {}
            if bias is not None:
                kw["bias"] = bias
            if scale is not None:
                kw["scale"] = scale
            return P.emit("act", lambda e: e.activation(out=out_ap, in_=in_ap, func=func, **kw), reads=reads,
                          writes=writes)

        def tt(eng, out_ap, a, b, op, reads, writes):
            return P.emit(eng, lambda e: e.tensor_tensor(out=out_ap, in0=a, in1=b, op=op), reads=reads, writes=writes)

        def ts(eng, out_ap, a, s1, op0, reads, writes, s2=None, op1=None):
            if op1 is None:
                return P.emit(eng, lambda e: e.tensor_scalar(out=out_ap, in0=a, scalar1=s1, scalar2=None, op0=op0),
                              reads=reads, writes=writes)
            return P.emit(eng, lambda e: e.tensor_scalar(out=out_ap, in0=a, scalar1=s1, scalar2=s2, op0=op0, op1=op1),
                          reads=reads, writes=writes)

        def stt(out_ap, a, s, b, op0, op1, reads, writes):
            return P.emit("dve", lambda e: e.scalar_tensor_tensor(out=out_ap, in0=a, scalar=s, in1=b, op0=op0, op1=op1),
                          reads=reads, writes=writes)

        def cp(eng, out_ap, in_ap, reads, writes):
            if eng == "act":
                return P.emit("act", lambda e: e.copy(out=out_ap, in_=in_ap), reads=reads, writes=writes)
            return P.emit(eng, lambda e: e.tensor_copy(out=out_ap, in_=in_ap), reads=reads, writes=writes)

        def mset(eng, ap, val, writes):
            return P.emit(eng, lambda e: e.memset(ap, val), writes=writes)

        def rows3(ap2d, p=128):
            return ap2d.rearrange("(m p) t -> p m t", p=p)

        mset("dve", cst.t[:, 0:1], EPS, [cst.b])
        mset("dve", cst.t[:, 1:2], float(np.log(32.0 ** -0.5)), [cst.b])
        mset("dve", cst.t[:, 2:3], 1.0, [cst.b])
        mset("dve", ones_f.t[:], 1.0, [ones_f.b])
        dma("pool", ones_bf.t[:], cmisc[:, 0:128], ones_bf.b, writes=[ones_bf.b])
        dma("pool", ident.t[:], cmisc[:, 128:256], ident.b, writes=[ident.b])
        dma("pool", bmask.t[:], cmisc[:, 256:640], bmask.b, writes=[bmask.b])
        dma("pool", wgt.t[:], wg, wgt.b, writes=[wgt.b])
        dma("sp", gmx.t[:], gmix, gmx.b, writes=[gmx.b])
        dma("sp", gff.t[:], gffn, gff.b, writes=[gff.b])
        dma("sp", gfn.t[:], gfin, gfn.b, writes=[gfn.b])
        dma("sp", scwt.t[:], scw, scwt.b, writes=[scwt.b])

        def load_win(l):
            for k in range(KD):
                dma("pool", arena.t[:, k * INW:(k + 1) * INW], win[l * D + k * 128: l * D + (k + 1) * 128, :],
                    arena.b, writes=[arena.b])

        def load_wout(l):
            for k in range(KD):
                dma("pool", arena.t[:, WOUT0 + k * D: WOUT0 + (k + 1) * D],
                    wout[l * D + k * 128: l * D + (k + 1) * 128, :], arena.b, writes=[arena.b])

        def load_wup(l):
            for k in range(KD):
                dma("pool", arena.t[:, k * 2 * FFN:(k + 1) * 2 * FFN],
                    wup[l * D + k * 128: l * D + (k + 1) * 128, :], arena.b, writes=[arena.b])

        def load_wdown(l, wdn):
            for c in range(NF):
                dma("pool", wdn.t[:, c * D:(c + 1) * D], wdown[l * FFN + c * 128: l * FFN + (c + 1) * 128, :], wdn.b,
                    writes=[wdn.b])

        with ExitStack() as ph:
            cin = sb(ph, "cin", [128, 16], F32)
            csl = sb(ph, "csl", [128, 16], F32)
            bad = sb(ph, "bad", [128, DEPTH * 96], F32)
            wst = [sb(ph, "wst%d" % i, [128, 8 * 768], F32) for i in range(2)]
            dma("sp", cin.t[:], cT, cin.b, writes=[cin.b])
            dma("sp", bad.t[:], bada, bad.b, writes=[bad.b])
            act(csl.t[:], cin.t[:], AF.Silu, [cin.b], [csl.b])
            pi = 0
            for l in range(DEPTH):
                bank = l % 2
                for piece in range(8):
                    w = wst[pi % 2]
                    pi += 1
                    for k in range(KD):
                        dma("sp", w.t[:, k * 768:(k + 1) * 768],
                            wada[l * D + k * 128: l * D + (k + 1) * 128, piece * 768:(piece + 1) * 768], w.b,
                            writes=[w.b])
                    for jj in range(6):
                        j = piece * 6 + jj
                        for k in range(KD):
                            mm(ps(bank, j * 2, j * 2 + 2), w.t[:, k * 768 + jj * 128: k * 768 + (jj + 1) * 128],
                               csl.t[:, k * 2:(k + 1) * 2], k == 0, k == KD - 1, [w.b, csl.b], [PB[bank]])
                tt("dve", mod.t[:, l * 96:(l + 1) * 96], ps(bank, 0, 96), bad.t[:, l * 96:(l + 1) * 96], ALU.add,
                   [PB[bank], bad.b], [mod.b])
                stt(A1.t[:, l * 16:(l + 1) * 16], mod.t[:, l * 96 + 16: l * 96 + 32], 1.0, gmx.t[:, l * 16:(l + 1) * 16],
                    ALU.add, ALU.mult, [mod.b, gmx.b], [A1.b])
                stt(A2.t[:, l * 16:(l + 1) * 16], mod.t[:, l * 96 + 64: l * 96 + 80], 1.0, gff.t[:, l * 16:(l + 1) * 16],
                    ALU.add, ALU.mult, [mod.b, gff.b], [A2.b])
            P.pe_fence = True
            P.barrier()

        def norm_mod(getx, n, A, l, shj, s, hT, sqs, sd, rstd, tmps, ssbank):
            for k in range(KD):
                xb, xap = getx(k)
                sq = sqs.next()
                act(sq.t[:, :n], xap, AF.Square, [xb.b], [sq.b])
                mm(ps(ssbank, 0, n), ones_bf.t[:], sq.t[:, :n], k == 0, k == KD - 1, [ones_bf.b, sq.b], [PB[ssbank]])
            act(sd.t[:, :n], ps(ssbank, 0, n), AF.Sqrt, [PB[ssbank], cst.b], [sd.b], bias=cst.t[:, 0:1], scale=1.0 / D)
            P.emit("dve", lambda e: e.reciprocal(out=rstd.t[:, :n], in_=sd.t[:, :n]), reads=[sd.b], writes=[rstd.b])
            for k in range(KD):
                xb, xap = getx(k)
                tm = tmps.next()
                stt(tm.t[:, :n], xap, acol(A, l, k, s), rstd.t[:, :n], ALU.mult, ALU.mult, [xb.b, A.b, rstd.b], [tm.b])
                act(hT.t[:, k, :n], tm.t[:, :n], AF.Identity, [tm.b, mod.b], [hT.b], bias=modcol(l, shj + k, s), scale=1.0)

        def xloader(X, lo, n, xks):
            def getx(k):
                xk = xks.next()
                dma("sp", xk.t[:, :n], X[k * 128:(k + 1) * 128, lo:lo + n], xk.b, writes=[xk.b])
                return xk, xk.t[:, :n]
            return getx

        tiles512 = [(0, LC, 1)] + [(LC + i * 512, 512, 0) for i in range(L // 512)]

        load_win(0)
        load_wout(0)
        for l in range(DEPTH):
            X0 = xin if l == 0 else XA

            if stop == 'PRO':
                break
            with ExitStack() as ph:
                xks = Rot([sb(ph, "xk%d" % i, [128, 512], F32) for i in range(3)])
                tri = sb(ph, "tri", [128, 512], F32)
                msk = sb(ph, "msk", [128, 384], BF16)
                dma("sp", tri.t[:], ctri, tri.b, writes=[tri.b])
                dma("pool", msk.t[:], cmask[:, 0:384], msk.b, writes=[msk.b])
                sqs = Rot([sb(ph, "sq%d" % i, [128, 512], BF16) for i in range(3)])
                sd = sb(ph, "sd", [128, 512], F32)
                rstd = sb(ph, "rstd", [128, 512], F32)
                tmps = Rot([sb(ph, "tm%d" % i, [128, 512], F32) for i in range(2)])
                hTs = [sb(ph, "hT%d" % i, [128, 8, 512], BF16) for i in range(2)]
                scos = Rot([sb(ph, "sco%d" % i, [128, 512], F32) for i in range(3)])
                nqks = Rot([sb(ph, "nqk%d" % i, [128, 512], BF16) for i in range(3)])
                glrs = [sb(ph, "glr%d" % i, [33, 512], BF16) for i in range(2)]

                def pool_(name, shape, dt, depth):
                    return [sb(ph, "%s%d" % (name, i), shape, dt) for i in range(depth)]
                rps = pool_("rp", [128, 576], F32, 2)
                qks = pool_("qk", [128, 384], F32, 2)
                kvs = pool_("kv", [128, 576], BF16, 6)
                rrs = pool_("rr", [128, 384], F32, 2)
                nvs = pool_("nv", [128, 390], BF16, 2)
                ees = pool_("ee", [128, 384], F32, 2)
                lls = pool_("ll", [128, 384], F32, 2)
                fac = sb(ph, "fac", [128, 6, 192], F32)
                tcs = sb(ph, "tcs", [128, 384], F32)
                m12 = sb(ph, "m12", [128, 2, 192], F32)
                rqs = pool_("rq", [128, 384], F32, 2)
                gps = pool_("gp", [128, 4, 192], BF16, 2)
                gTfs = pool_("gTf", [96, 512], BF16, 3)
                gTbs = pool_("gTb", [96, 512], BF16, 2)
                atms = pool_("atm", [128, 6, 128], BF16, 2)
                kefs = pool_("kef", [128, 192], BF16, 4)
                decs = pool_("dec", [96, 4], F32, 4)
                ofs = pool_("of", [128, 384], F32, 2)
                um = sb(ph, "um", [96, 384], F32)
                Sf = sb(ph, "Sf", [96, 384], F32)
                Sfb = sb(ph, "Sfb", [96, 384], BF16)
                for g_ in glrs:
                    mset("pool", g_.t[32:33, :], 1.0, [g_.b])
                for n_ in nvs:
                    mset("pool", n_.t[:], 1.0, [n_.b])
                mset("dve", Sf.t[:], 0.0, [Sf.b])
                mset("dve", Sfb.t[:], 0.0, [Sfb.b])
                fmb = Rot([1, 2])
                tmb = Rot([3, 4])
                evr = Rot(["act", "dve"])

                def tile_norm(ti):
                    t0, n, s = tiles512[ti]
                    norm_mod(xloader(X0, t0, n, xks), n, A1, l, 0, s, hTs[ti % 2], sqs, sd, rstd, tmps, 0)

                def tile_fm(ti):
                    t0, n, s = tiles512[ti]
                    hT = hTs[ti % 2]
                    glr = glrs[ti % 2]
                    for m in range(12):
                        bk = fmb.next()
                        c0 = 1536 + m * 128
                        for k in range(KD):
                            mm(ps(bk, 0, n), arena.t[:, k * INW + c0: k * INW + c0 + 128], hT.t[:, k, :n], k == 0,
                               k == KD - 1, [arena.b, hT.b], [PB[bk]])
                        if m < 6:
                            sco = scos.next()
                            cp(evr.next(), sco.t[:, :n], ps(bk, 0, n), [PB[bk]], [sco.b])
                            dma("sp", scu[m * 128:(m + 1) * 128, t0:t0 + n], sco.t[:, :n], sco.b, reads=[sco.b])
                        else:
                            nqk = nqks.next()
                            cp(evr.next(), nqk.t[:, :n], ps(bk, 0, n), [PB[bk]], [nqk.b])
                            dst = naq if m < 9 else nak
                            mm_ = (m - 6) % 3
                            dma("sp", dst[mm_ * 128:(mm_ + 1) * 128, t0:t0 + n], nqk.t[:, :n], nqk.b, reads=[nqk.b])
                    bk = fmb.next()
                    for k in range(KD):
                        mm(ps(bk, 0, n, 0, 32), arena.t[:, k * INW + 3072: k * INW + 3104], hT.t[:, k, :n], k == 0,
                           k == KD - 1, [arena.b, hT.b], [PB[bk]])
                    cp(evr.next(), glr.t[0:32, :n], ps(bk, 0, n, 0, 32), [PB[bk]], [glr.b])

                chunks = []
                for ti, (t0, n, s) in enumerate(tiles512):
                    for c in range(n // 128):
                        chunks.append((ti, c))

                def st0(i):
                    ti, c = chunks[i]
                    t0 = tiles512[ti][0]
                    tk0 = t0 + c * 128
                    cs = slice(c * 128, (c + 1) * 128)
                    hT = hTs[ti % 2]
                    qk, kv, rr, nv, rp = qks[i % 2], kvs[i % 6], rrs[i % 2], nvs[i % 2], rps[i % 2]
                    dma("sp", rp.t[:], rope[tk0:tk0 + 128, :], rp.b, writes=[rp.b])
                    for g in range(4):
                        bk = tmb.next()
                        for k in range(KD):
                            mm(ps(bk, 0, 384), hT.t[:, k, cs], arena.t[:, k * INW + g * 384: k * INW + (g + 1) * 384],
                               k == 0, k == KD - 1, [arena.b, hT.b], [PB[bk]])
                        if g == 0:
                            cp("act", qk.t[:], ps(bk, 0, 384), [PB[bk]], [qk.b])
                        elif g == 1:
                            cp("dve", kv.t[:, 0:384], ps(bk, 0, 384), [PB[bk]], [kv.b])
                        elif g == 2:
                            cp("act", rr.t[:], ps(bk, 0, 384), [PB[bk]], [rr.b])
                        else:
                            cp("dve", nv.t[:].rearrange("p (h e) -> p h e", h=6)[:, :, 0:64],
                               ps(bk, 0, 384).rearrange("p (h e) -> p h e", h=6), [PB[bk]], [nv.b])
                    dma("sp", gr[tk0:tk0 + 128, :], rr.t[:], rr.b, reads=[rr.b])
                    dma("sp", nav[tk0:tk0 + 128, :], nv.t[:], nv.b, reads=[nv.b])

                def st1(i):
                    ti, c = chunks[i]
                    cs = slice(c * 128, (c + 1) * 128)
                    glr = glrs[ti % 2]
                    qk, rp, rq, ee, ll = qks[i % 2], rps[i % 2], rqs[i % 2], ees[i % 2], lls[i % 2]
                    mm(ps(5, 0, 384), glr.t[0:33, cs], wgt.t[:, l * 384:(l + 1) * 384], True, True, [glr.b, wgt.b],
                       [PB[5]])
                    act(ee.t[:], ps(5, 0, 384), AF.Exp, [PB[5]], [ee.b], scale=-1.0)
                    act(ll.t[:], ee.t[:], AF.Ln, [ee.b, cst.b], [ll.b], bias=cst.t[:, 2:3], scale=1.0)
                    q4 = qk.t[:].rearrange("p (h t e) -> p h t e", h=12, t=2)
                    t4 = tcs.t[:].rearrange("p (h t e) -> p h t e", h=12, t=2)
                    r4 = rq.t[:].rearrange("p (h t e) -> p h t e", h=12, t=2)
                    sn = rp.t[:, 384:576].rearrange("p (h e) -> p h e", h=12)
                    tt("pool", tcs.t[:], qk.t[:], rp.t[:, 0:384], ALU.mult, [qk.b, rp.b], [tcs.b])
                    tt("pool", m12.t[:, 0, :].rearrange("p (h e) -> p h e", h=12), q4[:, :, 1, :], sn, ALU.mult,
                       [qk.b, rp.b], [m12.b])
                    tt("pool", m12.t[:, 1, :].rearrange("p (h e) -> p h e", h=12), q4[:, :, 0, :], sn, ALU.mult,
                       [qk.b, rp.b], [m12.b])
                    tt("pool", r4[:, :, 0, :], t4[:, :, 0, :], m12.t[:, 0, :].rearrange("p (h e) -> p h e", h=12),
                       ALU.subtract, [tcs.b, m12.b], [rq.b])
                    tt("pool", r4[:, :, 1, :], t4[:, :, 1, :], m12.t[:, 1, :].rearrange("p (h e) -> p h e", h=12),
                       ALU.add, [tcs.b, m12.b], [rq.b])

                def st2(i):
                    ti, c = chunks[i]
                    ch = (tiles512[ti][0] // 128) + c
                    tk0 = ch * 128
                    ll, rq, gp, kv, kef, dec = lls[i % 2], rqs[i % 2], gps[i % 2], kvs[i % 6], kefs[i % 4], decs[i % 4]
                    P.pe_fence = True
                    for d_ in range(2):
                        bk = 6 + d_
                        for jj in range(2):
                            mm(ps(bk, jj * 192, (jj + 1) * 192), tri.t[:, (d_ * 2 + jj) * 128:(d_ * 2 + jj + 1) * 128],
                               ll.t[:, d_ * 192:(d_ + 1) * 192], True, True, [tri.b, ll.b], [PB[bk]])
                    for d_ in range(2):
                        for g in range(2):
                            i_ = d_ * 2 + g
                            mm(ps(5, 400 + i_, 401 + i_, 0, 96), ll.t[:, d_ * 192 + g * 96: d_ * 192 + (g + 1) * 96],
                               ones_f.t[:, 0:1], True, True, [ll.b, ones_f.b], [PB[5]])
                    P.pe_fence = True
                    act(dec.t[:], ps(5, 400, 404, 0, 96), AF.Exp, [PB[5]], [dec.b], scale=-1.0 / 16)
                    cp("dve", decb.t[:, ch * 2:(ch + 1) * 2], dec.t[:, 2:4], [dec.b], [decb.b])
                    for d_ in range(2):
                        bk = 6 + d_
                        act(fac.t[:, d_ * 3 + 0, :], ps(bk, 0, 192), AF.Exp, [PB[bk], cst.b], [fac.b],
                            bias=cst.t[:, 1:2], scale=-1.0 / 16)
                        act(fac.t[:, d_ * 3 + 1, :], ps(bk, 0, 192), AF.Exp, [PB[bk]], [fac.b], scale=1.0 / 16)
                        act(fac.t[:, d_ * 3 + 2, :], ps(bk, 192, 384), AF.Exp, [PB[bk]], [fac.b], scale=-1.0 / 16)
                    tt("dve", gp.t[:, 0, :], rq.t[:, 0:192], fac.t[:, 0, :], ALU.mult, [rq.b, fac.b], [gp.b])
                    tt("dve", gp.t[:, 1, :], rq.t[:, 192:384], fac.t[:, 1, :], ALU.mult, [rq.b, fac.b], [gp.b])
                    tt("dve", gp.t[:, 2, :], rq.t[:, 0:192], fac.t[:, 3, :], ALU.mult, [rq.b, fac.b], [gp.b])
                    tt("dve", gp.t[:, 3, :], rq.t[:, 192:384], fac.t[:, 4, :], ALU.mult, [rq.b, fac.b], [gp.b])
                    tt("dve", kv.t[:, 384:576], rq.t[:, 192:384], fac.t[:, 5, :], ALU.mult, [rq.b, fac.b], [kv.b])
                    tt("pool", kef.t[:], rq.t[:, 192:384], fac.t[:, 2, :], ALU.mult, [rq.b, fac.b], [kef.b])
                    dma("sp", gkv[tk0:tk0 + 128, :], kv.t[:], kv.b, reads=[kv.b])

                def st3(i):
                    ti, c = chunks[i]
                    ch = (tiles512[ti][0] // 128) + c
                    gp, gTf, gTb = gps[i % 2], gTfs[i % 3], gTbs[i % 2]
                    for j in range(4):
                        for g in range(2):
                            tb_ = 1 if j < 2 else 2
                            tc_ = ((j % 2) * 2 + g) * 128
                            tr(ps(tb_, tc_, tc_ + 128, 0, 96), gp.t[:, j, g * 96:(g + 1) * 96], [gp.b], [PB[tb_]])
                    cp("act", gTf.t[:], ps(1, 0, 512, 0, 96), [PB[1]], [gTf.b])
                    cp("dve", gTb.t[:], ps(2, 0, 512, 0, 96), [PB[2]], [gTb.b])
                    dma("sp", gbT[ch * 96:(ch + 1) * 96, :], gTb.t[:], gTb.b, reads=[gTb.b])

                def st4(i):
                    gTf, atm = gTfs[i % 3], atms[i % 2]
                    for h in range(6):
                        g, hp = h // 3, h % 3
                        bk = 6 + g
                        mm(ps(bk, hp * 128, (hp + 1) * 128), gTf.t[32 * hp:32 * hp + 32, (2 + g) * 128:(3 + g) * 128],
                           gTf.t[32 * hp:32 * hp + 32, g * 128:(g + 1) * 128], True, True, [gTf.b], [PB[bk]], fence=True)
                    for g in range(2):
                        tt("dve", atm.t[:, 3 * g:3 * g + 3, :], ps(6 + g, 0, 384).rearrange("p (h i) -> p h i", h=3),
                           msk.t[:, 0:384].rearrange("p (h i) -> p h i", h=3), ALU.mult, [PB[6 + g], msk.b], [atm.b])

                def st5(i):
                    ti, c = chunks[i]
                    ch = (tiles512[ti][0] // 128) + c
                    tk0 = ch * 128
                    gTf, atm, kv, kef, dec, of = gTfs[i % 3], atms[i % 2], kvs[i % 6], kefs[i % 4], decs[i % 4], ofs[i % 2]
                    bo = tmb.next()
                    for h in range(6):
                        g, hp = h // 3, h % 3
                        mm(ps(bo, h * 64, (h + 1) * 64), atm.t[:, h, :], kv.t[:, h * 64:(h + 1) * 64], True, False,
                           [atm.b, kv.b], [PB[bo]])
                        mm(ps(bo, h * 64, (h + 1) * 64), gTf.t[32 * hp:32 * hp + 32, g * 128:(g + 1) * 128],
                           Sfb.t[32 * hp:32 * hp + 32, g * 192 + hp * 64: g * 192 + (hp + 1) * 64], False, True,
                           [gTf.b, Sfb.b], [PB[bo]])
                    cp("act", of.t[:], ps(bo, 0, 384), [PB[bo]], [of.b])
                    dma("sp", gof[tk0:tk0 + 128, :], of.t[:], of.b, reads=[of.b])
                    for g in range(2):
                        mm(ps(0, g * 192, (g + 1) * 192, 0, 96), kef.t[:, g * 96:(g + 1) * 96],
                           kv.t[:, g * 192:(g + 1) * 192], True, True, [kef.b, kv.b], [PB[0]])
                    tt("dve", um.t[:], ps(0, 0, 384, 0, 96), bmask.t[0:96, :], ALU.mult, [PB[0], bmask.b], [um.b])
                    for g in range(2):
                        stt(Sf.t[:, g * 192:(g + 1) * 192], Sf.t[:, g * 192:(g + 1) * 192], dec.t[:, g:g + 1],
                            um.t[:, g * 192:(g + 1) * 192], ALU.mult, ALU.add, [Sf.b, dec.b, um.b], [Sf.b])
                    cp("dve", Sfb.t[:], Sf.t[:], [Sf.b], [Sfb.b])

                stages = [st0, st1, st2, st3, st4, st5]
                if stop and stop.startswith('S1s'):
                    stages = stages[:int(stop[3:])]
                nchunks = len(chunks)
                tile_norm(0)
                for step in range(nchunks + len(stages) - 1):
                    if step < nchunks:
                        ti, c = chunks[step]
                        if c == 0:
                            tile_fm(ti)
                        if c == min(1, tiles512[ti][1] // 128 - 1) and ti + 1 < len(tiles512):
                            tile_norm(ti + 1)
                    for si, st in enumerate(stages):
                        i = step - si
                        if 0 <= i < nchunks:
                            st(i)
                P.barrier()

            if stop and stop.startswith('S1'):
                break
            with ExitStack() as ph:
                def pool_(name, shape, dt, depth):
                    return [sb(ph, "%s%d" % (name, i), shape, dt) for i in range(depth)]
                gTs = pool_("gT", [96, 512], BF16, 3)
                kvs = pool_("kvb", [128, 576], BF16, 3)
                ofs = pool_("ofb", [128, 384], F32, 3)
                rrs = pool_("rrb", [128, 384], F32, 4)
                atms = pool_("atmb", [128, 6, 128], BF16, 2)
                oos = pool_("oo", [128, 384], F32, 2)
                o2 = sb(ph, "o2", [128, 384], F32)
                ssq = sb(ph, "ssq", [128, 6], F32)
                rs = sb(ph, "rs", [128, 6], F32)
                on = sb(ph, "on", [128, 384], F32)
                sr = sb(ph, "sr", [128, 384], F32)
                gl = sb(ph, "gl", [128, 384], BF16)
                glTs = pool_("glT", [128, 3, 128], BF16, 2)
                um = sb(ph, "umb", [96, 384], F32)
                Sb = sb(ph, "Sb", [96, 384], F32)
                Sbb = sb(ph, "Sbb", [96, 384], BF16)
                msk = sb(ph, "mskb", [128, 384], BF16)
                gngt = sb(ph, "gngt", [128, 384], F32)
                dma("pool", msk.t[:], cmask[:, 768:1152], msk.b, writes=[msk.b])
                dma("sp", gngt.t[:], gng[:, l * 384:(l + 1) * 384], gngt.b, writes=[gngt.b])
                mset("dve", Sb.t[:], 0.0, [Sb.b])
                mset("dve", Sbb.t[:], 0.0, [Sbb.b])
                order = [1, 0] + list(range(NCH - 1, 1, -1))

                def g0(i):
                    ch = order[i]
                    tk0 = ch * 128
                    gT, kv, of, rr = gTs[i % 3], kvs[i % 3], ofs[i % 3], rrs[i % 4]
                    dma("sp", gT.t[:], gbT[ch * 96:(ch + 1) * 96, :], gT.b, writes=[gT.b])
                    dma("sp", kv.t[:], gkv[tk0:tk0 + 128, :], kv.b, writes=[kv.b])
                    dma("sp", of.t[:], gof[tk0:tk0 + 128, :], of.b, writes=[of.b])
                    dma("sp", rr.t[:], gr[tk0:tk0 + 128, :], rr.b, writes=[rr.b])

                def g1(i):
                    gT, atm = gTs[i % 3], atms[i % 2]
                    bA0 = 2 * (i % 2)
                    for h in range(6):
                        g, hp = h // 3, h % 3
                        bk = bA0 + g
                        mm(ps(bk, hp * 128, (hp + 1) * 128), gT.t[32 * hp:32 * hp + 32, (2 + g) * 128:(3 + g) * 128],
                           gT.t[32 * hp:32 * hp + 32, g * 128:(g + 1) * 128], True, True, [gT.b], [PB[bk]], fence=True)
                    for g in range(2):
                        tt("dve", atm.t[:, 3 * g:3 * g + 3, :], ps(bA0 + g, 0, 384).rearrange("p (h i) -> p h i", h=3),
                           msk.t[:, 0:384].rearrange("p (h i) -> p h i", h=3), ALU.mult, [PB[bA0 + g], msk.b], [atm.b])

                def g2(i):
                    ch = order[i]
                    gT, kv, of, atm, oo = gTs[i % 3], kvs[i % 3], ofs[i % 3], atms[i % 2], oos[i % 2]
                    bO, bU = 4 + (i % 2), 6
                    for h in range(6):
                        g, hp = h // 3, h % 3
                        mm(ps(bO, h * 64, (h + 1) * 64), atm.t[:, h, :], kv.t[:, h * 64:(h + 1) * 64], True, False,
                           [atm.b, kv.b], [PB[bO]])
                        mm(ps(bO, h * 64, (h + 1) * 64), gT.t[32 * hp:32 * hp + 32, g * 128:(g + 1) * 128],
                           Sbb.t[32 * hp:32 * hp + 32, g * 192 + hp * 64: g * 192 + (hp + 1) * 64], False, True,
                           [gT.b, Sbb.b], [PB[bO]])
                    for g in range(2):
                        mm(ps(bU, g * 192, (g + 1) * 192, 0, 96), kv.t[:, 384 + g * 96: 384 + (g + 1) * 96],
                           kv.t[:, g * 192:(g + 1) * 192], True, True, [kv.b], [PB[bU]])
                    tt("dve", um.t[:], ps(bU, 0, 384, 0, 96), bmask.t[0:96, :], ALU.mult, [PB[bU], bmask.b], [um.b])
                    for g in range(2):
                        stt(Sb.t[:, g * 192:(g + 1) * 192], Sb.t[:, g * 192:(g + 1) * 192],
                            decb.t[:, ch * 2 + g: ch * 2 + g + 1], um.t[:, g * 192:(g + 1) * 192], ALU.mult, ALU.add,
                            [Sb.b, decb.b, um.b], [Sb.b])
                    cp("dve", Sbb.t[:], Sb.t[:], [Sb.b], [Sbb.b])
                    tt("dve", oo.t[:], ps(bO, 0, 384), of.t[:], ALU.add, [PB[bO], of.b], [oo.b])

                def g3(i):
                    ch = order[i]
                    tk0 = ch * 128
                    oo, rr, glT = oos[i % 2], rrs[i % 4], glTs[i % 2]
                    tt("pool", o2.t[:], oo.t[:], oo.t[:], ALU.mult, [oo.b], [o2.b])
                    P.emit("dve", lambda e: e.tensor_reduce(
                        out=ssq.t[:], in_=o2.t[:].rearrange("p (h e) -> p h e", h=6), axis=AX.X, op=ALU.add),
                        reads=[o2.b], writes=[ssq.b])
                    act(rs.t[:], ssq.t[:], AF.Sqrt, [ssq.b, cst.b], [rs.b], bias=cst.t[:, 0:1], scale=1.0 / 64)
                    P.emit("dve", lambda e: e.reciprocal(out=rs.t[:], in_=rs.t[:]), reads=[rs.b], writes=[rs.b])
                    act(sr.t[:], rr.t[:], AF.Silu, [rr.b], [sr.b])
                    for h in range(6):
                        stt(on.t[:, h * 64:(h + 1) * 64], oo.t[:, h * 64:(h + 1) * 64], rs.t[:, h:h + 1],
                            gngt.t[:, h * 64:(h + 1) * 64], ALU.mult, ALU.mult, [oo.b, rs.b, gngt.b], [on.b])
                    tt("pool", gl.t[:], on.t[:], sr.t[:], ALU.mult, [on.b, sr.b], [gl.b])
                    for m in range(3):
                        tr(ps(7, m * 128, (m + 1) * 128), gl.t[:, m * 128:(m + 1) * 128], [gl.b], [PB[7]])
                    cp("act", glT.t[:], ps(7, 0, 384).rearrange("p (m t) -> p m t", m=3), [PB[7]], [glT.b])
                    dma("sp", rows3(mixT[0:384, tk0:tk0 + 128]), glT.t[:], glT.b, reads=[glT.b])

                stages = [g0, g1, g2, g3]
                for step in range(NCH + len(stages) - 1):
                    for si, st in enumerate(stages):
                        i = step - si
                        if 0 <= i < NCH:
                            st(i)
                P.barrier()

            if stop == 'GB':
                break
            with ExitStack() as ph:
                nb = sb(ph, "nb", [128, 6 * 5 * 128], F32)
                krs = [sb(ph, "kr%d" % i, [128, 3, 128], BF16) for i in range(8)]
                vrs = [sb(ph, "vr%d" % i, [128, 390], BF16) for i in range(8)]
                kc = sb(ph, "kc", [128, 3, 256], BF16)
                vc = sb(ph, "vc", [128, 2, 390], BF16)
                qts = Rot([sb(ph, "qt%d" % i, [128, 3, 128], BF16) for i in range(2)])
                stmps = Rot([sb(ph, "stmp%d" % i, [128, 640], F32) for i in range(2)])
                pts = Rot([sb(ph, "pt%d" % i, [128, 7, 128], BF16) for i in range(3)])
                rcs = [sb(ph, "rc%d" % i, [128, 6], F32) for i in range(2)]
                onts = [sb(ph, "ont%d" % i, [128, 384], BF16) for i in range(2)]
                naTs = Rot([sb(ph, "naT%d" % i, [128, 3, 128], BF16) for i in range(2)])
                dma("sp", kc.t[:], rows3(nak[:, 0:LC]), kc.b, writes=[kc.b])
                dma("sp", vc.t[:], nav[0:LC, :].rearrange("(c p) f -> p c f", p=128), vc.b, writes=[vc.b])
                loaded = {}
                cur_pat = [None]

                def ensure_chunk(cid):
                    slot = cid % 8
                    if loaded.get(slot) != cid:
                        tk = LC + cid * 128
                        dma("sp", krs[slot].t[:], rows3(nak[:, tk:tk + 128]), krs[slot].b, writes=[krs[slot].b])
                        dma("sp", vrs[slot].t[:], nav[tk:tk + 128, :], vrs[slot].b, writes=[vrs[slot].b])
                        loaded[slot] = cid
                    return slot

                def pat_of(a):
                    if a == 0:
                        return 1
                    if a == 1:
                        return 2
                    if a == NB - 2:
                        return 3
                    if a == NB - 1:
                        return 4
                    return 0

                blocks = [("c", 0), ("c", 1)] + [("l", a) for a in range(NB)]
                binfo = {}

                def blk_setup(bi):
                    kind, a = blocks[bi]
                    qt = qts.items[bi % 2]
                    if kind == "c":
                        tq = a * 128
                        slots = []
                    else:
                        tq = LC + a * 128
                        w0 = min(max(a - 2, 0), NB - 5)
                        slots = [ensure_chunk(w0 + c) for c in range(5)]
                        pat = pat_of(a)
                        if cur_pat[0] != pat:
                            r0 = (l * 5 + pat) * 128
                            dma("sp", nb.t[:], nab[r0:r0 + 128, :], nb.b, writes=[nb.b])
                            cur_pat[0] = pat
                    dma("sp", qt.t[:], rows3(naq[:, tq:tq + 128]), qt.b, writes=[qt.b])
                    binfo[bi] = (tq, slots, qt)

                def sa(g):
                    bi, h = divmod(g, 6)
                    if h == 0:
                        blk_setup(bi)
                    tq, slots, qt = binfo[bi]
                    m, pb = h // 2, (h % 2) * 64
                    b0 = 2 * (g % 2)
                    pt = pts.items[g % 3]
                    for c, sl in enumerate(slots):
                        bk, off = (b0, c * 128) if c < 4 else (b0 + 1, 0)
                        mm(ps(bk, off, off + 128), krs[sl].t[pb:pb + 64, m, :], qt.t[pb:pb + 64, m, :], True, True,
                           [krs[sl].b, qt.b], [PB[bk]])
                    for c in range(2):
                        mm(ps(b0 + 1, 128 + c * 128, 256 + c * 128), kc.t[pb:pb + 64, m, c * 128:(c + 1) * 128],
                           qt.t[pb:pb + 64, m, :], True, True, [kc.b, qt.b], [PB[b0 + 1]])
                    if slots:
                        stmp = stmps.items[g % 2]
                        nbh = nb.t[:, h * 640:(h + 1) * 640]
                        stt(stmp.t[:, 0:512], ps(b0, 0, 512), 0.125, nbh[:, 0:512], ALU.mult, ALU.add,
                            [PB[b0], nb.b], [stmp.b])
                        stt(stmp.t[:, 512:640], ps(b0 + 1, 0, 128), 0.125, nbh[:, 512:640], ALU.mult, ALU.add,
                            [PB[b0 + 1], nb.b], [stmp.b])
                        act(pt.t[:, 0:5, :], stmp.t[:].rearrange("p (c q) -> p c q", c=5), AF.Exp, [stmp.b], [pt.b])
                    act(pt.t[:, 5:7, :], ps(b0 + 1, 128, 384).rearrange("p (c q) -> p c q", c=2), AF.Exp, [PB[b0 + 1]],
                        [pt.b], scale=0.125)

                def sc(g):
                    bi, h = divmod(g, 6)
                    tq, slots, qt = binfo[bi]
                    bOV = 4 + (bi % 2)
                    pt = pts.items[g % 3]
                    nk = len(slots) + 2
                    ops_ = [(c, vrs[sl].t[:, h * 65:(h + 1) * 65], vrs[sl].b) for c, sl in enumerate(slots)]
                    ops_ += [(5 + c, vc.t[:, c, h * 65:(h + 1) * 65], vc.b) for c in range(2)]
                    for i_, (pc, vap, vb) in enumerate(ops_):
                        mm(ps(bOV, h * 65, (h + 1) * 65), pt.t[:, pc, :], vap, i_ == 0, i_ == nk - 1, [pt.b, vb], [PB[bOV]])

                def tail(bi):
                    tq, slots, qt = binfo[bi]
                    bOV = 4 + (bi % 2)
                    bTP = 6 + (bi % 2)
                    naT = naTs.items[bi % 2]
                    ont = onts[bi % 2]
                    rc = rcs[bi % 2]
                    ov3 = ps(bOV, 0, 390).rearrange("p (h e) -> p h e", h=6)
                    P.emit("dve", lambda e: e.reciprocal(out=rc.t[:], in_=ov3[:, :, 64]), reads=[PB[bOV]], writes=[rc.b])
                    for h in range(6):
                        if h % 2 == 0:
                            ts("dve", ont.t[:, h * 64:(h + 1) * 64], ps(bOV, h * 65, h * 65 + 64), rc.t[:, h:h + 1], ALU.mult,
                               [PB[bOV], rc.b], [ont.b])
                        else:
                            act(ont.t[:, h * 64:(h + 1) * 64], ps(bOV, h * 65, h * 65 + 64), AF.Identity, [PB[bOV], rc.b],
                                [ont.b], scale=rc.t[:, h:h + 1])
                    for m in range(3):
                        tr(ps(bTP, m * 128, (m + 1) * 128), ont.t[:, m * 128:(m + 1) * 128], [ont.b], [PB[bTP]])
                    cp("act", naT.t[:], ps(bTP, 0, 384).rearrange("p (m t) -> p m t", m=3), [PB[bTP]], [naT.b])
                    dma("sp", rows3(mixT[640:1024, tq:tq + 128]), naT.t[:], naT.b, reads=[naT.b])

                G = 6 * len(blocks)
                for g in range(G + 1):
                    if g < G:
                        sa(g)
                    if g >= 1:
                        sc(g - 1)
                        if (g - 1) % 6 == 5:
                            tail((g - 1) // 6)
                P.barrier()

            if stop == 'NA':
                break
            wstack = ExitStack()
            wdn = sb(wstack, "wdn", [128, NF * D], BF16, True)
            load_wdown(l, wdn)
            with ExitStack() as ph:
                mxs = [sb(ph, "mx%d" % i, [128, 8, 512], BF16) for i in range(2)]
                sus = [sb(ph, "su%d" % i, [128, 6, 514], F32) for i in range(2)]
                pr = sb(ph, "pr", [128, 2, 514], F32)
                cvs = Rot([sb(ph, "cv%d" % i, [128, 512], F32) for i in range(2)])
                xks = Rot([sb(ph, "xkc%d" % i, [128, 512], F32) for i in range(3)])
                xos = Rot([sb(ph, "xoc%d" % i, [128, 512], F32) for i in range(2)])
                bkr = Rot(list(range(8)))

                def c_prep(ti):
                    t0, n, s = tiles512[ti]
                    mx, su = mxs[ti % 2], sus[ti % 2]
                    s_lo, s_hi = (0, LC) if s == 1 else (LC, T)
                    lo, hi = max(t0 - 1, s_lo), min(t0 + n + 1, s_hi)
                    dst0 = lo - (t0 - 1)
                    if lo > t0 - 1:
                        mset("pool", su.t[:, :, 0:1], 0.0, [su.b])
                    if hi < t0 + n + 1:
                        mset("pool", su.t[:, :, n + 1:n + 2], 0.0, [su.b])
                    dma("sp", su.t[:, :, dst0:dst0 + hi - lo], rows3(scu[:, lo:hi]), su.b, writes=[su.b])
                    dma("sp", mx.t[:, 0:3, :n], rows3(mixT[0:384, t0:t0 + n]), mx.b, writes=[mx.b])
                    dma("sp", mx.t[:, 5:8, :n], rows3(mixT[640:1024, t0:t0 + n]), mx.b, writes=[mx.b])
                    tt("pool", pr.t[:, :, :n + 2], su.t[:, 2:4, :n + 2], su.t[:, 4:6, :n + 2], ALU.mult, [su.b], [pr.b])
                    for j in range(2):
                        cv = cvs.next()
                        wc = l * 8 + j * 4
                        act(cv.t[:, :n], pr.t[:, j, 1:n + 1], AF.Identity, [pr.b, scwt.b], [cv.b],
                            bias=scwt.t[:, wc + 3:wc + 4], scale=scwt.t[:, wc + 1:wc + 2])
                        stt(cv.t[:, :n], pr.t[:, j, 0:n], scwt.t[:, wc:wc + 1], cv.t[:, :n], ALU.mult, ALU.add,
                            [pr.b, scwt.b, cv.b], [cv.b])
                        stt(cv.t[:, :n], pr.t[:, j, 2:n + 2], scwt.t[:, wc + 2:wc + 3], cv.t[:, :n], ALU.mult, ALU.add,
                            [pr.b, scwt.b, cv.b], [cv.b])
                        tt("pool", mx.t[:, 3 + j, :n], su.t[:, j, 1:n + 1], cv.t[:, :n], ALU.mult, [su.b, cv.b], [mx.b])

                def c_main(ti):
                    t0, n, s = tiles512[ti]
                    mx = mxs[ti % 2]
                    for m in range(8):
                        bk = bkr.next()
                        xk = xks.next()
                        xo = xos.next()
                        dma("sp", xk.t[:, :n], X0[m * 128:(m + 1) * 128, t0:t0 + n], xk.b, writes=[xk.b])
                        for k in range(KD):
                            mm(ps(bk, 0, n), arena.t[:, WOUT0 + k * D + m * 128: WOUT0 + k * D + (m + 1) * 128],
                               mx.t[:, k, :n], k == 0, k == KD - 1, [arena.b, mx.b], [PB[bk]])
                        stt(xo.t[:, :n], ps(bk, 0, n), modcol(l, 16 + m, s), xk.t[:, :n], ALU.mult, ALU.add,
                            [PB[bk], mod.b, xk.b], [xo.b])
                        dma("sp", XB[m * 128:(m + 1) * 128, t0:t0 + n], xo.t[:, :n], xo.b, reads=[xo.b])

                c_prep(0)
                for ti in range(len(tiles512)):
                    if ti + 1 < len(tiles512):
                        c_prep(ti + 1)
                    c_main(ti)
                P.barrier()

            if stop == 'C':
                wstack.close()
                break
            load_wup(l)
            with ExitStack() as ph:
                WD = 484
                xks = Rot([sb(ph, "xkd%d" % i, [128, WD], F32) for i in range(3)])
                fcwt = sb(ph, "fcwt", [128, 176], F32)
                dma("sp", fcwt.t[:], fcw[:, l * 176:(l + 1) * 176], fcwt.b, writes=[fcwt.b])
                sqs = Rot([sb(ph, "sqd%d" % i, [128, WD], BF16) for i in range(2)])
                sd = sb(ph, "sdd", [128, WD], F32)
                rstd = sb(ph, "rstdd", [128, WD], F32)
                tmps = Rot([sb(ph, "tmd%d" % i, [128, WD], F32) for i in range(2)])
                h2s = [sb(ph, "h2_%d" % i, [128, 8, WD], BF16) for i in range(2)]
                acts = sb(ph, "acts", [128, NF, WD], BF16)
                cas = Rot([sb(ph, "ca%d" % i, [128, WD], F32) for i in range(2)])
                cbs = Rot([sb(ph, "cb%d" % i, [128, WD], F32) for i in range(2)])
                sas = Rot([sb(ph, "sa%d" % i, [128, WD], F32) for i in range(1)])
                xos = Rot([sb(ph, "xo%d" % i, [128, WD], F32) for i in range(1)])
                bkr = Rot([1, 2, 3, 4, 5, 6, 7])
                ntl = -(-L // 482)
                no_l = -(-L // ntl)
                dt = [(0, LC, 1)]
                o = 0
                while o < L:
                    dt.append((LC + o, min(no_l, L - o), 0))
                    o += no_l

                def geom(ti):
                    o0, no, s = dt[ti]
                    s_lo, s_hi = (0, LC) if s == 1 else (LC, T)
                    lo, hi = max(o0 - 1, s_lo), min(o0 + no + 1, s_hi)
                    return o0, no, s, lo, hi - lo, o0 - lo

                def d_norm(ti):
                    o0, no, s, lo, nin, off = geom(ti)
                    norm_mod(xloader(XB, lo, nin, xks), nin, A2, l, 24, s, h2s[ti % 2], sqs, sd, rstd, tmps, 0)

                def d_up(ti):
                    o0, no, s, lo, nin, off = geom(ti)
                    h2 = h2s[ti % 2]
                    i0 = 1 if off == 0 else 0
                    j0 = 1 if off + no == nin else 0
                    for c in range(NF):
                        bks, cvs_, wcs = [], [], []
                        for half, crot in ((0, cas), (1, cbs)):
                            bk = bkr.next()
                            fc = half * NF + c
                            col0 = fc * 128
                            for k in range(KD):
                                mm(ps(bk, 0, nin), arena.t[:, k * 2 * FFN + col0: k * 2 * FFN + col0 + 128], h2.t[:, k, :nin],
                                   k == 0, k == KD - 1, [arena.b, h2.b], [PB[bk]])
                            bks.append(bk)
                            cvs_.append(crot.next())
                            wcs.append(fc * 4)
                        for bk, cv, wc in zip(bks, cvs_, wcs):
                            act(cv.t[:, :no], ps(bk, off, off + no), AF.Identity, [PB[bk], fcwt.b], [cv.b],
                                bias=fcwt.t[:, wc + 3:wc + 4], scale=fcwt.t[:, wc + 1:wc + 2])
                        for bk, cv, wc in zip(bks, cvs_, wcs):
                            stt(cv.t[:, i0:no], ps(bk, off + i0 - 1, off + no - 1), fcwt.t[:, wc:wc + 1], cv.t[:, i0:no],
                                ALU.mult, ALU.add, [PB[bk], fcwt.b, cv.b], [cv.b])
                        for bk, cv, wc in zip(bks, cvs_, wcs):
                            stt(cv.t[:, 0:no - j0], ps(bk, off + 1, off + 1 + no - j0), fcwt.t[:, wc + 2:wc + 3],
                                cv.t[:, 0:no - j0], ALU.mult, ALU.add, [PB[bk], fcwt.b, cv.b], [cv.b])
                        sa = sas.next()
                        act(sa.t[:, :no], cvs_[0].t[:, :no], AF.Silu, [cvs_[0].b], [sa.b])
                        tt("pool", acts.t[:, c, :no], sa.t[:, :no], cvs_[1].t[:, :no], ALU.mult, [sa.b, cvs_[1].b], [acts.b])

                def d_down(ti):
                    o0, no, s, lo, nin, off = geom(ti)
                    for m in range(8):
                        bk = bkr.next()
                        for c in range(NF):
                            mm(ps(bk, 0, no), wdn.t[:, c * D + m * 128: c * D + (m + 1) * 128], acts.t[:, c, :no], c == 0,
                               c == NF - 1, [wdn.b, acts.b], [PB[bk]])
                        xk = xks.next()
                        xo = xos.next()
                        dma("sp", xk.t[:, :no], XB[m * 128:(m + 1) * 128, o0:o0 + no], xk.b, writes=[xk.b])
                        stt(xo.t[:, :no], ps(bk, 0, no), modcol(l, 40 + m, s), xk.t[:, :no], ALU.mult, ALU.add,
                            [PB[bk], mod.b, xk.b], [xo.b])
                        dma("sp", XA[m * 128:(m + 1) * 128, o0:o0 + no], xo.t[:, :no], xo.b, reads=[xo.b])

                d_norm(0)
                for ti in range(len(dt)):
                    d_up(ti)
                    if ti + 1 < len(dt):
                        d_norm(ti + 1)
                    d_down(ti)
                if l + 1 < DEPTH:
                    load_win(l + 1)
                    load_wout(l + 1)
                P.barrier()
            wstack.close()

        finals = []
        with ExitStack() as ph:
            if stop:
                tiles512 = tiles512[:1]
            xts = Rot([sb(ph, "xf%d" % i, [128, 8, 512], F32) for i in range(2)])
            sqs = Rot([sb(ph, "sqf%d" % i, [128, 512], BF16) for i in range(2)])
            sd = sb(ph, "sdf", [128, 512], F32)
            rstd = sb(ph, "rstdf", [128, 512], F32)
            xos = Rot([sb(ph, "xof%d" % i, [128, 8, 512], F32) for i in range(2)])
            for (t0, n, s) in tiles512[1:]:
                xt = xts.next()
                xo = xos.next()
                dma("sp", xt.t[:, :, :n], rows3(XA[:, t0:t0 + n]), xt.b, writes=[xt.b])
                for k in range(KD):
                    sq = sqs.next()
                    act(sq.t[:, :n], xt.t[:, k, :n], AF.Square, [xt.b], [sq.b])
                    mm(ps(0, 0, n), ones_bf.t[:], sq.t[:, :n], k == 0, k == KD - 1, [ones_bf.b, sq.b], [PB[0]])
                act(sd.t[:, :n], ps(0, 0, n), AF.Sqrt, [PB[0], cst.b], [sd.b], bias=cst.t[:, 0:1], scale=1.0 / D)
                P.emit("dve", lambda e, sd=sd, rstd=rstd, n=n: e.reciprocal(out=rstd.t[:, :n], in_=sd.t[:, :n]),
                       reads=[sd.b], writes=[rstd.b])
                for k in range(KD):
                    stt(xo.t[:, k, :n], xt.t[:, k, :n], gfn.t[:, k:k + 1], rstd.t[:, :n], ALU.mult, ALU.mult,
                        [xt.b, gfn.b, rstd.b], [xo.b])
                finals.append(dma("sp", rows3(out[:, t0 - LC:t0 - LC + n]), xo.t[:, :, :n], xo.b, reads=[xo.b]))

        with nc.Block() as block:
            P.replay(nc, top, block, final_waits=finals)
    return nc


def _na_bias_tables(rpb, L):
    rows = L // GRID_W
    NB = L // 128
    H = rpb.shape[0]
    reps = [2, 0, 1, NB - 2, NB - 1]
    kk = np.arange(640)
    wrow, kcol = kk // 64, kk % 64
    qq = np.arange(128)
    qr_l, qcol = qq // 64, qq % 64
    out = np.empty((5, 128, H, 5, 128), np.float32)
    for pi, a in enumerate(reps):
        w0 = min(max(a - 2, 0), NB - 5)
        krow = 2 * w0 + wrow
        qrow = 2 * a + qr_l
        rs = np.clip(qrow - 4, 0, rows - 8)
        vrow = (krow[:, None] >= rs[None, :]) & (krow[:, None] < rs[None, :] + 8)
        cstart = np.clip(qcol - 8, 0, GRID_W - 16)
        vcol = (kcol[:, None] >= cstart[None, :]) & (kcol[:, None] < cstart[None, :] + 16)
        dr = np.clip(krow[:, None] - qrow[None, :] + 7, 0, 14)
        dc = np.clip(kcol[:, None] - qcol[None, :] + 15, 0, 30)
        valid = vrow & vcol
        for h in range(H):
            tab = np.where(valid, rpb[h][dr, dc], np.float32(-1e30)).astype(np.float32)
            out[pi, :, h, :, :] = tab.reshape(5, 128, 128).transpose(1, 0, 2)
    return out.reshape(5, 128, H * 5 * 128)


def _rope_table(L):
    T = LC + L
    t = np.arange(L)
    inv = (10000.0 ** (-np.arange(8, dtype=np.float32) / 8)).astype(np.float32)
    row = (t // GRID_W).astype(np.float32)[:, None] * inv
    col = (t % GRID_W).astype(np.float32)[:, None] * inv
    ang = np.concatenate([row, col], axis=-1).astype(np.float32)
    cos = np.ones((T, 16), np.float32)
    sin = np.zeros((T, 16), np.float32)
    cos[LC:] = np.cos(ang)
    sin[LC:] = np.sin(ang)
    tab = np.empty((T, 576), np.float32)
    tab[:, 0:384] = np.tile(cos, (1, 24))
    tab[:, 384:576] = np.tile(sin, (1, 12))
    return tab


def _pvec(v, nchunk):
    return np.ascontiguousarray(v.reshape(nchunk, 128).T)


def prepare_inputs(inp, L, DEPTH, ncores):
    f = lambda a: np.ascontiguousarray(np.asarray(a, dtype=np.float32))
    x, c, ctx, c_ctx = f(inp["x"]), f(inp["c"]), f(inp["ctx"]), f(inp["c_ctx"])
    perm32 = np.concatenate([np.arange(0, 32, 2), np.arange(1, 32, 2)])
    permq = np.concatenate([h * 32 + perm32 for h in range(6)])
    w_in = f(inp["w_in"])[:DEPTH]
    cols = np.concatenate([permq, 192 + permq, np.arange(384, 768), np.arange(800, 1184), np.arange(2720, 3104),
                           np.arange(1184, 1952), np.arange(1952, 2336), np.arange(2336, 2720), np.arange(768, 800)])
    assert cols.shape[0] == INW
    win = np.ascontiguousarray(w_in[:, :, cols]).reshape(DEPTH * D, INW)
    wg = np.zeros((33, DEPTH, 384), np.float32)
    for l in range(DEPTH):
        wg[0:16, l, 0:192] = f(inp["gla_wg2_fw"])[l][:, permq]
        wg[16:32, l, 192:384] = f(inp["gla_wg2_bw"])[l][:, permq]
        wg[32, l, 0:192] = f(inp["gla_bg_fw"])[l][permq]
        wg[32, l, 192:384] = f(inp["gla_bg_bw"])[l][permq]
    rep2 = lambda a: np.repeat(a[..., None], 2, axis=-1)
    bada = np.stack([rep2(_pvec(f(inp["b_ada"])[l], 48)) for l in range(DEPTH)], 1).reshape(128, DEPTH * 96)
    gmix = np.stack([rep2(_pvec(f(inp["norm_mix_g"])[l], 8)) for l in range(DEPTH)], 1).reshape(128, DEPTH * 16)
    gffn = np.stack([rep2(_pvec(f(inp["norm_ffn_g"])[l], 8)) for l in range(DEPTH)], 1).reshape(128, DEPTH * 16)
    gfin = _pvec(f(inp["final_norm_g"]), 8)
    gng = np.stack([np.broadcast_to(np.tile(f(inp["gla_norm_g"])[l], 6)[None, :], (128, 384)) for l in range(DEPTH)],
                   1).reshape(128, DEPTH * 384)
    scw = np.empty((128, DEPTH, 2, 4), np.float32)
    fcw = np.empty((128, DEPTH, 44, 4), np.float32)
    for l in range(DEPTH):
        for t_ in range(3):
            scw[:, l, :, t_] = _pvec(f(inp["sc_conv_w"])[l, t_], 2)
            fcw[:, l, :, t_] = _pvec(f(inp["ffn_conv_w"])[l, t_], 44)
        scw[:, l, :, 3] = _pvec(f(inp["sc_conv_b"])[l], 2)
        fcw[:, l, :, 3] = _pvec(f(inp["ffn_conv_b"])[l], 44)
    nab = np.concatenate([_na_bias_tables(f(inp["na_rpb"])[l], L) for l in range(DEPTH)], 0).reshape(DEPTH * 5 * 128, 3840)
    j = np.arange(128)
    M1 = (j[:, None] <= j[None, :]).astype(np.float32)
    M2 = (j[:, None] > j[None, :]).astype(np.float32)
    M3 = (j[:, None] >= j[None, :]).astype(np.float32)
    M4 = (j[:, None] < j[None, :]).astype(np.float32)
    ctri = np.concatenate([M1, M2, M3, M4], 1)
    cmask = np.concatenate([np.tile(M1, (1, 6)), np.tile(M3, (1, 6))], 1)
    bm = np.zeros((128, 384), np.float32)
    for g in range(2):
        for hp in range(3):
            bm[32 * hp:32 * hp + 32, g * 192 + hp * 64: g * 192 + (hp + 1) * 64] = 1.0
    cmisc = np.concatenate([np.ones((128, 128), np.float32), np.eye(128, dtype=np.float32), bm], 1)
    shared = {
        "wada": f(inp["w_ada"])[:DEPTH].reshape(DEPTH * D, 6 * D), "bada": bada, "gmix": gmix, "gffn": gffn, "gfin": gfin,
        "win": win, "wg": wg.reshape(33, DEPTH * 384), "gng": np.ascontiguousarray(gng),
        "scw": scw.reshape(128, DEPTH * 8), "nab": np.ascontiguousarray(nab),
        "wout": f(inp["w_out"])[:DEPTH].reshape(DEPTH * D, D), "wup": f(inp["ffn_w_up"])[:DEPTH].reshape(DEPTH * D, 2 * FFN),
        "fcw": fcw.reshape(128, DEPTH * 176), "wdown": f(inp["ffn_w_down"])[:DEPTH].reshape(DEPTH * FFN, D),
        "rope": _rope_table(L), "ctri": ctri, "cmask": cmask, "cmisc": cmisc,
    }
    maps = []
    for b in range(ncores):
        m = dict(shared)
        m["xin"] = np.ascontiguousarray(np.concatenate([ctx[b].T, x[b].T], axis=1))
        cc = np.stack([_pvec(c[b], 8), _pvec(c_ctx, 8)], -1).reshape(128, 16)
        m["cT"] = np.ascontiguousarray(cc)
        maps.append(m)
    return maps


_NC_CACHE = {}


def run(inputs, L, DEPTH, ncores):
    key = (L, DEPTH)
    if key not in _NC_CACHE:
        _NC_CACHE[key] = build_program(L, DEPTH)
    nc = _NC_CACHE[key]
    maps = prepare_inputs(inputs, L, DEPTH, ncores)
    res = run_bass_kernel_spmd(nc, maps, core_ids=list(range(ncores)))
    return np.stack([np.ascontiguousarray(r["out"].T) for r in res.results], 0).astype(np.float32)


def kernel(**inputs):
    return run(inputs, 8192, 4, 8)
```

```python
import numpy as np
from contextlib import ExitStack
import concourse.bass as bass
import concourse.mybir as mybir
from concourse.bass_utils import run_bass_kernel_spmd

F32 = mybir.dt.float32
BF16 = mybir.dt.bfloat16
AF = mybir.ActivationFunctionType
ALU = mybir.AluOpType
AX = mybir.AxisListType

D = 1024
KD = 8
LC = 256
GRID_W = 64
EPS = 1e-6
FFN = 2816
NF = 22
INW = 3104
ENGS = ("pe", "act", "dve", "pool", "sp")


class Buf:
    __slots__ = ("name", "lw", "rd", "dsem", "persist")

    def __init__(self, name, persist=False):
        self.name = name
        self.lw = None
        self.rd = []
        self.dsem = None
        self.persist = persist


class Ins:
    __slots__ = ("eng", "fn", "deps", "signal", "count", "dma", "dsem", "dval")

    def __init__(self, eng, fn, dma):
        self.eng = eng
        self.fn = fn
        self.deps = []
        self.signal = False
        self.count = 0
        self.dma = dma
        self.dsem = None
        self.dval = 0


class Prog:
    def __init__(self):
        self.q = {e: [] for e in ENGS}
        self.dma_cnt = []
        self.dma_last = []
        self.dma_persist = []
        self.last = {e: None for e in ENGS}
        self.free_slots = {"sp": [], "pool": [], "act": []}
        self.pe_fence = False
        self.phase_slots = []

    def _dsem_for(self, buf, eng):
        if buf.dsem is None:
            if (not buf.persist) and self.free_slots[eng]:
                buf.dsem = self.free_slots[eng].pop()
            else:
                buf.dsem = len(self.dma_cnt)
                self.dma_cnt.append(0)
                self.dma_last.append(None)
                self.dma_persist.append(buf.persist)
            if not buf.persist:
                self.phase_slots.append((eng, buf.dsem))
        return buf.dsem

    def emit(self, eng, fn, reads=(), writes=(), dma_key=None, group_cont=False):
        ins = Ins(eng, fn, dma_key is not None)
        deps = []
        raw = set()
        for b in reads:
            if b.lw is not None:
                deps.append(b.lw)
                raw.add(id(b.lw))
        for b in writes:
            if b.lw is not None:
                deps.append(b.lw)
            deps.extend(b.rd)
        if dma_key is None:
            deps = [d for d in deps if d.dma or d.eng != eng or eng != "pe"]
            if eng == "pe" and self.pe_fence and self.last["pe"] is not None:
                deps.append(self.last["pe"])
                self.pe_fence = False

        if dma_key is not None:
            s = self._dsem_for(dma_key, eng)
            ins.dsem = s
            self.dma_cnt[s] += 16
            ins.dval = self.dma_cnt[s]
            if self.dma_last[s] is not None:
                deps.append(self.dma_last[s])
            if group_cont:
                deps = [d for d in deps if not (d.dma and d.dsem == s)]
            self.dma_last[s] = ins
        seen = set()
        for d in deps:
            if d is ins or id(d) in seen:
                continue
            seen.add(id(d))
            if not d.dma:
                d.signal = True
            ins.deps.append(d)
        for b in reads:
            b.rd.append(ins)
        for b in writes:
            b.lw = ins
            b.rd = []
        self.q[eng].append(ins)
        if dma_key is None:
            self.last[eng] = ins
        return ins

    def barrier(self):
        deps = [self.last[e] for e in ENGS if self.last[e] is not None]
        deps += [d for d, p in zip(self.dma_last, self.dma_persist) if d is not None and not p]
        for e in ("act", "dve", "pool", "sp"):
            ins = Ins(e, None, False)
            for d in deps:
                if not d.dma:
                    d.signal = True
                ins.deps.append(d)
            self.q[e].append(ins)
        for e_, sl in self.phase_slots:
            self.free_slots[e_].append(sl)
        self.phase_slots = []

    def replay(self, nc, stack, block, final_waits=()):
        for e in ENGS:
            c = 0
            for ins in self.q[e]:
                if (not ins.dma) and ins.signal:
                    c += 1
                    ins.count = c
        esem = {e: stack.enter_context(nc.semaphore("es_" + e)) for e in ENGS}
        dsem = [stack.enter_context(nc.semaphore("ds_%d" % i)) for i in range(len(self.dma_cnt))]
        prog = self

        def run(engname, engobj):
            waited_e = {e: 0 for e in ENGS}
            waited_d = [0] * len(prog.dma_cnt)
            for ins in prog.q[engname]:
                for d in ins.deps:
                    if d.dma:
                        if waited_d[d.dsem] < d.dval:
                            engobj.wait_ge(dsem[d.dsem], d.dval)
                            waited_d[d.dsem] = d.dval
                    else:
                        if waited_e[d.eng] < d.count:
                            engobj.wait_ge(esem[d.eng], d.count)
                            waited_e[d.eng] = d.count
                if ins.fn is None:
                    continue
                r = ins.fn(engobj)
                if ins.dma:
                    r.then_inc(dsem[ins.dsem], 16)
                elif ins.signal:
                    r.then_inc(esem[ins.eng], 1)
            if engname == "sp":
                for d in final_waits:
                    engobj.wait_ge(dsem[d.dsem], d.dval)

        block.tensor(lambda t: run("pe", t))
        block.scalar(lambda t: run("act", t))
        block.vector(lambda t: run("dve", t))
        block.gpsimd(lambda t: run("pool", t))
        block.sync(lambda t: run("sp", t))


class Tl:
    __slots__ = ("t", "b")

    def __init__(self, t, name, persist=False):
        self.t = t
        self.b = Buf(name, persist)


class Rot:
    def __init__(self, items):
        self.items = items
        self.i = 0

    def next(self):
        r = self.items[self.i % len(self.items)]
        self.i += 1
        return r


def build_program(L, DEPTH, stop=None):
    import os
    stop = stop or os.environ.get('KSTOP')
    T = LC + L
    NCH = T // 128
    NB = L // 128
    nc = bass.Bass("TRN2", target_bir_lowering=False)
    P = Prog()

    def din(name, shape, dt=F32):
        return nc.dram_tensor(name, list(shape), dt, kind="ExternalInput").ap()

    def dscr(name, shape, dt):
        return nc.dram_tensor(name, list(shape), dt).ap()

    xin = din("xin", [D, T])
    cT = din("cT", [128, 16])
    wada = din("wada", [DEPTH * D, 6 * D])
    bada = din("bada", [128, DEPTH * 96])
    gmix = din("gmix", [128, DEPTH * 16])
    gffn = din("gffn", [128, DEPTH * 16])
    gfin = din("gfin", [128, 8])
    win = din("win", [DEPTH * D, INW])
    wg = din("wg", [33, DEPTH * 384])
    gng = din("gng", [128, DEPTH * 384])
    scw = din("scw", [128, DEPTH * 8])
    nab = din("nab", [DEPTH * 5 * 128, 6 * 5 * 128])
    wout = din("wout", [DEPTH * D, D])
    wup = din("wup", [DEPTH * D, 2 * FFN])
    fcw = din("fcw", [128, DEPTH * 176])
    wdown = din("wdown", [DEPTH * FFN, D])
    rope = din("rope", [T, 576])
    ctri = din("ctri", [128, 512])
    cmask = din("cmask", [128, 2 * 768])
    cmisc = din("cmisc", [128, 128 + 128 + 384])
    out = nc.dram_tensor("out", [D, L], F32, kind="ExternalOutput").ap()

    XA = dscr("XA", [D, T], F32)
    XB = dscr("XB", [D, T], F32)
    scu = dscr("scu", [768, T], F32)
    naq = dscr("naq", [384, T], BF16)
    nak = dscr("nak", [384, T], BF16)
    nav = dscr("nav", [T, 390], BF16)
    gbT = dscr("gbT", [NCH * 96, 512], BF16)
    gkv = dscr("gkv", [T, 576], BF16)
    gof = dscr("gof", [T, 384], F32)
    gr = dscr("gr", [T, 384], F32)
    mixT = dscr("mixT", [D, T], BF16)

    with ExitStack() as top:
        E = top.enter_context

        uid = [0]

        def sb(stack, name, shape, dt, persist=False):
            uid[0] += 1
            nm = "%s_%d" % (name, uid[0])
            return Tl(stack.enter_context(nc.sbuf_tensor(nm, list(shape), dt)), nm, persist)

        psum = E(nc.psum_tensor("psum", [128, 4096], F32))
        psum_bf = psum.bitcast(BF16)
        PB = [Buf("bank%d" % i, True) for i in range(8)]

        def ps(bank, c0, c1, p0=0, p1=128):
            return psum[p0:p1, bank * 512 + c0: bank * 512 + c1]

        def psb(bank, c0, c1, p0=0, p1=128):
            return psum_bf[p0:p1, bank * 1024 + c0: bank * 1024 + c1]

        arena = sb(top, "arena", [128, 8 * 2 * FFN], BF16, True)
        WOUT0 = 8 * INW
        cst = sb(top, "cst", [128, 8], F32, True)
        ones_bf = sb(top, "ones_bf", [128, 128], BF16, True)
        ident = sb(top, "ident", [128, 128], BF16, True)
        bmask = sb(top, "bmask", [128, 384], BF16, True)
        ones_f = sb(top, "ones_f", [128, 2], F32, True)
        mod = sb(top, "mod", [128, DEPTH * 96], F32, True)
        A1 = sb(top, "A1", [128, DEPTH * 16], F32, True)
        A2 = sb(top, "A2", [128, DEPTH * 16], F32, True)
        gmx = sb(top, "gmx", [128, DEPTH * 16], F32, True)
        gff = sb(top, "gff", [128, DEPTH * 16], F32, True)
        gfn = sb(top, "gfn", [128, 8], F32, True)
        scwt = sb(top, "scwt", [128, DEPTH * 8], F32, True)
        wgt = sb(top, "wgt", [33, DEPTH * 384], BF16, True)
        decb = sb(top, "decb", [96, NCH * 2], F32, True)

        def modcol(l, j, s):
            c = l * 96 + j * 2 + s
            return mod.t[:, c:c + 1]

        def acol(A, l, k, s):
            c = l * 16 + k * 2 + s
            return A.t[:, c:c + 1]

        def dma(eng, out_ap, in_ap, key, reads=(), writes=(), cont=False):
            return P.emit(eng, lambda e: e.dma_start(out=out_ap, in_=in_ap), reads=reads, writes=writes, dma_key=key,
                          group_cont=cont)

        def mm(out_ap, lhsT, rhs, start, stop, reads, writes, fence=False):
            if fence:
                P.pe_fence = True
            r = P.emit("pe", lambda e: e.matmul(out_ap, lhsT=lhsT, rhs=rhs, start=start, stop=stop),
                       reads=reads, writes=writes)
            if fence:
                P.pe_fence = True
            return r

        def tr(out_ap, in_ap, reads, writes):
            return P.emit("pe", lambda e: e.matmul(out_ap, lhsT=in_ap, rhs=ident.t[:], start=True, stop=True),
                          reads=list(reads) + [ident.b], writes=writes)

        def act(out_ap, in_ap, func, reads, writes, bias=None, scale=None):
            kw = {}
            if bias is not None:
                kw["bias"] = bias
            if scale is not None:
                kw["scale"] = scale
            return P.emit("act", lambda e: e.activation(out=out_ap, in_=in_ap, func=func, **kw), reads=reads,
                          writes=writes)

        def tt(eng, out_ap, a, b, op, reads, writes):
            return P.emit(eng, lambda e: e.tensor_tensor(out=out_ap, in0=a, in1=b, op=op), reads=reads, writes=writes)

        def ts(eng, out_ap, a, s1, op0, reads, writes, s2=None, op1=None):
            if op1 is None:
                return P.emit(eng, lambda e: e.tensor_scalar(out=out_ap, in0=a, scalar1=s1, scalar2=None, op0=op0),
                              reads=reads, writes=writes)
            return P.emit(eng, lambda e: e.tensor_scalar(out=out_ap, in0=a, scalar1=s1, scalar2=s2, op0=op0, op1=op1),
                          reads=reads, writes=writes)

        def stt(out_ap, a, s, b, op0, op1, reads, writes):
            return P.emit("dve", lambda e: e.scalar_tensor_tensor(out=out_ap, in0=a, scalar=s, in1=b, op0=op0, op1=op1),
                          reads=reads, writes=writes)

        def cp(eng, out_ap, in_ap, reads, writes):
            if eng == "act":
                return P.emit("act", lambda e: e.copy(out=out_ap, in_=in_ap), reads=reads, writes=writes)
            return P.emit(eng, lambda e: e.tensor_copy(out=out_ap, in_=in_ap), reads=reads, writes=writes)

        def mset(eng, ap, val, writes):
            return P.emit(eng, lambda e: e.memset(ap, val), writes=writes)

        def rows3(ap2d, p=128):
            return ap2d.rearrange("(m p) t -> p m t", p=p)

        mset("dve", cst.t[:, 0:1], EPS, [cst.b])
        mset("dve", cst.t[:, 1:2], float(np.log(32.0 ** -0.5)), [cst.b])
        mset("dve", cst.t[:, 2:3], 1.0, [cst.b])
        mset("dve", ones_f.t[:], 1.0, [ones_f.b])
        dma("pool", ones_bf.t[:], cmisc[:, 0:128], ones_bf.b, writes=[ones_bf.b])
        dma("pool", ident.t[:], cmisc[:, 128:256], ident.b, writes=[ident.b])
        dma("pool", bmask.t[:], cmisc[:, 256:640], bmask.b, writes=[bmask.b])
        dma("pool", wgt.t[:], wg, wgt.b, writes=[wgt.b])
        dma("sp", gmx.t[:], gmix, gmx.b, writes=[gmx.b])
        dma("sp", gff.t[:], gffn, gff.b, writes=[gff.b])
        dma("sp", gfn.t[:], gfin, gfn.b, writes=[gfn.b])
        dma("sp", scwt.t[:], scw, scwt.b, writes=[scwt.b])

        def load_win(l):
            for k in range(KD):
                dma("pool", arena.t[:, k * INW:(k + 1) * INW], win[l * D + k * 128: l * D + (k + 1) * 128, :],
                    arena.b, writes=[arena.b], cont=(k > 0))

        def load_wout(l):
            for k in range(KD):
                dma("pool", arena.t[:, WOUT0 + k * D: WOUT0 + (k + 1) * D],
                    wout[l * D + k * 128: l * D + (k + 1) * 128, :], arena.b, writes=[arena.b], cont=True)

        def load_wup(l):
            for k in range(KD):
                dma("pool", arena.t[:, k * 2 * FFN:(k + 1) * 2 * FFN],
                    wup[l * D + k * 128: l * D + (k + 1) * 128, :], arena.b, writes=[arena.b], cont=(k > 0))

        def load_wdown(l, wdn):
            for c in range(NF):
                dma("pool", wdn.t[:, c * D:(c + 1) * D], wdown[l * FFN + c * 128: l * FFN + (c + 1) * 128, :], wdn.b,
                    writes=[wdn.b], cont=(c > 0))

        with ExitStack() as ph:
            cin = sb(ph, "cin", [128, 16], F32)
            csl = sb(ph, "csl", [128, 16], F32)
            bad = sb(ph, "bad", [128, DEPTH * 96], F32)
            wst = [sb(ph, "wst%d" % i, [128, 8 * 768], F32) for i in range(2)]
            dma("sp", cin.t[:], cT, cin.b, writes=[cin.b])
            dma("sp", bad.t[:], bada, bad.b, writes=[bad.b])
            act(csl.t[:], cin.t[:], AF.Silu, [cin.b], [csl.b])
            pi = 0
            for l in range(DEPTH):
                bank = l % 2
                for piece in range(8):
                    w = wst[pi % 2]
                    pi += 1
                    for k in range(KD):
                        dma("sp", w.t[:, k * 768:(k + 1) * 768],
                            wada[l * D + k * 128: l * D + (k + 1) * 128, piece * 768:(piece + 1) * 768], w.b,
                            writes=[w.b])
                    for jj in range(6):
                        j = piece * 6 + jj
                        for k in range(KD):
                            mm(ps(bank, j * 2, j * 2 + 2), w.t[:, k * 768 + jj * 128: k * 768 + (jj + 1) * 128],
                               csl.t[:, k * 2:(k + 1) * 2], k == 0, k == KD - 1, [w.b, csl.b], [PB[bank]])
                tt("dve", mod.t[:, l * 96:(l + 1) * 96], ps(bank, 0, 96), bad.t[:, l * 96:(l + 1) * 96], ALU.add,
                   [PB[bank], bad.b], [mod.b])
                stt(A1.t[:, l * 16:(l + 1) * 16], mod.t[:, l * 96 + 16: l * 96 + 32], 1.0, gmx.t[:, l * 16:(l + 1) * 16],
                    ALU.add, ALU.mult, [mod.b, gmx.b], [A1.b])
                stt(A2.t[:, l * 16:(l + 1) * 16], mod.t[:, l * 96 + 64: l * 96 + 80], 1.0, gff.t[:, l * 16:(l + 1) * 16],
                    ALU.add, ALU.mult, [mod.b, gff.b], [A2.b])
            P.pe_fence = True
            P.barrier()

        def norm_mod(getx, n, A, l, shj, s, hT, sqs, sd, rstd, tmps, ssbank):
            for k in range(KD):
                xb, xap = getx(k)
                sq = sqs.next()
                act(sq.t[:, :n], xap, AF.Square, [xb.b], [sq.b])
                mm(ps(ssbank, 0, n), ones_bf.t[:], sq.t[:, :n], k == 0, k == KD - 1, [ones_bf.b, sq.b], [PB[ssbank]])
            act(sd.t[:, :n], ps(ssbank, 0, n), AF.Sqrt, [PB[ssbank], cst.b], [sd.b], bias=cst.t[:, 0:1], scale=1.0 / D)
            P.emit("dve", lambda e: e.reciprocal(out=rstd.t[:, :n], in_=sd.t[:, :n]), reads=[sd.b], writes=[rstd.b])
            for k in range(KD):
                xb, xap = getx(k)
                tm = tmps.next()
                stt(tm.t[:, :n], xap, acol(A, l, k, s), rstd.t[:, :n], ALU.mult, ALU.mult, [xb.b, A.b, rstd.b], [tm.b])
                act(hT.t[:, k, :n], tm.t[:, :n], AF.Identity, [tm.b, mod.b], [hT.b], bias=modcol(l, shj + k, s), scale=1.0)

        def xloader(X, lo, n, xks):
            def getx(k):
                xk = xks.next()
                dma("sp", xk.t[:, :n], X[k * 128:(k + 1) * 128, lo:lo + n], xk.b, writes=[xk.b])
                return xk, xk.t[:, :n]
            return getx

        tiles512 = [(0, LC, 1)] + [(LC + i * 512, 512, 0) for i in range(L // 512)]

        load_win(0)
        load_wout(0)
        for l in range(DEPTH):
            X0 = xin if l == 0 else XA

            if stop == 'PRO':
                break
            with ExitStack() as ph:
                xks = Rot([sb(ph, "xk%d" % i, [128, 512], F32) for i in range(3)])
                tri = sb(ph, "tri", [128, 512], F32)
                msk = sb(ph, "msk", [128, 384], BF16)
                dma("sp", tri.t[:], ctri, tri.b, writes=[tri.b])
                dma("pool", msk.t[:], cmask[:, 0:384], msk.b, writes=[msk.b])
                sqs = Rot([sb(ph, "sq%d" % i, [128, 512], BF16) for i in range(3)])
                sd = sb(ph, "sd", [128, 512], F32)
                rstd = sb(ph, "rstd", [128, 512], F32)
                tmps = Rot([sb(ph, "tm%d" % i, [128, 512], F32) for i in range(2)])
                hTs = [sb(ph, "hT%d" % i, [128, 8, 512], BF16) for i in range(2)]
                scos = Rot([sb(ph, "sco%d" % i, [128, 512], F32) for i in range(3)])
                nqks = Rot([sb(ph, "nqk%d" % i, [128, 512], BF16) for i in range(3)])
                glrs = [sb(ph, "glr%d" % i, [33, 512], BF16) for i in range(2)]

                def pool_(name, shape, dt, depth):
                    return [sb(ph, "%s%d" % (name, i), shape, dt) for i in range(depth)]
                rps = pool_("rp", [128, 576], F32, 2)
                qks = pool_("qk", [128, 384], F32, 2)
                kvs = pool_("kv", [128, 576], BF16, 6)
                rrs = pool_("rr", [128, 384], F32, 2)
                nvs = pool_("nv", [128, 390], BF16, 2)
                ees = pool_("ee", [128, 384], F32, 2)
                lls = pool_("ll", [128, 384], F32, 2)
                fac = sb(ph, "fac", [128, 6, 192], F32)
                tcs = sb(ph, "tcs", [128, 384], F32)
                m12 = sb(ph, "m12", [128, 2, 192], F32)
                rqs = pool_("rq", [128, 384], F32, 2)
                gps = pool_("gp", [128, 4, 192], BF16, 2)
                gTfs = pool_("gTf", [96, 512], BF16, 3)
                gTbs = pool_("gTb", [96, 512], BF16, 2)
                atms = pool_("atm", [128, 6, 128], BF16, 2)
                kefs = pool_("kef", [128, 192], BF16, 4)
                decs = pool_("dec", [96, 4], F32, 4)
                ofs = pool_("of", [128, 384], F32, 2)
                um = sb(ph, "um", [96, 384], F32)
                Sf = sb(ph, "Sf", [96, 384], F32)
                Sfb = sb(ph, "Sfb", [96, 384], BF16)
                for g_ in glrs:
                    mset("pool", g_.t[32:33, :], 1.0, [g_.b])
                for n_ in nvs:
                    mset("pool", n_.t[:], 1.0, [n_.b])
                mset("dve", Sf.t[:], 0.0, [Sf.b])
                mset("dve", Sfb.t[:], 0.0, [Sfb.b])
                fmb = Rot([1, 2])
                tmb = Rot([3, 4])
                evr = Rot(["act", "dve"])

                def tile_norm(ti):
                    t0, n, s = tiles512[ti]
                    norm_mod(xloader(X0, t0, n, xks), n, A1, l, 0, s, hTs[ti % 2], sqs, sd, rstd, tmps, 0)

                def tile_fm(ti):
                    t0, n, s = tiles512[ti]
                    hT = hTs[ti % 2]
                    glr = glrs[ti % 2]
                    for m in range(12):
                        bk = fmb.next()
                        c0 = 1536 + m * 128
                        for k in range(KD):
                            mm(ps(bk, 0, n), arena.t[:, k * INW + c0: k * INW + c0 + 128], hT.t[:, k, :n], k == 0,
                               k == KD - 1, [arena.b, hT.b], [PB[bk]])
                        if m < 6:
                            sco = scos.next()
                            cp(evr.next(), sco.t[:, :n], ps(bk, 0, n), [PB[bk]], [sco.b])
                            dma("sp", scu[m * 128:(m + 1) * 128, t0:t0 + n], sco.t[:, :n], sco.b, reads=[sco.b])
                        else:
                            nqk = nqks.next()
                            cp(evr.next(), nqk.t[:, :n], ps(bk, 0, n), [PB[bk]], [nqk.b])
                            dst = naq if m < 9 else nak
                            mm_ = (m - 6) % 3
                            dma("sp", dst[mm_ * 128:(mm_ + 1) * 128, t0:t0 + n], nqk.t[:, :n], nqk.b, reads=[nqk.b])
                    bk = fmb.next()
                    for k in range(KD):
                        mm(ps(bk, 0, n, 0, 32), arena.t[:, k * INW + 3072: k * INW + 3104], hT.t[:, k, :n], k == 0,
                           k == KD - 1, [arena.b, hT.b], [PB[bk]])
                    cp(evr.next(), glr.t[0:32, :n], ps(bk, 0, n, 0, 32), [PB[bk]], [glr.b])

                chunks = []
                for ti, (t0, n, s) in enumerate(tiles512):
                    for c in range(n // 128):
                        chunks.append((ti, c))

                def st0(i):
                    ti, c = chunks[i]
                    t0 = tiles512[ti][0]
                    tk0 = t0 + c * 128
                    cs = slice(c * 128, (c + 1) * 128)
                    hT = hTs[ti % 2]
                    qk, kv, rr, nv, rp = qks[i % 2], kvs[i % 6], rrs[i % 2], nvs[i % 2], rps[i % 2]
                    dma("sp", rp.t[:], rope[tk0:tk0 + 128, :], rp.b, writes=[rp.b])
                    for g in range(4):
                        bk = tmb.next()
                        for k in range(KD):
                            mm(ps(bk, 0, 384), hT.t[:, k, cs], arena.t[:, k * INW + g * 384: k * INW + (g + 1) * 384],
                               k == 0, k == KD - 1, [arena.b, hT.b], [PB[bk]])
                        if g == 0:
                            cp("act", qk.t[:], ps(bk, 0, 384), [PB[bk]], [qk.b])
                        elif g == 1:
                            cp("dve", kv.t[:, 0:384], ps(bk, 0, 384), [PB[bk]], [kv.b])
                        elif g == 2:
                            cp("act", rr.t[:], ps(bk, 0, 384), [PB[bk]], [rr.b])
                        else:
                            cp("dve", nv.t[:].rearrange("p (h e) -> p h e", h=6)[:, :, 0:64],
                               ps(bk, 0, 384).rearrange("p (h e) -> p h e", h=6), [PB[bk]], [nv.b])
                    dma("sp", gr[tk0:tk0 + 128, :], rr.t[:], rr.b, reads=[rr.b])
                    dma("sp", nav[tk0:tk0 + 128, :], nv.t[:], nv.b, reads=[nv.b])

                def st1(i):
                    ti, c = chunks[i]
                    cs = slice(c * 128, (c + 1) * 128)
                    glr = glrs[ti % 2]
                    qk, rp, rq, ee, ll = qks[i % 2], rps[i % 2], rqs[i % 2], ees[i % 2], lls[i % 2]
                    mm(ps(5, 0, 384), glr.t[0:33, cs], wgt.t[:, l * 384:(l + 1) * 384], True, True, [glr.b, wgt.b],
                       [PB[5]])
                    act(ee.t[:], ps(5, 0, 384), AF.Exp, [PB[5]], [ee.b], scale=-1.0)
                    act(ll.t[:], ee.t[:], AF.Ln, [ee.b, cst.b], [ll.b], bias=cst.t[:, 2:3], scale=1.0)
                    q4 = qk.t[:].rearrange("p (h t e) -> p h t e", h=12, t=2)
                    t4 = tcs.t[:].rearrange("p (h t e) -> p h t e", h=12, t=2)
                    r4 = rq.t[:].rearrange("p (h t e) -> p h t e", h=12, t=2)
                    sn = rp.t[:, 384:576].rearrange("p (h e) -> p h e", h=12)
                    tt("pool", tcs.t[:], qk.t[:], rp.t[:, 0:384], ALU.mult, [qk.b, rp.b], [tcs.b])
                    tt("pool", m12.t[:, 0, :].rearrange("p (h e) -> p h e", h=12), q4[:, :, 1, :], sn, ALU.mult,
                       [qk.b, rp.b], [m12.b])
                    tt("pool", m12.t[:, 1, :].rearrange("p (h e) -> p h e", h=12), q4[:, :, 0, :], sn, ALU.mult,
                       [qk.b, rp.b], [m12.b])
                    tt("pool", r4[:, :, 0, :], t4[:, :, 0, :], m12.t[:, 0, :].rearrange("p (h e) -> p h e", h=12),
                       ALU.subtract, [tcs.b, m12.b], [rq.b])
                    tt("pool", r4[:, :, 1, :], t4[:, :, 1, :], m12.t[:, 1, :].rearrange("p (h e) -> p h e", h=12),
                       ALU.add, [tcs.b, m12.b], [rq.b])

                def st2(i):
                    ti, c = chunks[i]
                    ch = (tiles512[ti][0] // 128) + c
                    tk0 = ch * 128
                    ll, rq, gp, kv, kef, dec = lls[i % 2], rqs[i % 2], gps[i % 2], kvs[i % 6], kefs[i % 4], decs[i % 4]
                    P.pe_fence = True
                    for d_ in range(2):
                        bk = 6 + d_
                        for jj in range(2):
                            mm(ps(bk, jj * 192, (jj + 1) * 192), tri.t[:, (d_ * 2 + jj) * 128:(d_ * 2 + jj + 1) * 128],
                               ll.t[:, d_ * 192:(d_ + 1) * 192], True, True, [tri.b, ll.b], [PB[bk]])
                    for d_ in range(2):
                        for g in range(2):
                            i_ = d_ * 2 + g
                            mm(ps(5, 400 + i_, 401 + i_, 0, 96), ll.t[:, d_ * 192 + g * 96: d_ * 192 + (g + 1) * 96],
                               ones_f.t[:, 0:1], True, True, [ll.b, ones_f.b], [PB[5]])
                    P.pe_fence = True
                    act(dec.t[:], ps(5, 400, 404, 0, 96), AF.Exp, [PB[5]], [dec.b], scale=-1.0 / 16)
                    cp("dve", decb.t[:, ch * 2:(ch + 1) * 2], dec.t[:, 2:4], [dec.b], [decb.b])
                    for d_ in range(2):
                        bk = 6 + d_
                        act(fac.t[:, d_ * 3 + 0, :], ps(bk, 0, 192), AF.Exp, [PB[bk], cst.b], [fac.b],
                            bias=cst.t[:, 1:2], scale=-1.0 / 16)
                        act(fac.t[:, d_ * 3 + 1, :], ps(bk, 0, 192), AF.Exp, [PB[bk]], [fac.b], scale=1.0 / 16)
                        act(fac.t[:, d_ * 3 + 2, :], ps(bk, 192, 384), AF.Exp, [PB[bk]], [fac.b], scale=-1.0 / 16)
                    tt("dve", gp.t[:, 0, :], rq.t[:, 0:192], fac.t[:, 0, :], ALU.mult, [rq.b, fac.b], [gp.b])
                    tt("dve", gp.t[:, 1, :], rq.t[:, 192:384], fac.t[:, 1, :], ALU.mult, [rq.b, fac.b], [gp.b])
                    tt("dve", gp.t[:, 2, :], rq.t[:, 0:192], fac.t[:, 3, :], ALU.mult, [rq.b, fac.b], [gp.b])
                    tt("dve", gp.t[:, 3, :], rq.t[:, 192:384], fac.t[:, 4, :], ALU.mult, [rq.b, fac.b], [gp.b])
                    tt("dve", kv.t[:, 384:576], rq.t[:, 192:384], fac.t[:, 5, :], ALU.mult, [rq.b, fac.b], [kv.b])
                    tt("pool", kef.t[:], rq.t[:, 192:384], fac.t[:, 2, :], ALU.mult, [rq.b, fac.b], [kef.b])
                    dma("sp", gkv[tk0:tk0 + 128, :], kv.t[:], kv.b, reads=[kv.b])

                def st3(i):
                    ti, c = chunks[i]
                    ch = (tiles512[ti][0] // 128) + c
                    gp, gTf, gTb = gps[i % 2], gTfs[i % 3], gTbs[i % 2]
                    for j in range(4):
                        for g in range(2):
                            tb_ = 1 if j < 2 else 2
                            tc_ = ((j % 2) * 2 + g) * 128
                            tr(ps(tb_, tc_, tc_ + 128, 0, 96), gp.t[:, j, g * 96:(g + 1) * 96], [gp.b], [PB[tb_]])
                    cp("act", gTf.t[:], ps(1, 0, 512, 0, 96), [PB[1]], [gTf.b])
                    cp("dve", gTb.t[:], ps(2, 0, 512, 0, 96), [PB[2]], [gTb.b])
                    dma("sp", gbT[ch * 96:(ch + 1) * 96, :], gTb.t[:], gTb.b, reads=[gTb.b])

                def st4(i):
                    gTf, atm = gTfs[i % 3], atms[i % 2]
                    for h in range(6):
                        g, hp = h // 3, h % 3
                        bk = 6 + g
                        mm(ps(bk, hp * 128, (hp + 1) * 128), gTf.t[32 * hp:32 * hp + 32, (2 + g) * 128:(3 + g) * 128],
                           gTf.t[32 * hp:32 * hp + 32, g * 128:(g + 1) * 128], True, True, [gTf.b], [PB[bk]], fence=True)
                    for g in range(2):
                        tt("dve", atm.t[:, 3 * g:3 * g + 3, :], ps(6 + g, 0, 384).rearrange("p (h i) -> p h i", h=3),
                           msk.t[:, 0:384].rearrange("p (h i) -> p h i", h=3), ALU.mult, [PB[6 + g], msk.b], [atm.b])

                def st5(i):
                    ti, c = chunks[i]
                    ch = (tiles512[ti][0] // 128) + c
                    tk0 = ch * 128
                    gTf, atm, kv, kef, dec, of = gTfs[i % 3], atms[i % 2], kvs[i % 6], kefs[i % 4], decs[i % 4], ofs[i % 2]
                    bo = tmb.next()
                    for h in range(6):
                        g, hp = h // 3, h % 3
                        mm(ps(bo, h * 64, (h + 1) * 64), atm.t[:, h, :], kv.t[:, h * 64:(h + 1) * 64], True, False,
                           [atm.b, kv.b], [PB[bo]])
                        mm(ps(bo, h * 64, (h + 1) * 64), gTf.t[32 * hp:32 * hp + 32, g * 128:(g + 1) * 128],
                           Sfb.t[32 * hp:32 * hp + 32, g * 192 + hp * 64: g * 192 + (hp + 1) * 64], False, True,
                           [gTf.b, Sfb.b], [PB[bo]])
                    cp("act", of.t[:], ps(bo, 0, 384), [PB[bo]], [of.b])
                    dma("sp", gof[tk0:tk0 + 128, :], of.t[:], of.b, reads=[of.b])
                    for g in range(2):
                        mm(ps(0, g * 192, (g + 1) * 192, 0, 96), kef.t[:, g * 96:(g + 1) * 96],
                           kv.t[:, g * 192:(g + 1) * 192], True, True, [kef.b, kv.b], [PB[0]])
                    tt("dve", um.t[:], ps(0, 0, 384, 0, 96), bmask.t[0:96, :], ALU.mult, [PB[0], bmask.b], [um.b])
                    for g in range(2):
                        stt(Sf.t[:, g * 192:(g + 1) * 192], Sf.t[:, g * 192:(g + 1) * 192], dec.t[:, g:g + 1],
                            um.t[:, g * 192:(g + 1) * 192], ALU.mult, ALU.add, [Sf.b, dec.b, um.b], [Sf.b])
                    cp("dve", Sfb.t[:], Sf.t[:], [Sf.b], [Sfb.b])

                stages = [st0, st1, st2, st3, st4, st5]
                if stop and stop.startswith('S1s'):
                    stages = stages[:int(stop[3:])]
                nchunks = len(chunks)
                tile_norm(0)
                for step in range(nchunks + len(stages) - 1):
                    if step < nchunks:
                        ti, c = chunks[step]
                        if c == 0:
                            tile_fm(ti)
                        if c == min(1, tiles512[ti][1] // 128 - 1) and ti + 1 < len(tiles512):
                            tile_norm(ti + 1)
                    for si, st in enumerate(stages):
                        i = step - si
                        if 0 <= i < nchunks:
                            st(i)
                P.barrier()

            if stop and stop.startswith('S1'):
                break
            with ExitStack() as ph:
                def pool_(name, shape, dt, depth):
                    return [sb(ph, "%s%d" % (name, i), shape, dt) for i in range(depth)]
                gTs = pool_("gT", [96, 512], BF16, 3)
                kvs = pool_("kvb", [128, 576], BF16, 3)
                ofs = pool_("ofb", [128, 384], F32, 3)
                rrs = pool_("rrb", [128, 384], F32, 4)
                atms = pool_("atmb", [128, 6, 128], BF16, 2)
                oos = pool_("oo", [128, 384], F32, 2)
                o2 = sb(ph, "o2", [128, 384], F32)
                ssq = sb(ph, "ssq", [128, 6], F32)
                rs = sb(ph, "rs", [128, 6], F32)
                on = sb(ph, "on", [128, 384], F32)
                sr = sb(ph, "sr", [128, 384], F32)
                gl = sb(ph, "gl", [128, 384], BF16)
                glTs = pool_("glT", [128, 3, 128], BF16, 2)
                um = sb(ph, "umb", [96, 384], F32)
                Sb = sb(ph, "Sb", [96, 384], F32)
                Sbb = sb(ph, "Sbb", [96, 384], BF16)
                msk = sb(ph, "mskb", [128, 384], BF16)
                gngt = sb(ph, "gngt", [128, 384], F32)
                dma("pool", msk.t[:], cmask[:, 768:1152], msk.b, writes=[msk.b])
                dma("sp", gngt.t[:], gng[:, l * 384:(l + 1) * 384], gngt.b, writes=[gngt.b])
                mset("dve", Sb.t[:], 0.0, [Sb.b])
                mset("dve", Sbb.t[:], 0.0, [Sbb.b])
                order = [1, 0] + list(range(NCH - 1, 1, -1))

                def g0(i):
                    ch = order[i]
                    tk0 = ch * 128
                    gT, kv, of, rr = gTs[i % 3], kvs[i % 3], ofs[i % 3], rrs[i % 4]
                    dma("sp", gT.t[:], gbT[ch * 96:(ch + 1) * 96, :], gT.b, writes=[gT.b])
                    dma("sp", kv.t[:], gkv[tk0:tk0 + 128, :], kv.b, writes=[kv.b])
                    dma("sp", of.t[:], gof[tk0:tk0 + 128, :], of.b, writes=[of.b])
                    dma("sp", rr.t[:], gr[tk0:tk0 + 128, :], rr.b, writes=[rr.b])

                def g1(i):
                    gT, atm = gTs[i % 3], atms[i % 2]
                    bA0 = 2 * (i % 2)
                    for h in range(6):
                        g, hp = h // 3, h % 3
                        bk = bA0 + g
                        mm(ps(bk, hp * 128, (hp + 1) * 128), gT.t[32 * hp:32 * hp + 32, (2 + g) * 128:(3 + g) * 128],
                           gT.t[32 * hp:32 * hp + 32, g * 128:(g + 1) * 128], True, True, [gT.b], [PB[bk]], fence=True)
                    for g in range(2):
                        tt("dve", atm.t[:, 3 * g:3 * g + 3, :], ps(bA0 + g, 0, 384).rearrange("p (h i) -> p h i", h=3),
                           msk.t[:, 0:384].rearrange("p (h i) -> p h i", h=3), ALU.mult, [PB[bA0 + g], msk.b], [atm.b])

                def g2(i):
                    ch = order[i]
                    gT, kv, of, atm, oo = gTs[i % 3], kvs[i % 3], ofs[i % 3], atms[i % 2], oos[i % 2]
                    bO, bU = 4 + (i % 2), 6
                    for h in range(6):
                        g, hp = h // 3, h % 3
                        mm(ps(bO, h * 64, (h + 1) * 64), atm.t[:, h, :], kv.t[:, h * 64:(h + 1) * 64], True, False,
                           [atm.b, kv.b], [PB[bO]])
                        mm(ps(bO, h * 64, (h + 1) * 64), gT.t[32 * hp:32 * hp + 32, g * 128:(g + 1) * 128],
                           Sbb.t[32 * hp:32 * hp + 32, g * 192 + hp * 64: g * 192 + (hp + 1) * 64], False, True,
                           [gT.b, Sbb.b], [PB[bO]])
                    for g in range(2):
                        mm(ps(bU, g * 192, (g + 1) * 192, 0, 96), kv.t[:, 384 + g * 96: 384 + (g + 1) * 96],
                           kv.t[:, g * 192:(g + 1) * 192], True, True, [kv.b], [PB[bU]])
                    tt("dve", um.t[:], ps(bU, 0, 384, 0, 96), bmask.t[0:96, :], ALU.mult, [PB[bU], bmask.b], [um.b])
                    for g in range(2):
                        stt(Sb.t[:, g * 192:(g + 1) * 192], Sb.t[:, g * 192:(g + 1) * 192],
                            decb.t[:, ch * 2 + g: ch * 2 + g + 1], um.t[:, g * 192:(g + 1) * 192], ALU.mult, ALU.add,
                            [Sb.b, decb.b, um.b], [Sb.b])
                    cp("dve", Sbb.t[:], Sb.t[:], [Sb.b], [Sbb.b])
                    tt("dve", oo.t[:], ps(bO, 0, 384), of.t[:], ALU.add, [PB[bO], of.b], [oo.b])

                def g3(i):
                    ch = order[i]
                    tk0 = ch * 128
                    oo, rr, glT = oos[i % 2], rrs[i % 4], glTs[i % 2]
                    tt("pool", o2.t[:], oo.t[:], oo.t[:], ALU.mult, [oo.b], [o2.b])
                    P.emit("dve", lambda e: e.tensor_reduce(
                        out=ssq.t[:], in_=o2.t[:].rearrange("p (h e) -> p h e", h=6), axis=AX.X, op=ALU.add),
                        reads=[o2.b], writes=[ssq.b])
                    act(rs.t[:], ssq.t[:], AF.Sqrt, [ssq.b, cst.b], [rs.b], bias=cst.t[:, 0:1], scale=1.0 / 64)
                    P.emit("dve", lambda e: e.reciprocal(out=rs.t[:], in_=rs.t[:]), reads=[rs.b], writes=[rs.b])
                    act(sr.t[:], rr.t[:], AF.Silu, [rr.b], [sr.b])
                    for h in range(6):
                        stt(on.t[:, h * 64:(h + 1) * 64], oo.t[:, h * 64:(h + 1) * 64], rs.t[:, h:h + 1],
                            gngt.t[:, h * 64:(h + 1) * 64], ALU.mult, ALU.mult, [oo.b, rs.b, gngt.b], [on.b])
                    tt("pool", gl.t[:], on.t[:], sr.t[:], ALU.mult, [on.b, sr.b], [gl.b])
                    for m in range(3):
                        tr(ps(7, m * 128, (m + 1) * 128), gl.t[:, m * 128:(m + 1) * 128], [gl.b], [PB[7]])
                    cp("act", glT.t[:], ps(7, 0, 384).rearrange("p (m t) -> p m t", m=3), [PB[7]], [glT.b])
                    dma("sp", rows3(mixT[0:384, tk0:tk0 + 128]), glT.t[:], glT.b, reads=[glT.b])

                stages = [g0, g1, g2, g3]
                for step in range(NCH + len(stages) - 1):
                    for si, st in enumerate(stages):
                        i = step - si
                        if 0 <= i < NCH:
                            st(i)
                P.barrier()

            if stop == 'GB':
                break
            with ExitStack() as ph:
                nb = sb(ph, "nb", [128, 6 * 5 * 128], F32)
                krs = [sb(ph, "kr%d" % i, [128, 3, 128], BF16) for i in range(8)]
                vrs = [sb(ph, "vr%d" % i, [128, 390], BF16) for i in range(8)]
                kc = sb(ph, "kc", [128, 3, 256], BF16)
                vc = sb(ph, "vc", [128, 2, 390], BF16)
                qts = Rot([sb(ph, "qt%d" % i, [128, 3, 128], BF16) for i in range(2)])
                stmps = Rot([sb(ph, "stmp%d" % i, [128, 640], F32) for i in range(2)])
                pts = Rot([sb(ph, "pt%d" % i, [128, 7, 128], BF16) for i in range(3)])
                rcs = [sb(ph, "rc%d" % i, [128, 6], F32) for i in range(2)]
                onts = [sb(ph, "ont%d" % i, [128, 384], BF16) for i in range(2)]
                naTs = Rot([sb(ph, "naT%d" % i, [128, 3, 128], BF16) for i in range(2)])
                dma("sp", kc.t[:], rows3(nak[:, 0:LC]), kc.b, writes=[kc.b])
                dma("sp", vc.t[:], nav[0:LC, :].rearrange("(c p) f -> p c f", p=128), vc.b, writes=[vc.b])
                loaded = {}
                cur_pat = [None]

                def ensure_chunk(cid):
                    slot = cid % 8
                    if loaded.get(slot) != cid:
                        tk = LC + cid * 128
                        dma("sp", krs[slot].t[:], rows3(nak[:, tk:tk + 128]), krs[slot].b, writes=[krs[slot].b])
                        dma("sp", vrs[slot].t[:], nav[tk:tk + 128, :], vrs[slot].b, writes=[vrs[slot].b])
                        loaded[slot] = cid
                    return slot

                def pat_of(a):
                    if a == 0:
                        return 1
                    if a == 1:
                        return 2
                    if a == NB - 2:
                        return 3
                    if a == NB - 1:
                        return 4
                    return 0

                blocks = [("c", 0), ("c", 1)] + [("l", a) for a in range(NB)]
                binfo = {}

                def blk_setup(bi):
                    kind, a = blocks[bi]
                    qt = qts.items[bi % 2]
                    if kind == "c":
                        tq = a * 128
                        slots = []
                    else:
                        tq = LC + a * 128
                        w0 = min(max(a - 2, 0), NB - 5)
                        slots = [ensure_chunk(w0 + c) for c in range(5)]
                        pat = pat_of(a)
                        if cur_pat[0] != pat:
                            r0 = (l * 5 + pat) * 128
                            dma("sp", nb.t[:], nab[r0:r0 + 128, :], nb.b, writes=[nb.b])
                            cur_pat[0] = pat
                    dma("sp", qt.t[:], rows3(naq[:, tq:tq + 128]), qt.b, writes=[qt.b])
                    binfo[bi] = (tq, slots, qt)

                def sa(g):
                    bi, h = divmod(g, 6)
                    if h == 0:
                        blk_setup(bi)
                    tq, slots, qt = binfo[bi]
                    m, pb = h // 2, (h % 2) * 64
                    b0 = 2 * (g % 2)
                    pt = pts.items[g % 3]
                    for c, sl in enumerate(slots):
                        bk, off = (b0, c * 128) if c < 4 else (b0 + 1, 0)
                        mm(ps(bk, off, off + 128), krs[sl].t[pb:pb + 64, m, :], qt.t[pb:pb + 64, m, :], True, True,
                           [krs[sl].b, qt.b], [PB[bk]])
                    for c in range(2):
                        mm(ps(b0 + 1, 128 + c * 128, 256 + c * 128), kc.t[pb:pb + 64, m, c * 128:(c + 1) * 128],
                           qt.t[pb:pb + 64, m, :], True, True, [kc.b, qt.b], [PB[b0 + 1]])
                    if slots:
                        stmp = stmps.items[g % 2]
                        nbh = nb.t[:, h * 640:(h + 1) * 640]
                        stt(stmp.t[:, 0:512], ps(b0, 0, 512), 0.125, nbh[:, 0:512], ALU.mult, ALU.add,
                            [PB[b0], nb.b], [stmp.b])
                        stt(stmp.t[:, 512:640], ps(b0 + 1, 0, 128), 0.125, nbh[:, 512:640], ALU.mult, ALU.add,
                            [PB[b0 + 1], nb.b], [stmp.b])
                        act(pt.t[:, 0:5, :], stmp.t[:].rearrange("p (c q) -> p c q", c=5), AF.Exp, [stmp.b], [pt.b])
                    act(pt.t[:, 5:7, :], ps(b0 + 1, 128, 384).rearrange("p (c q) -> p c q", c=2), AF.Exp, [PB[b0 + 1]],
                        [pt.b], scale=0.125)

                def sc(g):
                    bi, h = divmod(g, 6)
                    tq, slots, qt = binfo[bi]
                    bOV = 4 + (bi % 2)
                    pt = pts.items[g % 3]
                    nk = len(slots) + 2
                    ops_ = [(c, vrs[sl].t[:, h * 65:(h + 1) * 65], vrs[sl].b) for c, sl in enumerate(slots)]
                    ops_ += [(5 + c, vc.t[:, c, h * 65:(h + 1) * 65], vc.b) for c in range(2)]
                    for i_, (pc, vap, vb) in enumerate(ops_):
                        mm(ps(bOV, h * 65, (h + 1) * 65), pt.t[:, pc, :], vap, i_ == 0, i_ == nk - 1, [pt.b, vb], [PB[bOV]])

                def tail(bi):
                    tq, slots, qt = binfo[bi]
                    bOV = 4 + (bi % 2)
                    bTP = 6 + (bi % 2)
                    naT = naTs.items[bi % 2]
                    ont = onts[bi % 2]
                    rc = rcs[bi % 2]
                    ov3 = ps(bOV, 0, 390).rearrange("p (h e) -> p h e", h=6)
                    P.emit("dve", lambda e: e.reciprocal(out=rc.t[:], in_=ov3[:, :, 64]), reads=[PB[bOV]], writes=[rc.b])
                    for h in range(6):
                        if h % 2 == 0:
                            ts("dve", ont.t[:, h * 64:(h + 1) * 64], ps(bOV, h * 65, h * 65 + 64), rc.t[:, h:h + 1], ALU.mult,
                               [PB[bOV], rc.b], [ont.b])
                        else:
                            act(ont.t[:, h * 64:(h + 1) * 64], ps(bOV, h * 65, h * 65 + 64), AF.Identity, [PB[bOV], rc.b],
                                [ont.b], scale=rc.t[:, h:h + 1])
                    for m in range(3):
                        tr(ps(bTP, m * 128, (m + 1) * 128), ont.t[:, m * 128:(m + 1) * 128], [ont.b], [PB[bTP]])
                    cp("act", naT.t[:], ps(bTP, 0, 384).rearrange("p (m t) -> p m t", m=3), [PB[bTP]], [naT.b])
                    dma("sp", rows3(mixT[640:1024, tq:tq + 128]), naT.t[:], naT.b, reads=[naT.b])

                G = 6 * len(blocks)
                for g in range(G + 1):
                    if g < G:
                        sa(g)
                    if g >= 1:
                        sc(g - 1)
                        if (g - 1) % 6 == 5:
                            tail((g - 1) // 6)
                P.barrier()

            if stop == 'NA':
                break
            wstack = ExitStack()
            wdn = sb(wstack, "wdn", [128, NF * D], BF16, True)
            load_wdown(l, wdn)
            with ExitStack() as ph:
                mxs = [sb(ph, "mx%d" % i, [128, 8, 512], BF16) for i in range(2)]
                sus = [sb(ph, "su%d" % i, [128, 6, 514], F32) for i in range(2)]
                pr = sb(ph, "pr", [128, 2, 514], F32)
                cvs = Rot([sb(ph, "cv%d" % i, [128, 512], F32) for i in range(2)])
                xks = Rot([sb(ph, "xkc%d" % i, [128, 512], F32) for i in range(3)])
                xos = Rot([sb(ph, "xoc%d" % i, [128, 512], F32) for i in range(2)])
                bkr = Rot(list(range(8)))

                def c_prep(ti):
                    t0, n, s = tiles512[ti]
                    mx, su = mxs[ti % 2], sus[ti % 2]
                    s_lo, s_hi = (0, LC) if s == 1 else (LC, T)
                    lo, hi = max(t0 - 1, s_lo), min(t0 + n + 1, s_hi)
                    dst0 = lo - (t0 - 1)
                    if lo > t0 - 1:
                        mset("pool", su.t[:, :, 0:1], 0.0, [su.b])
                    if hi < t0 + n + 1:
                        mset("pool", su.t[:, :, n + 1:n + 2], 0.0, [su.b])
                    dma("sp", su.t[:, :, dst0:dst0 + hi - lo], rows3(scu[:, lo:hi]), su.b, writes=[su.b])
                    dma("sp", mx.t[:, 0:3, :n], rows3(mixT[0:384, t0:t0 + n]), mx.b, writes=[mx.b])
                    dma("sp", mx.t[:, 5:8, :n], rows3(mixT[640:1024, t0:t0 + n]), mx.b, writes=[mx.b])
                    tt("pool", pr.t[:, :, :n + 2], su.t[:, 2:4, :n + 2], su.t[:, 4:6, :n + 2], ALU.mult, [su.b], [pr.b])
                    for j in range(2):
                        cv = cvs.next()
                        wc = l * 8 + j * 4
                        act(cv.t[:, :n], pr.t[:, j, 1:n + 1], AF.Identity, [pr.b, scwt.b], [cv.b],
                            bias=scwt.t[:, wc + 3:wc + 4], scale=scwt.t[:, wc + 1:wc + 2])
                        stt(cv.t[:, :n], pr.t[:, j, 0:n], scwt.t[:, wc:wc + 1], cv.t[:, :n], ALU.mult, ALU.add,
                            [pr.b, scwt.b, cv.b], [cv.b])
                        stt(cv.t[:, :n], pr.t[:, j, 2:n + 2], scwt.t[:, wc + 2:wc + 3], cv.t[:, :n], ALU.mult, ALU.add,
                            [pr.b, scwt.b, cv.b], [cv.b])
                        tt("pool", mx.t[:, 3 + j, :n], su.t[:, j, 1:n + 1], cv.t[:, :n], ALU.mult, [su.b, cv.b], [mx.b])

                def c_main(ti):
                    t0, n, s = tiles512[ti]
                    mx = mxs[ti % 2]
                    for m in range(8):
                        bk = bkr.next()
                        xk = xks.next()
                        xo = xos.next()
                        dma("sp", xk.t[:, :n], X0[m * 128:(m + 1) * 128, t0:t0 + n], xk.b, writes=[xk.b])
                        for k in range(KD):
                            mm(ps(bk, 0, n), arena.t[:, WOUT0 + k * D + m * 128: WOUT0 + k * D + (m + 1) * 128],
                               mx.t[:, k, :n], k == 0, k == KD - 1, [arena.b, mx.b], [PB[bk]])
                        stt(xo.t[:, :n], ps(bk, 0, n), modcol(l, 16 + m, s), xk.t[:, :n], ALU.mult, ALU.add,
                            [PB[bk], mod.b, xk.b], [xo.b])
                        dma("sp", XB[m * 128:(m + 1) * 128, t0:t0 + n], xo.t[:, :n], xo.b, reads=[xo.b])

                c_prep(0)
                for ti in range(len(tiles512)):
                    if ti + 1 < len(tiles512):
                        c_prep(ti + 1)
                    c_main(ti)
                P.barrier()

            if stop == 'C':
                wstack.close()
                break
            load_wup(l)
            with ExitStack() as ph:
                WD = 484
                xks = Rot([sb(ph, "xkd%d" % i, [128, WD], F32) for i in range(3)])
                fcwt = sb(ph, "fcwt", [128, 176], F32)
                dma("sp", fcwt.t[:], fcw[:, l * 176:(l + 1) * 176], fcwt.b, writes=[fcwt.b])
                sqs = Rot([sb(ph, "sqd%d" % i, [128, WD], BF16) for i in range(2)])
                sd = sb(ph, "sdd", [128, WD], F32)
                rstd = sb(ph, "rstdd", [128, WD], F32)
                tmps = Rot([sb(ph, "tmd%d" % i, [128, WD], F32) for i in range(2)])
                h2s = [sb(ph, "h2_%d" % i, [128, 8, WD], BF16) for i in range(2)]
                acts = sb(ph, "acts", [128, NF, WD], BF16)
                cas = Rot([sb(ph, "ca%d" % i, [128, WD], F32) for i in range(2)])
                cbs = Rot([sb(ph, "cb%d" % i, [128, WD], F32) for i in range(2)])
                sas = Rot([sb(ph, "sa%d" % i, [128, WD], F32) for i in range(1)])
                xos = Rot([sb(ph, "xo%d" % i, [128, WD], F32) for i in range(1)])
                bkr = Rot([1, 2, 3, 4, 5, 6, 7])
                ntl = -(-L // 482)
                no_l = -(-L // ntl)
                dt = [(0, LC, 1)]
                o = 0
                while o < L:
                    dt.append((LC + o, min(no_l, L - o), 0))
                    o += no_l

                def geom(ti):
                    o0, no, s = dt[ti]
                    s_lo, s_hi = (0, LC) if s == 1 else (LC, T)
                    lo, hi = max(o0 - 1, s_lo), min(o0 + no + 1, s_hi)
                    return o0, no, s, lo, hi - lo, o0 - lo

                def d_norm(ti):
                    o0, no, s, lo, nin, off = geom(ti)
                    norm_mod(xloader(XB, lo, nin, xks), nin, A2, l, 24, s, h2s[ti % 2], sqs, sd, rstd, tmps, 0)

                def d_up(ti):
                    o0, no, s, lo, nin, off = geom(ti)
                    h2 = h2s[ti % 2]
                    i0 = 1 if off == 0 else 0
                    j0 = 1 if off + no == nin else 0
                    for c in range(NF):
                        bks, cvs_, wcs = [], [], []
                        for half, crot in ((0, cas), (1, cbs)):
                            bk = bkr.next()
                            fc = half * NF + c
                            col0 = fc * 128
                            for k in range(KD):
                                mm(ps(bk, 0, nin), arena.t[:, k * 2 * FFN + col0: k * 2 * FFN + col0 + 128], h2.t[:, k, :nin],
                                   k == 0, k == KD - 1, [arena.b, h2.b], [PB[bk]])
                            bks.append(bk)
                            cvs_.append(crot.next())
                            wcs.append(fc * 4)
                        for bk, cv, wc in zip(bks, cvs_, wcs):
                            act(cv.t[:, :no], ps(bk, off, off + no), AF.Identity, [PB[bk], fcwt.b], [cv.b],
                                bias=fcwt.t[:, wc + 3:wc + 4], scale=fcwt.t[:, wc + 1:wc + 2])
                        for bk, cv, wc in zip(bks, cvs_, wcs):
                            stt(cv.t[:, i0:no], ps(bk, off + i0 - 1, off + no - 1), fcwt.t[:, wc:wc + 1], cv.t[:, i0:no],
                                ALU.mult, ALU.add, [PB[bk], fcwt.b, cv.b], [cv.b])
                        for bk, cv, wc in zip(bks, cvs_, wcs):
                            stt(cv.t[:, 0:no - j0], ps(bk, off + 1, off + 1 + no - j0), fcwt.t[:, wc + 2:wc + 3],
                                cv.t[:, 0:no - j0], ALU.mult, ALU.add, [PB[bk], fcwt.b, cv.b], [cv.b])
                        sa = sas.next()
                        act(sa.t[:, :no], cvs_[0].t[:, :no], AF.Silu, [cvs_[0].b], [sa.b])
                        tt("pool", acts.t[:, c, :no], sa.t[:, :no], cvs_[1].t[:, :no], ALU.mult, [sa.b, cvs_[1].b], [acts.b])

                def d_down(ti):
                    o0, no, s, lo, nin, off = geom(ti)
                    for m in range(8):
                        bk = bkr.next()
                        for c in range(NF):
                            mm(ps(bk, 0, no), wdn.t[:, c * D + m * 128: c * D + (m + 1) * 128], acts.t[:, c, :no], c == 0,
                               c == NF - 1, [wdn.b, acts.b], [PB[bk]])
                        xk = xks.next()
                        xo = xos.next()
                        dma("sp", xk.t[:, :no], XB[m * 128:(m + 1) * 128, o0:o0 + no], xk.b, writes=[xk.b])
                        stt(xo.t[:, :no], ps(bk, 0, no), modcol(l, 40 + m, s), xk.t[:, :no], ALU.mult, ALU.add,
                            [PB[bk], mod.b, xk.b], [xo.b])
                        dma("sp", XA[m * 128:(m + 1) * 128, o0:o0 + no], xo.t[:, :no], xo.b, reads=[xo.b])

                d_norm(0)
                for ti in range(len(dt)):
                    d_up(ti)
                    if ti + 1 < len(dt):
                        d_norm(ti + 1)
                    d_down(ti)
                if l + 1 < DEPTH:
                    load_win(l + 1)
                    load_wout(l + 1)
                P.barrier()
            wstack.close()

        finals = []
        with ExitStack() as ph:
            if stop:
                tiles512 = tiles512[:1]
            xts = Rot([sb(ph, "xf%d" % i, [128, 8, 512], F32) for i in range(2)])
            sqs = Rot([sb(ph, "sqf%d" % i, [128, 512], BF16) for i in range(2)])
            sd = sb(ph, "sdf", [128, 512], F32)
            rstd = sb(ph, "rstdf", [128, 512], F32)
            xos = Rot([sb(ph, "xof%d" % i, [128, 8, 512], F32) for i in range(2)])
            for (t0, n, s) in tiles512[1:]:
                xt = xts.next()
                xo = xos.next()
                dma("sp", xt.t[:, :, :n], rows3(XA[:, t0:t0 + n]), xt.b, writes=[xt.b])
                for k in range(KD):
                    sq = sqs.next()
                    act(sq.t[:, :n], xt.t[:, k, :n], AF.Square, [xt.b], [sq.b])
                    mm(ps(0, 0, n), ones_bf.t[:], sq.t[:, :n], k == 0, k == KD - 1, [ones_bf.b, sq.b], [PB[0]])
                act(sd.t[:, :n], ps(0, 0, n), AF.Sqrt, [PB[0], cst.b], [sd.b], bias=cst.t[:, 0:1], scale=1.0 / D)
                P.emit("dve", lambda e, sd=sd, rstd=rstd, n=n: e.reciprocal(out=rstd.t[:, :n], in_=sd.t[:, :n]),
                       reads=[sd.b], writes=[rstd.b])
                for k in range(KD):
                    stt(xo.t[:, k, :n], xt.t[:, k, :n], gfn.t[:, k:k + 1], rstd.t[:, :n], ALU.mult, ALU.mult,
                        [xt.b, gfn.b, rstd.b], [xo.b])
                finals.append(dma("sp", rows3(out[:, t0 - LC:t0 - LC + n]), xo.t[:, :, :n], xo.b, reads=[xo.b]))

        with nc.Block() as block:
            P.replay(nc, top, block, final_waits=finals)
    return nc


def _na_bias_tables(rpb, L):
    rows = L // GRID_W
    NB = L // 128
    H = rpb.shape[0]
    reps = [2, 0, 1, NB - 2, NB - 1]
    kk = np.arange(640)
    wrow, kcol = kk // 64, kk % 64
    qq = np.arange(128)
    qr_l, qcol = qq // 64, qq % 64
    out = np.empty((5, 128, H, 5, 128), np.float32)
    for pi, a in enumerate(reps):
        w0 = min(max(a - 2, 0), NB - 5)
        krow = 2 * w0 + wrow
        qrow = 2 * a + qr_l
        rs = np.clip(qrow - 4, 0, rows - 8)
        vrow = (krow[:, None] >= rs[None, :]) & (krow[:, None] < rs[None, :] + 8)
        cstart = np.clip(qcol - 8, 0, GRID_W - 16)
        vcol = (kcol[:, None] >= cstart[None, :]) & (kcol[:, None] < cstart[None, :] + 16)
        dr = np.clip(krow[:, None] - qrow[None, :] + 7, 0, 14)
        dc = np.clip(kcol[:, None] - qcol[None, :] + 15, 0, 30)
        valid = vrow & vcol
        for h in range(H):
            tab = np.where(valid, rpb[h][dr, dc], np.float32(-1e30)).astype(np.float32)
            out[pi, :, h, :, :] = tab.reshape(5, 128, 128).transpose(1, 0, 2)
    return out.reshape(5, 128, H * 5 * 128)


def _rope_table(L):
    T = LC + L
    t = np.arange(L)
    inv = (10000.0 ** (-np.arange(8, dtype=np.float32) / 8)).astype(np.float32)
    row = (t // GRID_W).astype(np.float32)[:, None] * inv
    col = (t % GRID_W).astype(np.float32)[:, None] * inv
    ang = np.concatenate([row, col], axis=-1).astype(np.float32)
    cos = np.ones((T, 16), np.float32)
    sin = np.zeros((T, 16), np.float32)
    cos[LC:] = np.cos(ang)
    sin[LC:] = np.sin(ang)
    tab = np.empty((T, 576), np.float32)
    tab[:, 0:384] = np.tile(cos, (1, 24))
    tab[:, 384:576] = np.tile(sin, (1, 12))
    return tab


def _pvec(v, nchunk):
    return np.ascontiguousarray(v.reshape(nchunk, 128).T)


def prepare_inputs(inp, L, DEPTH, ncores):
    f = lambda a: np.ascontiguousarray(np.asarray(a, dtype=np.float32))
    x, c, ctx, c_ctx = f(inp["x"]), f(inp["c"]), f(inp["ctx"]), f(inp["c_ctx"])
    perm32 = np.concatenate([np.arange(0, 32, 2), np.arange(1, 32, 2)])
    permq = np.concatenate([h * 32 + perm32 for h in range(6)])
    w_in = f(inp["w_in"])[:DEPTH]
    cols = np.concatenate([permq, 192 + permq, np.arange(384, 768), np.arange(800, 1184), np.arange(2720, 3104),
                           np.arange(1184, 1952), np.arange(1952, 2336), np.arange(2336, 2720), np.arange(768, 800)])
    assert cols.shape[0] == INW
    win = np.ascontiguousarray(w_in[:, :, cols]).reshape(DEPTH * D, INW)
    wg = np.zeros((33, DEPTH, 384), np.float32)
    for l in range(DEPTH):
        wg[0:16, l, 0:192] = f(inp["gla_wg2_fw"])[l][:, permq]
        wg[16:32, l, 192:384] = f(inp["gla_wg2_bw"])[l][:, permq]
        wg[32, l, 0:192] = f(inp["gla_bg_fw"])[l][permq]
        wg[32, l, 192:384] = f(inp["gla_bg_bw"])[l][permq]
    rep2 = lambda a: np.repeat(a[..., None], 2, axis=-1)
    bada = np.stack([rep2(_pvec(f(inp["b_ada"])[l], 48)) for l in range(DEPTH)], 1).reshape(128, DEPTH * 96)
    gmix = np.stack([rep2(_pvec(f(inp["norm_mix_g"])[l], 8)) for l in range(DEPTH)], 1).reshape(128, DEPTH * 16)
    gffn = np.stack([rep2(_pvec(f(inp["norm_ffn_g"])[l], 8)) for l in range(DEPTH)], 1).reshape(128, DEPTH * 16)
    gfin = _pvec(f(inp["final_norm_g"]), 8)
    gng = np.stack([np.broadcast_to(np.tile(f(inp["gla_norm_g"])[l], 6)[None, :], (128, 384)) for l in range(DEPTH)],
                   1).reshape(128, DEPTH * 384)
    scw = np.empty((128, DEPTH, 2, 4), np.float32)
    fcw = np.empty((128, DEPTH, 44, 4), np.float32)
    for l in range(DEPTH):
        for t_ in range(3):
            scw[:, l, :, t_] = _pvec(f(inp["sc_conv_w"])[l, t_], 2)
            fcw[:, l, :, t_] = _pvec(f(inp["ffn_conv_w"])[l, t_], 44)
        scw[:, l, :, 3] = _pvec(f(inp["sc_conv_b"])[l], 2)
        fcw[:, l, :, 3] = _pvec(f(inp["ffn_conv_b"])[l], 44)
    nab = np.concatenate([_na_bias_tables(f(inp["na_rpb"])[l], L) for l in range(DEPTH)], 0).reshape(DEPTH * 5 * 128, 3840)
    j = np.arange(128)
    M1 = (j[:, None] <= j[None, :]).astype(np.float32)
    M2 = (j[:, None] > j[None, :]).astype(np.float32)
    M3 = (j[:, None] >= j[None, :]).astype(np.float32)
    M4 = (j[:, None] < j[None, :]).astype(np.float32)
    ctri = np.concatenate([M1, M2, M3, M4], 1)
    cmask = np.concatenate([np.tile(M1, (1, 6)), np.tile(M3, (1, 6))], 1)
    bm = np.zeros((128, 384), np.float32)
    for g in range(2):
        for hp in range(3):
            bm[32 * hp:32 * hp + 32, g * 192 + hp * 64: g * 192 + (hp + 1) * 64] = 1.0
    cmisc = np.concatenate([np.ones((128, 128), np.float32), np.eye(128, dtype=np.float32), bm], 1)
    shared = {
        "wada": f(inp["w_ada"])[:DEPTH].reshape(DEPTH * D, 6 * D), "bada": bada, "gmix": gmix, "gffn": gffn, "gfin": gfin,
        "win": win, "wg": wg.reshape(33, DEPTH * 384), "gng": np.ascontiguousarray(gng),
        "scw": scw.reshape(128, DEPTH * 8), "nab": np.ascontiguousarray(nab),
        "wout": f(inp["w_out"])[:DEPTH].reshape(DEPTH * D, D), "wup": f(inp["ffn_w_up"])[:DEPTH].reshape(DEPTH * D, 2 * FFN),
        "fcw": fcw.reshape(128, DEPTH * 176), "wdown": f(inp["ffn_w_down"])[:DEPTH].reshape(DEPTH * FFN, D),
        "rope": _rope_table(L), "ctri": ctri, "cmask": cmask, "cmisc": cmisc,
    }
    maps = []
    for b in range(ncores):
        m = dict(shared)
        m["xin"] = np.ascontiguousarray(np.concatenate([ctx[b].T, x[b].T], axis=1))
        cc = np.stack([_pvec(c[b], 8), _pvec(c_ctx, 8)], -1).reshape(128, 16)
        m["cT"] = np.ascontiguousarray(cc)
        maps.append(m)
    return maps


_NC_CACHE = {}


def run(inputs, L, DEPTH, ncores):
    key = (L, DEPTH)
    if key not in _NC_CACHE:
        _NC_CACHE[key] = build_program(L, DEPTH)
    nc = _NC_CACHE[key]
    maps = prepare_inputs(inputs, L, DEPTH, ncores)
    res = run_bass_kernel_spmd(nc, maps, core_ids=list(range(ncores)))
    return np.stack([np.ascontiguousarray(r["out"].T) for r in res.results], 0).astype(np.float32)


def kernel(**inputs):
    return run(inputs, 8192, 4, 8)
```
